# Optimizing a Trainium2 kernel written in Bass

```python
import jax, jax.numpy as jnp
from jax import lax
import numpy as np

D_MODEL = 1024
BATCH = 4
SEQ = 8192
DEPTH = 2

MLA_HEADS = 8
QK_NOPE_DIM = 64
QK_ROPE_DIM = 32
V_HEAD_DIM = 64
Q_LORA_RANK = 384
KV_LORA_RANK = 256
ROPE_THETA = 10000.0
Q_BLOCK = 128
MLA_WIDTH = MLA_HEADS * V_HEAD_DIM
POOL_WINDOWS = (2, 4, 8, 16)
POOL_GROUP_DIM = 64
POOL_WIDTH = len(POOL_WINDOWS) * POOL_GROUP_DIM
SG_HEADS = 4
SG_HEAD_DIM = 64
SG_WIDTH = SG_HEADS * SG_HEAD_DIM
SG_CHUNK = 128
D_MIX = MLA_WIDTH + POOL_WIDTH + SG_WIDTH
IN_COLS = Q_LORA_RANK + KV_LORA_RANK + QK_ROPE_DIM + POOL_WIDTH + 2 * SG_WIDTH
D_FF = -(-(8 * D_MODEL) // (3 * 256)) * 256
EPS = 1e-6

kernel_name = "hybrid_mla_pool_sgmlp_encoder"


def rmsnorm(x, g):
    xf = x.astype(jnp.float32)
    y = xf * lax.rsqrt(jnp.mean(xf * xf, axis=-1, keepdims=True) + EPS)
    return (y * g.astype(jnp.float32)).astype(x.dtype)


def rope_tables(positions):
    inv_freq = ROPE_THETA ** (-jnp.arange(0, QK_ROPE_DIM, 2, dtype=jnp.float32) / QK_ROPE_DIM)
    ang = positions.astype(jnp.float32)[..., None] * inv_freq
    return jnp.cos(ang), jnp.sin(ang)


def apply_rope(t, cos, sin):
    tf = t.astype(jnp.float32)
    half = QK_ROPE_DIM // 2
    t1, t2 = tf[..., :half], tf[..., half:]
    out = jnp.concatenate([t1 * cos - t2 * sin, t1 * sin + t2 * cos], axis=-1)
    return out.astype(t.dtype)


def mla_mixer(h_q, h_kv, k_rope_in, positions, q_norm_g, kv_norm_g, w_uq, w_ukv):
    B, S, _ = h_q.shape
    cq = rmsnorm(h_q, q_norm_g)
    q = jnp.einsum('bsr,rn->bsn', cq, w_uq).reshape(B, S, MLA_HEADS, QK_NOPE_DIM + QK_ROPE_DIM)
    q_nope, q_rope = q[..., :QK_NOPE_DIM], q[..., QK_NOPE_DIM:]
    ckv = rmsnorm(h_kv, kv_norm_g)
    kv = jnp.einsum('bsr,rn->bsn', ckv, w_ukv).reshape(B, S, MLA_HEADS, QK_NOPE_DIM + V_HEAD_DIM)
    k_nope, v = kv[..., :QK_NOPE_DIM], kv[..., QK_NOPE_DIM:]
    cos, sin = rope_tables(positions)
    q_rope = apply_rope(q_rope, cos[:, :, None, :], sin[:, :, None, :])
    k_rope = apply_rope(k_rope_in, cos, sin)
    scale = (QK_NOPE_DIM + QK_ROPE_DIM) ** -0.5
    nb = S // Q_BLOCK

    def blockify(t):
        return jnp.moveaxis(t.reshape(B, nb, Q_BLOCK, *t.shape[2:]), 1, 0)

    def attend(blk):
        qn, qr = blk
        s = (jnp.einsum('bqhd,bkhd->bhqk', qn, k_nope)
             + jnp.einsum('bqhr,bkr->bhqk', qr, k_rope))
        p = jax.nn.softmax(s.astype(jnp.float32) * scale, axis=-1).astype(v.dtype)
        return jnp.einsum('bhqk,bkhd->bqhd', p, v)

    o = lax.map(attend, (blockify(q_nope), blockify(q_rope)))
    return jnp.moveaxis(o, 0, 1).reshape(B, S, MLA_WIDTH)


def pool_mixer(h, w_pool, pool_scale):
    B, S, _ = h.shape
    hf = h.astype(jnp.float32)
    cs = jnp.concatenate([jnp.zeros((B, 1, POOL_WIDTH), jnp.float32), lax.cumsum(hf, axis=1)], axis=1)
    t = jnp.arange(S)
    outs = []
    for g, w in enumerate(POOL_WINDOWS):
        left = w // 2
        right = w - 1 - left
        lo = jnp.clip(t - left, 0, S)
        hi = jnp.clip(t + right + 1, 0, S)
        sl = slice(g * POOL_GROUP_DIM, (g + 1) * POOL_GROUP_DIM)
        csg = cs[:, :, sl]
        mean = (csg[:, hi] - csg[:, lo]) / (hi - lo).astype(jnp.float32)[None, :, None]
        d = (mean - hf[:, :, sl]).astype(h.dtype)
        outs.append(jnp.einsum('bsc,cd->bsd', d, w_pool[g]))
    return jnp.concatenate(outs, axis=-1) * pool_scale


def sg_mixer(h_uv, sg_norm_g, w_s, b_s):
    B, S, _ = h_uv.shape
    z = jax.nn.gelu(h_uv)
    u, v = z[..., :SG_WIDTH], z[..., SG_WIDTH:]
    v = rmsnorm(v.reshape(B, S, SG_HEADS, SG_HEAD_DIM), sg_norm_g.reshape(SG_HEADS, SG_HEAD_DIM))
    vc = v.reshape(B, S // SG_CHUNK, SG_CHUNK, SG_HEADS, SG_HEAD_DIM)
    mixed = jnp.einsum('gpq,bcqgd->bcpgd', w_s, vc) + b_s.T[None, None, :, :, None]
    return u * mixed.reshape(B, S, SG_WIDTH)


def setup_inputs(seed: int = 0) -> dict:
    key = jax.random.key(seed)
    ks = jax.random.split(key, 20)
    f32 = jnp.float32

    def nrm(k, shape, fan_in):
        return jax.random.normal(k, shape, f32) * (fan_in ** -0.5)

    def gain(k, shape):
        return 1.0 + 0.05 * jax.random.normal(k, shape, f32)

    x = jax.random.normal(ks[0], (BATCH, SEQ, D_MODEL), f32)
    positions = jnp.broadcast_to(jnp.arange(SEQ, dtype=jnp.int32), (BATCH, SEQ))
    return {
        "x": x,
        "positions": positions,
        "mix_norm": gain(ks[1], (DEPTH, D_MODEL)),
        "w_in": nrm(ks[2], (DEPTH, D_MODEL, IN_COLS), D_MODEL),
        "q_norm": gain(ks[3], (DEPTH, Q_LORA_RANK)),
        "kv_norm": gain(ks[4], (DEPTH, KV_LORA_RANK)),
        "w_uq": nrm(ks[5], (DEPTH, Q_LORA_RANK, MLA_HEADS * (QK_NOPE_DIM + QK_ROPE_DIM)), Q_LORA_RANK),
        "w_ukv": nrm(ks[6], (DEPTH, KV_LORA_RANK, MLA_HEADS * (QK_NOPE_DIM + V_HEAD_DIM)), KV_LORA_RANK),
        "w_pool": nrm(ks[7], (DEPTH, len(POOL_WINDOWS), POOL_GROUP_DIM, POOL_GROUP_DIM), POOL_GROUP_DIM),
        "pool_scale": gain(ks[8], (DEPTH, POOL_WIDTH)),
        "sg_norm": gain(ks[9], (DEPTH, SG_WIDTH)),
        "w_s": nrm(ks[10], (DEPTH, SG_HEADS, SG_CHUNK, SG_CHUNK), SG_CHUNK),
        "b_s": 1.0 + 0.05 * jax.random.normal(ks[11], (DEPTH, SG_HEADS, SG_CHUNK), f32),
        "w_o": nrm(ks[12], (DEPTH, D_MIX, D_MODEL), D_MIX),
        "ffn_norm": gain(ks[13], (DEPTH, D_MODEL)),
        "w_gate": nrm(ks[14], (DEPTH, D_MODEL, D_FF), D_MODEL),
        "w_up": nrm(ks[15], (DEPTH, D_MODEL, D_FF), D_MODEL),
        "w_down": nrm(ks[16], (DEPTH, D_FF, D_MODEL), D_FF),
        "final_norm": gain(ks[17], (D_MODEL,)),
    }


def reference(x, positions, mix_norm, w_in, q_norm, kv_norm, w_uq, w_ukv, w_pool, pool_scale,
              sg_norm, w_s, b_s, w_o, ffn_norm, w_gate, w_up, w_down, final_norm):
    o1 = Q_LORA_RANK
    o2 = o1 + KV_LORA_RANK
    o3 = o2 + QK_ROPE_DIM
    o4 = o3 + POOL_WIDTH
    for l in range(DEPTH):
        h = rmsnorm(x, mix_norm[l])
        p = jnp.einsum('bsd,dn->bsn', h, w_in[l])
        a = mla_mixer(p[..., :o1], p[..., o1:o2], p[..., o2:o3], positions,
                      q_norm[l], kv_norm[l], w_uq[l], w_ukv[l])
        b = pool_mixer(p[..., o3:o4], w_pool[l], pool_scale[l])
        c = sg_mixer(p[..., o4:], sg_norm[l], w_s[l], b_s[l])
        mix = jnp.concatenate([a, b, c], axis=-1)
        x = x + jnp.einsum('bsn,nd->bsd', mix, w_o[l])
        h = rmsnorm(x, ffn_norm[l])
        g = jnp.einsum('bsd,df->bsf', h, w_gate[l])
        u = jnp.einsum('bsd,df->bsf', h, w_up[l])
        x = x + jnp.einsum('bsf,fd->bsd', jax.nn.silu(g) * u, w_down[l])
    return rmsnorm(x, final_norm)
```

```python
import numpy as np
import ml_dtypes
from contextlib import ExitStack
import concourse.bass as bass
import concourse.mybir as mybir
from concourse.bass_utils import run_bass_kernel_spmd

F32 = mybir.dt.float32
BF16 = mybir.dt.bfloat16
I32 = mybir.dt.int32
AF = mybir.ActivationFunctionType
ALU = mybir.AluOpType
AX = mybir.AxisListType

NCORES = 8
D = 1024
T = 4096
NT = 32
S = 8192
H = 8
DFF = 2816
NFF = 22
EPS = 1e-6
SCALE = 96.0 ** -0.5
PI = float(np.pi)
FUSED = True


class Tl:
    __slots__ = ("name", "w", "r")

    def __init__(self, name=""):
        self.name = name
        self.w = []
        self.r = []


class Sched:
    STREAMS = ("pe", "act", "dve", "pool", "sp")
    KRING = 8

    def __init__(self, same_sync=True):
        self.ops = {s: [] for s in self.STREAMS}
        self.ndma = {s: 0 for s in self.STREAMS}
        self.known_e = {s: {} for s in self.STREAMS}
        self.known_d = {s: set() for s in self.STREAMS}
        self.same_sync = same_sync
        self.tiles = []
        self.ncc = 0

    def tile(self, name=""):
        t = Tl(name)
        self.tiles.append(t)
        return t

    def _filter(self, stream, deps):
        ke = self.known_e[stream]
        kd = self.known_d[stream]
        best = {}
        res = []
        for tok in deps:
            if tok[0] == "e":
                _, P, i = tok
                if P == stream and (stream == "pe" or not self.same_sync):
                    continue
                if ke.get(P, -1) >= i:
                    continue
                if best.get(P, -1) < i:
                    best[P] = i
            else:
                if tok in kd:
                    continue
                kd.add(tok)
                res.append(tok)
        for P, i in best.items():
            res.append(("e", P, i))
            ke[P] = i
            self.ops[P][i]["inc"] = True
        return res

    def add(self, stream, fn, reads=(), writes=(), dma=False, cc=False):
        deps = set()
        for t in reads:
            deps.update(t.w)
        for t in writes:
            deps.update(t.w)
            deps.update(t.r)
        ops = self.ops[stream]
        idx = len(ops)
        if dma:
            j = self.ndma[stream]
            self.ndma[stream] += 1
            tok = ("d", stream, j)
            if j >= self.KRING:
                deps.add(("d", stream, j - self.KRING))
        elif cc:
            tok = ("c", "cc", self.ncc)
            if self.ncc > 0:
                deps.add(("c", "cc", self.ncc - 1))
            self.ncc += 1
        else:
            tok = ("e", stream, idx)
        waits = self._filter(stream, deps)
        ops.append(dict(fn=fn, waits=waits, dma=dma, cc=cc, tok=tok, inc=False))
        for t in reads:
            if tok[0] == "e":
                t.r = [x for x in t.r if not (x[0] == "e" and x[1] == tok[1])]
            t.r.append(tok)
        for t in writes:
            t.w = [tok]
            t.r = []
        return tok

    def barrier(self):
        toks = []
        for s in self.STREAMS:
            ops = self.ops[s]
            for i in range(len(ops) - 1, -1, -1):
                if (not ops[i]["dma"]) and (not ops[i].get("cc")) and ops[i]["fn"] is not None:
                    toks.append(("e", s, i))
                    break
            n = self.ndma[s]
            for j in range(max(0, n - self.KRING), n):
                toks.append(("d", s, j))
        if self.ncc > 0:
            toks.append(("c", "cc", self.ncc - 1))
        for s in self.STREAMS:
            waits = self._filter(s, set(toks))
            self.ops[s].append(dict(fn=None, waits=waits, dma=False, tok=None, inc=False))
        for t in self.tiles:
            t.w = []
            t.r = []

    def emit(self, nc, es):
        K = self.KRING
        esem = {s: es.enter_context(nc.semaphore("e_" + s)) for s in self.STREAMS}
        csem = es.enter_context(nc.semaphore("ccsem"))
        dsem = {s: [es.enter_context(nc.semaphore("d_%s_%d" % (s, k))) for k in range(K)]
                for s in ("sp", "pool", "act")}
        for s in self.STREAMS:
            c = 0
            for op in self.ops[s]:
                if op["inc"]:
                    c += 1
                op["cnt"] = c
        ops_all = self.ops

        def run(stream, eng):
            for op in ops_all[stream]:
                for tok in op["waits"]:
                    if tok[0] == "e":
                        eng.wait_ge(esem[tok[1]], ops_all[tok[1]][tok[2]]["cnt"])
                    elif tok[0] == "c":
                        eng.wait_ge(csem, tok[2] + 1)
                    else:
                        eng.wait_ge(dsem[tok[1]][tok[2] % K], 16 * (tok[2] // K + 1))
                if op["fn"] is None:
                    continue
                ins = op["fn"](eng)
                if op["dma"]:
                    j = op["tok"][2]
                    ins.then_inc(dsem[stream][j % K], 16)
                elif op.get("cc"):
                    ins.then_inc(csem, 1)
                elif op["inc"]:
                    ins.then_inc(esem[stream], 1)

        with nc.Block() as block:
            @block.tensor
            def _(e):
                run("pe", e)

            @block.scalar
            def _(e):
                run("act", e)

            @block.vector
            def _(e):
                run("dve", e)

            @block.gpsimd
            def _(e):
                run("pool", e)

            @block.sync
            def _(e):
                run("sp", e)


class Arena:
    def __init__(self, nc, sc, nbytes):
        self.sc = sc
        self.nb = nbytes
        self.t = nc.alloc_sbuf_tensor("arena", [128, nbytes // 2], BF16)
        self.off = 0

    def mark(self):
        return self.off

    def release(self, m):
        self.off = m

    def alloc(self, name, free, dtype, tl=True):
        esz = 4 if dtype in (F32, I32) else 2
        n = 1
        for f in free:
            n *= f
        nby = (n * esz + 63) // 64 * 64
        assert self.off + nby <= self.nb, ("SBUF arena overflow", name, self.off, nby)
        v = self.t[:, self.off // 2:(self.off + n * esz) // 2]
        if dtype != BF16:
            v = v.bitcast(dtype)
        if len(free) == 2:
            v = v.rearrange("p (a b) -> p a b", b=free[1])
        elif len(free) == 3:
            v = v.rearrange("p (a b c) -> p a b c", b=free[1], c=free[2])
        self.off += nby
        return v, (self.sc.tile(name) if tl else None)


def MM(sc, out, lhsT, rhs, start, stop, R, W):
    sc.add("pe", lambda e: e.matmul(out, lhsT, rhs, start=start, stop=stop), R, W)


def TR(sc, out, in_, ident, R, W):
    sc.add("pe", lambda e: e.transpose(out, in_, ident), R, W)


def ACTF(sc, out, in_, func, R, W, bias=0.0, scale=1.0, accum=None):
    if accum is None:
        sc.add("act", lambda e: e.activation(out, in_, func, bias=bias, scale=scale), R, W)
    else:
        sc.add("act", lambda e: e.activation(out, in_, func, bias=bias, scale=scale,
                                             accum_out=accum), R, W)


def TS(sc, eng, out, in0, s1, s2, op0, op1, R, W):
    if op1 is None:
        sc.add(eng, lambda e: e.tensor_scalar(out, in0, s1, None, op0), R, W)
    else:
        sc.add(eng, lambda e: e.tensor_scalar(out, in0, s1, s2, op0, op1), R, W)


def TT(sc, eng, out, in0, in1, op, R, W):
    sc.add(eng, lambda e: e.tensor_tensor(out, in0, in1, op), R, W)


def STT(sc, eng, out, in0, scalar, in1, op0, op1, R, W):
    sc.add(eng, lambda e: e.scalar_tensor_tensor(out, in0, scalar, in1, op0, op1), R, W)


def CP(sc, eng, out, in_, R, W):
    if eng == "act":
        sc.add("act", lambda e: e.copy(out, in_), R, W)
    else:
        sc.add(eng, lambda e: e.tensor_copy(out, in_), R, W)


def RSTD(sc, out, ssq, n, eps_ap, R, W):
    sc.add("act", lambda e: e.activation(out, ssq, AF.Sqrt, bias=eps_ap, scale=1.0 / n), R, W)
    sc.add("dve", lambda e: e.reciprocal(out, out), W, W)


def DMA(sc, stream, out, in_, R, W):
    sc.add(stream, lambda e: e.dma_start(out=out, in_=in_), R, W, dma=True)


class Prog:
    def __init__(self, post_layer, a_layer, final, fused=False):
        self.post_layer = post_layer
        self.a_layer = a_layer
        self.final = final
        self.fused = fused
        self.nc = bass.Bass("TRN2", target_bir_lowering=False)
        self.sc = Sched()
        self.ext_in = []
        self.ext_out = []
        self.dr = {}

    def din(self, name, shape, dtype):
        h = self.nc.dram_tensor(name, list(shape), dtype, kind="ExternalInput")
        self.ext_in.append(name)
        self.dr[name] = h
        return h

    def dout(self, name, shape, dtype):
        h = self.nc.dram_tensor(name, list(shape), dtype, kind="ExternalOutput")
        self.ext_out.append(name)
        self.dr[name] = h
        return h

    def dint(self, name, shape, dtype):
        key = name.rstrip("0123456789") if self.fused else name
        if not hasattr(self, "_shared"):
            self._shared = {}
        if key not in self._shared:
            self._shared[key] = self.nc.dram_tensor(key, list(shape), dtype)
        h = self._shared[key]
        self.dr[name] = h
        return h

    def build(self):
        nc, sc = self.nc, self.sc
        self.A = Arena(nc, sc, 210500)
        A = self.A
        self.ps2 = []
        self.pst = []
        for b in range(4):
            self.ps2.append(nc.alloc_psum_tensor("psd%d" % b, [128, 1024], F32))
        for b in range(8):
            self.pst.append(sc.tile("ps%d" % b))
        ident_d = self.din("ident", [128, 128], F32)
        self.identf, tl_if = A.alloc("identf", (128,), F32)
        self.ident, self.tl_id = A.alloc("ident", (128,), BF16)
        DMA(sc, "sp", self.identf, ident_d[:, :], [], [tl_if])
        CP(sc, "dve", self.ident, self.identf, [tl_if], [self.tl_id])
        self.epsb, self.tl_eps = A.alloc("epsb", (1,), F32)
        epsb = self.epsb
        sc.add("dve", lambda e: e.memset(epsb, EPS), [], [self.tl_eps])
        self.base_mark = A.mark()

        import os
        self.onel = bool(os.environ.get("ONEL"))
        if self.fused and self.onel:
            self.stage_a(0)
            self.exchange(0)
            self.stage_post(0)
        elif self.fused:
            self.stage_a(0)
            self.exchange(0)
            self.stage_post(0)
            self.stage_a(1)
            self.exchange(1)
            self.stage_post(1)
        else:
            if self.post_layer is not None:
                self.stage_post(self.post_layer)
            if self.a_layer is not None:
                self.stage_a(self.a_layer)
        sc.barrier()
        with ExitStack() as es:
            sc.emit(nc, es)
        return nc

    def psb(self, b, dtype=F32):
        v = self.ps2[b // 2][:, (b % 2) * 512:(b % 2 + 1) * 512]
        if dtype == BF16:
            v = v.bitcast(BF16)
        return v

    def psd(self, k):
        return self.ps2[k][:, :]

    def setup_stage_ring(self, ncols=1440):
        A = self.A
        self.wst = [A.alloc("wst%d" % i, (ncols,), F32) for i in range(3)]
        self.wst_i = 0

    def load_cast(self, dst, dst_tl, src, ncols, scale=None, eng=None, scale_tl=None):
        sc = self.sc
        sR = [scale_tl] if scale_tl is not None else []
        stg, stl = self.wst[self.wst_i % 3]
        if eng is None:
            eng = ("pool", "dve", "act")[self.wst_i % 3]
        self.wst_i += 1
        sv = stg[:, 0:ncols]
        DMA(sc, "sp", sv, src, [], [stl])
        if scale is None:
            CP(sc, eng, dst, sv, [stl], [dst_tl])
        elif eng == "act":
            ACTF(sc, dst, sv, AF.Copy, [stl] + sR, [dst_tl], scale=scale)
        else:
            TS(sc, eng, dst, sv, scale, None, ALU.mult, None, [stl] + sR, [dst_tl])

    def load_f32(self, dst, dst_tl, src):
        DMA(self.sc, "sp", dst, src, [], [dst_tl])

    def stage_a(self, l):
        nc, sc, A = self.nc, self.sc, self.A
        sc.barrier()
        A.release(self.base_mark)
        sfx = "_a%d" % l
        if self.fused:
            xin = self.din("x_in" + sfx, [T, D], F32) if l == 0 else self.dr["xout_p0"]
        elif self.post_layer is not None:
            xin = self.dr["xout_p%d" % self.post_layer]
        else:
            xin = self.din("x_in" + sfx, [T, D], F32)
        mk_out = self.dint if self.fused else self.dout
        latT = mk_out("latT" + sfx, [288, T], BF16)
        halo = mk_out("halo" + sfx, [256, 16], F32)
        QT = mk_out("QT" + sfx, [H * 96, T], BF16)
        ppT = mk_out("ppT" + sfx, [256, T], F32)
        cT = mk_out("cT" + sfx, [256, T], BF16)
        pos_d = self.din("pos" + sfx, [128, NT], I32)
        invf_d = self.din("invf" + sfx, [128, 16], F32)
        w_in_d = self.din("w_in" + sfx, [128, 8, 1440], F32)
        mixg_d = self.din("mixg" + sfx, [128, 8], F32)
        w_uq_d = self.din("w_uq" + sfx, [128, 3, 768], F32)
        qg_d = self.din("qg" + sfx, [128, 3], F32)
        kvg_d = self.din("kvg" + sfx, [128, 256], F32)
        sgg_d = self.din("sgg" + sfx, [128, 256], F32)
        wsT_d = self.din("wsT" + sfx, [128, 4, 128], F32)
        bsT_d = self.din("bsT" + sfx, [128, 4], F32)

        self.setup_stage_ring()
        w_in, tl_win = A.alloc("w_in", (8, 1440), BF16)
        w_uq, tl_wuq = A.alloc("w_uq", (3, 768), BF16)
        wsT, tl_wsT = A.alloc("wsT", (4, 128), BF16)
        mixg, tl_mixg = A.alloc("mixg", (8,), F32)
        qg, tl_qg = A.alloc("qg", (3,), F32)
        kvg, tl_kvg = A.alloc("kvg", (256,), F32)
        sgg, tl_sgg = A.alloc("sgg", (256,), F32)
        bsT, tl_bsT = A.alloc("bsT", (4,), F32)
        self.load_f32(mixg, tl_mixg, mixg_d[:, :])
        self.load_f32(qg, tl_qg, qg_d[:, :])
        self.load_f32(kvg, tl_kvg, kvg_d[:, :])
        self.load_f32(sgg, tl_sgg, sgg_d[:, :])
        self.load_f32(bsT, tl_bsT, bsT_d[:, :])
        for k in range(8):
            self.load_cast(w_in[:, k, :], tl_win, w_in_d[:, k, :], 1440, scale=mixg[:, k:k + 1],
                           scale_tl=tl_mixg)
        for k in range(3):
            self.load_cast(w_uq[:, k, :], tl_wuq, w_uq_d[:, k, :], 768, scale=qg[:, k:k + 1],
                           scale_tl=tl_qg)
        self.load_cast(wsT.rearrange("p a b -> p (a b)"), tl_wsT,
                       wsT_d[:, :, :].rearrange("p a b -> p (a b)"), 512)

        posi, tl_posi = A.alloc("posi", (NT,), I32)
        posf, tl_posf = A.alloc("posf", (NT,), F32)
        invf, tl_invf = A.alloc("invf", (16,), F32)
        ang, tl_ang = A.alloc("ang", (NT, 16), F32)
        ang2, tl_ang2 = A.alloc("ang2", (NT, 16), F32)
        cost, tl_cos = A.alloc("cost", (NT, 16), F32)
        sint, tl_sin = A.alloc("sint", (NT, 16), F32)
        angi, tl_angi = A.alloc("angi", (NT, 16), I32)
        DMA(sc, "sp", posi, pos_d[:, :], [], [tl_posi])
        DMA(sc, "sp", invf, invf_d[:, :], [], [tl_invf])
        CP(sc, "dve", posf, posi, [tl_posi], [tl_posf])
        TT(sc, "dve", ang, posf.unsqueeze(2).to_broadcast([128, NT, 16]),
           invf.unsqueeze(1).to_broadcast([128, NT, 16]), ALU.mult, [tl_posf, tl_invf], [tl_ang])
        for (dst, dtl, shift) in ((sint, tl_sin, 0.0), (cost, tl_cos, 0.5 * PI)):
            TS(sc, "dve", ang2, ang, shift, 1.0 / (2 * PI), ALU.add, ALU.mult, [tl_ang], [tl_ang2])
            CP(sc, "dve", angi, ang2, [tl_ang2], [tl_angi])
            CP(sc, "dve", ang2, angi, [tl_angi], [tl_ang2])
            STT(sc, "dve", ang2, ang2, -2 * PI, ang, ALU.mult, ALU.add, [tl_ang2, tl_ang], [tl_ang2])
            if shift != 0.0:
                TS(sc, "dve", ang2, ang2, shift, None, ALU.add, None, [tl_ang2], [tl_ang2])
            TS(sc, "dve", ang2, ang2, PI, -PI, ALU.min, ALU.max, [tl_ang2], [tl_ang2])
            ACTF(sc, dst, ang2, AF.Sin, [tl_ang2], [dtl])
        xt = [A.alloc("xt%d" % i, (D,), F32) for i in range(2)]
        junk, tl_junk = A.alloc("junk", (D,), BF16)
        hn = [A.alloc("hn%d" % i, (D,), BF16) for i in range(2)]
        hT = [A.alloc("hT%d" % i, (8, 128), BF16) for i in range(2)]
        st = [A.alloc("st%d" % i, (16,), F32) for i in range(2)]
        cq = [A.alloc("cq%d" % i, (384,), BF16) for i in range(2)]
        cqT = [A.alloc("cqT%d" % i, (3, 128), BF16) for i in range(2)]
        qsb = [A.alloc("qsb%d" % i, (8, 96), BF16) for i in range(2)]
        rm1 = [A.alloc("rm1%d" % i, (8, 2, 16), F32) for i in range(2)]
        rm2 = [A.alloc("rm2%d" % i, (8, 2, 16), F32) for i in range(2)]
        km1 = [A.alloc("km1%d" % i, (2, 16), F32) for i in range(2)]
        km2 = [A.alloc("km2%d" % i, (2, 16), F32) for i in range(2)]
        lat = [A.alloc("lat%d" % i, (288,), BF16) for i in range(2)]
        latf = [A.alloc("latf%d" % i, (256,), F32) for i in range(2)]
        z = [A.alloc("z%d" % i, (512,), F32) for i in range(2)]
        vsq = [A.alloc("vsq%d" % i, (256,), F32) for i in range(2)]
        vtmp = [A.alloc("vtmp%d" % i, (256,), F32) for i in range(2)]
        vn = [A.alloc("vn%d" % i, (256,), BF16) for i in range(2)]
        cc = [A.alloc("cc%d" % i, (256,), BF16) for i in range(2)]
        QTst = [A.alloc("QTst%d" % i, (8, 512), BF16) for i in range(2)]
        latTst = [A.alloc("latTst%d" % i, (3, 512), BF16) for i in range(2)]
        ppTst = [A.alloc("ppTst%d" % i, (2, 512), F32) for i in range(2)]
        cTst = [A.alloc("cTst%d" % i, (2, 512), BF16) for i in range(2)]

        ident, tl_id = self.ident, self.tl_id
        B_TPA, B_TPB, B_PA, B_PB, B_PC, B_PP, B_Q0, B_Q1 = range(8)
        pst = self.pst
        qps = self.psb(B_Q0)
        qps2 = self.psb(B_Q1)

        for i in range(NT):
            par = i % 2
            g4, j4 = i // 4, i % 4
            gp = g4 % 2
            x_ap, x_tl = xt[par]
            hn_ap, hn_tl = hn[par]
            hT_ap, hT_tl = hT[par]
            st_ap, st_tl = st[par]
            DMA(sc, "sp", x_ap, xin[i * 128:(i + 1) * 128, :], [], [x_tl])
            ACTF(sc, junk, x_ap, AF.Square, [x_tl], [tl_junk, st_tl], accum=st_ap[:, 0:1])
            RSTD(sc, st_ap[:, 1:2], st_ap[:, 0:1], D, self.epsb[:, 0:1], [st_tl, self.tl_eps],
                 [st_tl])
            TS(sc, "dve", hn_ap, x_ap, st_ap[:, 1:2], None, ALU.mult, None,
               [x_tl, st_tl], [hn_tl])
            tpa = self.psb(B_TPA, BF16)
            for k in range(8):
                TR(sc, tpa[:, k * 128:(k + 1) * 128], hn_ap[:, k * 128:(k + 1) * 128], ident,
                   [hn_tl, tl_id], [pst[B_TPA]])
            CP(sc, "act", hT_ap.rearrange("p a b -> p (a b)"), tpa, [pst[B_TPA]], [hT_tl])
            pA = self.psb(B_PA)
            pB = self.psb(B_PB)
            pC = self.psb(B_PC)
            pP = self.psb(B_PP)
            for k in range(8):
                MM(sc, pA[:, 0:416], hT_ap[:, k, :], w_in[:, k, 0:416], k == 0, k == 7,
                   [hT_tl, tl_win], [pst[B_PA]])
            for k in range(8):
                MM(sc, pB[:, 0:256], hT_ap[:, k, :], w_in[:, k, 416:672], k == 0, k == 7,
                   [hT_tl, tl_win], [pst[B_PB]])
            for k in range(8):
                MM(sc, pC[:, 0:512], hT_ap[:, k, :], w_in[:, k, 672:1184], k == 0, k == 7,
                   [hT_tl, tl_win], [pst[B_PC]])
            for c in range(2):
                for k in range(8):
                    MM(sc, pP[:, c * 128:(c + 1) * 128],
                       w_in[:, k, 1184 + c * 128:1184 + (c + 1) * 128], hT_ap[:, k, :],
                       k == 0, k == 7, [hT_tl, tl_win], [pst[B_PP]])
            pp_ap, pp_tl = ppTst[gp]
            CP(sc, "dve", pp_ap[:, :, j4 * 128:(j4 + 1) * 128],
               pP[:, 0:256].rearrange("p (c t) -> p c t", c=2), [pst[B_PP]], [pp_tl])
            cq_ap, cq_tl = cq[par]
            ACTF(sc, junk[:, 0:384], pA[:, 0:384], AF.Square, [pst[B_PA]], [tl_junk, st_tl],
                 accum=st_ap[:, 2:3])
            RSTD(sc, st_ap[:, 3:4], st_ap[:, 2:3], 384, self.epsb[:, 0:1], [st_tl, self.tl_eps],
                 [st_tl])
            TS(sc, "dve", cq_ap, pA[:, 0:384], st_ap[:, 3:4], None, ALU.mult,
               None, [pst[B_PA], st_tl], [cq_tl])
            tpb = self.psb(B_TPB, BF16)
            for k in range(3):
                TR(sc, tpb[:, k * 128:(k + 1) * 128], cq_ap[:, k * 128:(k + 1) * 128], ident,
                   [cq_tl, tl_id], [pst[B_TPB]])
            cqT_ap, cqT_tl = cqT[par]
            CP(sc, "act", cqT_ap.rearrange("p a b -> p (a b)"), tpb[:, 0:384], [pst[B_TPB]],
               [cqT_tl])
            lat_ap, lat_tl = lat[par]
            latf_ap, latf_tl = latf[par]
            ACTF(sc, junk[:, 0:256], pB[:, 0:256], AF.Square, [pst[B_PB]], [tl_junk, st_tl],
                 accum=st_ap[:, 4:5])
            RSTD(sc, st_ap[:, 5:6], st_ap[:, 4:5], 256, self.epsb[:, 0:1], [st_tl, self.tl_eps],
                 [st_tl])
            TS(sc, "dve", latf_ap, pB[:, 0:256], st_ap[:, 5:6], None, ALU.mult, None,
               [pst[B_PB], st_tl], [latf_tl])
            TT(sc, "dve", lat_ap[:, 0:256], latf_ap, kvg, ALU.mult, [latf_tl, tl_kvg], [lat_tl])
            cos_i = cost[:, i, :]
            sin_i = sint[:, i, :]
            k1_ap, k1_tl = km1[par]
            k2_ap, k2_tl = km2[par]
            krv = pA[:, 384:416].rearrange("p (a b) -> p a b", a=2)
            TT(sc, "dve", k1_ap, krv, cos_i.unsqueeze(1).to_broadcast([128, 2, 16]), ALU.mult,
               [pst[B_PA], tl_cos], [k1_tl])
            TT(sc, "dve", k2_ap, krv, sin_i.unsqueeze(1).to_broadcast([128, 2, 16]), ALU.mult,
               [pst[B_PA], tl_sin], [k2_tl])
            TT(sc, "dve", lat_ap[:, 256:272], k1_ap[:, 0, :], k2_ap[:, 1, :], ALU.subtract,
               [k1_tl, k2_tl], [lat_tl])
            TT(sc, "dve", lat_ap[:, 272:288], k2_ap[:, 0, :], k1_ap[:, 1, :], ALU.add,
               [k1_tl, k2_tl], [lat_tl])
            for k in range(3):
                MM(sc, qps[:, 0:512], cqT_ap[:, k, :], w_uq[:, k, 0:512], k == 0, k == 2,
                   [cqT_tl, tl_wuq], [pst[B_Q0]])
            for k in range(3):
                MM(sc, qps2[:, 0:256], cqT_ap[:, k, :], w_uq[:, k, 512:768], k == 0, k == 2,
                   [cqT_tl, tl_wuq], [pst[B_Q1]])
            q_ap, q_tl = qsb[par]
            r1_ap, r1_tl = rm1[par]
            r2_ap, r2_tl = rm2[par]
            def rope_batch(src, srct, h0, nh):
                sv = src.rearrange("p (h d) -> p h d", h=nh)
                CP(sc, "act", q_ap[:, h0:h0 + nh, 0:64], sv[:, :, 0:64], srct, [q_tl])
                rv = sv[:, :, 64:96].rearrange("p h (a b) -> p h a b", a=2)
                cb = cos_i.unsqueeze(1).unsqueeze(1).to_broadcast([128, nh, 2, 16])
                sb_ = sin_i.unsqueeze(1).unsqueeze(1).to_broadcast([128, nh, 2, 16])
                TT(sc, "dve", r1_ap[:, h0:h0 + nh, :, :], rv, cb, ALU.mult, srct + [tl_cos], [r1_tl])
                TT(sc, "dve", r2_ap[:, h0:h0 + nh, :, :], rv, sb_, ALU.mult, srct + [tl_sin], [r2_tl])

            rope_batch(qps[:, 0:480], [pst[B_Q0]], 0, 5)
            rope_batch(qps2[:, 64:256], [pst[B_Q1]], 6, 2)
            CP(sc, "act", q_ap[:, 5, 0:32], qps[:, 480:512], [pst[B_Q0]], [q_tl])
            CP(sc, "act", q_ap[:, 5, 32:64], qps2[:, 0:32], [pst[B_Q1]], [q_tl])
            rv5 = qps2[:, 32:64].rearrange("p (a b) -> p a b", a=2)
            TT(sc, "dve", r1_ap[:, 5, :, :], rv5, cos_i.unsqueeze(1).to_broadcast([128, 2, 16]),
               ALU.mult, [pst[B_Q1], tl_cos], [r1_tl])
            TT(sc, "dve", r2_ap[:, 5, :, :], rv5, sin_i.unsqueeze(1).to_broadcast([128, 2, 16]),
               ALU.mult, [pst[B_Q1], tl_sin], [r2_tl])
            TT(sc, "dve", q_ap[:, :, 64:80], r1_ap[:, :, 0, :], r2_ap[:, :, 1, :], ALU.subtract,
               [r1_tl, r2_tl], [q_tl])
            TT(sc, "dve", q_ap[:, :, 80:96], r2_ap[:, :, 0, :], r1_ap[:, :, 1, :], ALU.add,
               [r1_tl, r2_tl], [q_tl])
            for h in range(H):
                TR(sc, tpb[0:96, h * 128:(h + 1) * 128], q_ap[:, h, :], ident, [q_tl, tl_id],
                   [pst[B_TPB]])
            qt_ap, qt_tl = QTst[gp]
            CP(sc, "act", qt_ap[0:96, :, j4 * 128:(j4 + 1) * 128],
               tpb[0:96, :].rearrange("p (h t) -> p h t", h=8), [pst[B_TPB]], [qt_tl])
            lt_ap, lt_tl = latTst[gp]
            TR(sc, tpa[:, 0:128], lat_ap[:, 0:128], ident, [lat_tl, tl_id], [pst[B_TPA]])
            TR(sc, tpa[:, 128:256], lat_ap[:, 128:256], ident, [lat_tl, tl_id], [pst[B_TPA]])
            TR(sc, tpa[0:32, 256:384], lat_ap[:, 256:288], ident, [lat_tl, tl_id], [pst[B_TPA]])
            CP(sc, "act", lt_ap[:, 0:2, j4 * 128:(j4 + 1) * 128],
               tpa[:, 0:256].rearrange("p (c t) -> p c t", c=2), [pst[B_TPA]], [lt_tl])
            CP(sc, "act", lt_ap[0:32, 2, j4 * 128:(j4 + 1) * 128], tpa[0:32, 256:384],
               [pst[B_TPA]], [lt_tl])
            z_ap, z_tl = z[par]
            ACTF(sc, z_ap, pC[:, 0:512], AF.Gelu_apprx_tanh, [pst[B_PC]], [z_tl])
            vs_ap, vs_tl = vsq[par]
            vt_ap, vt_tl = vtmp[par]
            vn_ap, vn_tl = vn[par]
            TT(sc, "dve", vs_ap, z_ap[:, 256:512], z_ap[:, 256:512], ALU.mult, [z_tl], [vs_tl])
            sc.add("dve", (lambda o, a: (lambda e: e.reduce_sum(o, a, AX.X)))(
                st_ap[:, 8:12], vs_ap.rearrange("p (g d) -> p g d", g=4)), [vs_tl], [st_tl])
            RSTD(sc, st_ap[:, 12:16], st_ap[:, 8:12], 64, self.epsb[:, 0:1], [st_tl, self.tl_eps],
                 [st_tl])
            TT(sc, "dve", vt_ap.rearrange("p (g d) -> p g d", g=4),
               z_ap[:, 256:512].rearrange("p (g d) -> p g d", g=4),
               st_ap[:, 12:16].unsqueeze(2).to_broadcast([128, 4, 64]), ALU.mult,
               [z_tl, st_tl], [vt_tl])
            TT(sc, "dve", vn_ap, vt_ap, sgg, ALU.mult, [vt_tl, tl_sgg], [vn_tl])
            for g in range(4):
                MM(sc, pP[:, 256 + g * 64:256 + (g + 1) * 64], wsT[:, g, :],
                   vn_ap[:, g * 64:(g + 1) * 64], True, True, [vn_tl, tl_wsT], [pst[B_PP]])
            c_ap, c_tl = cc[par]
            for g in range(4):
                STT(sc, "dve", c_ap[:, g * 64:(g + 1) * 64], pP[:, 256 + g * 64:256 + (g + 1) * 64],
                    bsT[:, g:g + 1], z_ap[:, g * 64:(g + 1) * 64], ALU.add, ALU.mult,
                    [pst[B_PP], tl_bsT, z_tl], [c_tl])
            TR(sc, tpb[:, 0:128], c_ap[:, 0:128], ident, [c_tl, tl_id], [pst[B_TPB]])
            TR(sc, tpb[:, 128:256], c_ap[:, 128:256], ident, [c_tl, tl_id], [pst[B_TPB]])
            ct_ap, ct_tl = cTst[gp]
            CP(sc, "act", ct_ap[:, :, j4 * 128:(j4 + 1) * 128],
               tpb[:, 0:256].rearrange("p (c t) -> p c t", c=2), [pst[B_TPB]], [ct_tl])
            if j4 == 3:
                ts_ = slice(g4 * 512, (g4 + 1) * 512)
                DMA(sc, "pool", QT[:, ts_].rearrange("(h r) t -> r h t", h=8), qt_ap[0:96, :, :],
                    [qt_tl], [])
                DMA(sc, "pool", latT[0:256, ts_].rearrange("(c p) t -> p c t", c=2),
                    lt_ap[:, 0:2, :], [lt_tl], [])
                DMA(sc, "pool", latT[256:288, ts_], lt_ap[0:32, 2, :], [lt_tl], [])
                DMA(sc, "pool", ppT[:, ts_].rearrange("(c p) t -> p c t", c=2), pp_ap, [pp_tl], [])
                DMA(sc, "pool", cT[:, ts_].rearrange("(c p) t -> p c t", c=2), ct_ap, [ct_tl], [])
                if g4 == 0:
                    DMA(sc, "pool", halo[:, 0:8].rearrange("(c p) t -> p c t", c=2),
                        pp_ap[:, :, 0:8], [pp_tl], [])
                if g4 == 7:
                    DMA(sc, "pool", halo[:, 8:16].rearrange("(c p) t -> p c t", c=2),
                        pp_ap[:, :, 504:512], [pp_tl], [])

    def exchange(self, l):
        sc = self.sc
        sc.barrier()
        latT = self.dr["latT_a%d" % l]
        halo = self.dr["halo_a%d" % l]
        g1 = self.dint("g1_%d" % l, [2 * 128, T], BF16)
        g2 = self.dint("g2_%d" % l, [2 * 160, T], BF16)
        gh = self.dint("gh_%d" % l, [2 * 256, 16], F32)
        tl = sc.tile("gath%d" % l)
        self.tl_g = [tl]
        pairs = [[0, 1], [2, 3], [4, 5], [6, 7]]
        import os
        if os.environ.get("NOCC"):
            for (src, dst, n) in ((latT[0:128, :], g1, 128), (latT[128:288, :], g2, 160),
                                  (halo[:, :], gh, 256)):
                for r in range(2):
                    DMA(sc, "pool", dst[r * n:(r + 1) * n, :], src, [], [tl])
            return
        for (src, dst) in ((latT[0:128, :], g1), (latT[128:288, :], g2), (halo[:, :], gh)):
            sc.add("pool", (lambda a, b: (lambda e: e.collective_compute(
                "AllGather", ALU.bypass, replica_groups=pairs, ins=[a], outs=[b[:, :]])))(src, dst),
                [], [tl], cc=True)

    def stage_post(self, l):
        nc, sc, A = self.nc, self.sc, self.A
        sfx = "_p%d" % l
        pst = self.pst
        ident, tl_id = self.ident, self.tl_id
        if self.fused:
            final = (l == 1) or self.onel
            xin = self.dr["x_in_a0"] if l == 0 else self.dr["xout_p0"]
            g1, g2, gh = self.dr["g1_%d" % l], self.dr["g2_%d" % l], self.dr["gh_%d" % l]
            tl_g = self.tl_g
            ckv_src = lambda r, c: (g1[r * 128:(r + 1) * 128, :] if c == 0
                                    else g2[r * 160:r * 160 + 128, :])
            kr_src = lambda r: g2[r * 160 + 128:r * 160 + 160, :]
            halo_src = lambda r, a, b: gh[r * 256:(r + 1) * 256, a:b]
            QT = self.dr["QT_a%d" % l]
            ppT = self.dr["ppT_a%d" % l]
            cTd = self.dr["cT_a%d" % l]
            xmid_ = self.dint("xmid" + sfx, [T, D], F32)
            if final:
                xout = self.dout("y_out", [T, D], F32)
            else:
                xout = xmid_
                self.dr["xout" + sfx] = xout
        else:
            final = self.final
            tl_g = []
            xin = self.din("x_in" + sfx, [T, D], F32)
            latp = self.din("latp" + sfx, [2 * 288, T], BF16)
            halop = self.din("halop" + sfx, [2 * 256, 16], F32)
            ckv_src = lambda r, c: latp[r * 288 + c * 128:r * 288 + (c + 1) * 128, :]
            kr_src = lambda r: latp[r * 288 + 256:r * 288 + 288, :]
            halo_src = lambda r, a, b: halop[r * 256:(r + 1) * 256, a:b]
            QT = self.din("QT" + sfx, [H * 96, T], BF16)
            ppT = self.din("ppT" + sfx, [256, T], F32)
            cTd = self.din("cT" + sfx, [256, T], BF16)
            if final:
                xout = self.dout("y_out", [T, D], F32)
            else:
                xout = self.dout("xout" + sfx, [T, D], F32)
        xmid = self.dint("xmid" + sfx, [T, D], F32)
        hmask_d = self.din("hmask" + sfx, [128, 2], F32)
        edge_d = self.din("edge" + sfx, [128, 2, 16], F32)
        wkn_d = self.din("wkn" + sfx, [128, 2, 512], F32)
        wv_d = self.din("wv" + sfx, [128, 2, 512], F32)
        wpb_d = self.din("wpb" + sfx, [128, 2, 128], F32)
        psc_d = self.din("psc" + sfx, [128, 2], F32)
        w_o_d = self.din("w_o" + sfx, [128, 8, 1024], F32)
        ffng_d = self.din("ffng" + sfx, [128, 8], F32)
        wg_d = self.din("w_gate" + sfx, [128, 8, DFF], F32)
        wu_d = self.din("w_up" + sfx, [128, 8, DFF], F32)
        wd_d = self.din("w_down" + sfx, [128, NFF, 1024], F32)
        if final:
            fin_d = self.din("fing", [128, D], F32)

        sc.barrier()
        A.release(self.base_mark)
        aT, tl_aT = A.alloc("aT", (4, T), BF16, tl=False)
        bT, tl_bT = A.alloc("bT", (2, T), BF16)
        aT_tl = [[sc.tile("aT%d_%d" % (h, qb)) for qb in range(8)] for h in range(H)]
        mix_mark = A.mark()

        self.setup_stage_ring(512)
        wpb, tl_wpb = A.alloc("wpb", (2, 128), BF16)
        psc, tl_psc = A.alloc("psc", (2,), F32)
        hmask, tl_hm = A.alloc("hmask", (2,), F32)
        edge, tl_edge = A.alloc("edge", (2, 16), F32)
        self.load_cast(wpb.rearrange("p a b -> p (a b)"), tl_wpb,
                       wpb_d[:, :, :].rearrange("p a b -> p (a b)"), 256, eng="dve")
        self.load_f32(psc, tl_psc, psc_d[:, :])
        self.load_f32(hmask, tl_hm, hmask_d[:, :])
        self.load_f32(edge, tl_edge, edge_d[:, :, :])
        W = T + 16
        ppx, tl_ppx = A.alloc("ppx", (2, W), F32)
        s1, tl_s1 = A.alloc("pl_s1", (W,), F32)
        s2, tl_s2 = A.alloc("pl_s2", (W,), F32)
        dT, tl_dT = A.alloc("dT", (2, T), BF16)
        etmp, tl_et = A.alloc("etmp", (16,), F32)
        DMA(sc, "sp", ppx[:, :, 8:8 + T], ppT[:, :].rearrange("(c p) t -> p c t", c=2), [],
            [tl_ppx])
        hl, tl_hl = A.alloc("hl", (2, 16), F32)
        DMA(sc, "sp", hl[:, :, 0:8], halo_src(0, 8, 16).rearrange("(c p) t -> p c t", c=2), tl_g,
            [tl_hl])
        DMA(sc, "sp", hl[:, :, 8:16], halo_src(1, 0, 8).rearrange("(c p) t -> p c t", c=2), tl_g,
            [tl_hl])
        TS(sc, "dve", ppx[:, :, 0:8], hl[:, :, 0:8], hmask[:, 0:1], None, ALU.mult, None,
           [tl_hl, tl_hm], [tl_ppx])
        TS(sc, "dve", ppx[:, :, 8 + T:16 + T], hl[:, :, 8:16], hmask[:, 1:2], None, ALU.mult, None,
           [tl_hl, tl_hm], [tl_ppx])
        for c in range(2):
            p_c = ppx[:, c, :]
            levels_needed = (1, 2) if c == 0 else (3, 4)
            TT(sc, "dve", s1[:, 0:W - 1], p_c[:, 0:W - 1], p_c[:, 1:W], ALU.add, [tl_ppx], [tl_s1])
            cur, cur_tl, oth, oth_tl = s1, tl_s1, s2, tl_s2
            lvl = 1
            ln = W - 1
            for gi in range(2):
                g = c * 2 + gi
                want = levels_needed[gi]
                while lvl < want:
                    sh = 1 << lvl
                    TT(sc, "dve", oth[:, 0:ln - sh], cur[:, 0:ln - sh], cur[:, sh:ln], ALU.add,
                       [cur_tl], [oth_tl])
                    ln -= sh
                    cur, cur_tl, oth, oth_tl = oth, oth_tl, cur, cur_tl
                    lvl += 1
                w = 2 << g
                left = w // 2
                pr = slice(gi * 64, gi * 64 + 64)
                o = 8 - left
                STT(sc, "dve", dT[pr, c, :], cur[pr, o:o + T], 1.0 / w, p_c[pr, 8:8 + T],
                    ALU.mult, ALU.subtract, [cur_tl, tl_ppx], [tl_dT])
                for (tc0, ec0) in ((0, 0), (T - 8, 8)):
                    TT(sc, "dve", etmp[pr, 0:8], cur[pr, o + tc0:o + tc0 + 8],
                       edge[pr, c, ec0:ec0 + 8], ALU.mult, [cur_tl, tl_edge], [tl_et])
                    TT(sc, "dve", dT[pr, c, tc0:tc0 + 8], etmp[pr, 0:8],
                       p_c[pr, 8 + tc0:16 + tc0], ALU.subtract, [tl_et, tl_ppx], [tl_dT])
        for c in range(2):
            for tb in range(8):
                bk = 2 + (tb % 2)
                MM(sc, self.psb(bk)[:, 0:512], wpb[:, c, :], dT[:, c, tb * 512:(tb + 1) * 512],
                   True, True, [tl_wpb, tl_dT], [pst[bk]])
                TS(sc, "dve", bT[:, c, tb * 512:(tb + 1) * 512], self.psb(bk)[:, 0:512],
                   psc[:, c:c + 1], None, ALU.mult, None, [pst[bk], tl_psc], [tl_bT])

        sc.barrier()
        A.release(mix_mark)
        self.setup_stage_ring(512)
        wkn, tl_wkn = A.alloc("wkn", (2, 512), BF16)
        wv, tl_wv = A.alloc("wv", (2, 512), BF16)
        for k in range(2):
            self.load_cast(wkn[:, k, :], tl_wkn, wkn_d[:, k, :], 512, eng="dve")
            self.load_cast(wv[:, k, :], tl_wv, wv_d[:, k, :], 512, eng="pool")
        ones_b, tl_ones = A.alloc("ones_b", (128,), BF16)
        sc.add("dve", lambda e: e.memset(ones_b, 1.0), [], [tl_ones])
        rhi, tl_rhi = A.alloc("rhi", (512,), BF16)
        rlo, tl_rlo = A.alloc("rlo", (512,), BF16)
        lt = [A.alloc("lat_r%d" % r, (2, T), BF16) for r in range(2)]
        for r in range(2):
            for c in range(2):
                DMA(sc, "sp", lt[r][0][:, c, :], ckv_src(r, c), tl_g, [lt[r][1]])
        KT = [A.alloc("KT%d" % i, (S,), BF16) for i in range(2)]
        for i in range(2):
            for r in range(2):
                DMA(sc, "sp", KT[i][0][64:96, r * T:(r + 1) * T], kr_src(r), tl_g, [KT[i][1]])
        V = [A.alloc("V%d" % i, (64, 193), BF16) for i in range(2)]
        for i in range(2):
            v_ap, v_tl = V[i]
            sc.add("pool", (lambda a: (lambda e: e.memset(a, 0.0)))(v_ap), [], [v_tl])
            sc.add("pool", (lambda a: (lambda e: e.memset(a, 1.0)))(v_ap[:, :, 64:65]), [], [v_tl])
            sc.add("pool", (lambda a: (lambda e: e.memset(a, 1.0)))(v_ap[:, :, 129:130]), [], [v_tl])
        QTh = [A.alloc("QTh%d" % i, (T,), BF16) for i in range(2)]
        P = [A.alloc("P%d" % i, (1024,), BF16) for i in range(3)]
        rcp, tl_rcp = A.alloc("rcp", (512,), F32)
        osb = [A.alloc("osb%d" % i, (512,), F32) for i in range(2)]
        an = [A.alloc("an%d" % i, (512,), BF16) for i in range(2)]
        B_ACC = (4, 5)
        B_BC = 6
        B_KV = 7

        def kv_items(h):
            items = []
            kt_ap, kt_tl = KT[h % 2]
            pair = h // 2
            v_ap, v_tl = V[pair % 2]
            for r in range(2):
                for tb in range(8):
                    def k_item(r=r, tb=tb):
                        bk = B_KV
                        for k in range(2):
                            MM(sc, self.psb(bk)[0:64, 0:512], wkn[:, k, h * 64:(h + 1) * 64],
                               lt[r][0][:, k, tb * 512:(tb + 1) * 512], k == 0, k == 1,
                               [tl_wkn, lt[r][1]], [pst[bk]])
                        CP(sc, "dve", kt_ap[0:64, r * T + tb * 512:r * T + (tb + 1) * 512],
                           self.psb(bk)[0:64, 0:512], [pst[bk]], [kt_tl])
                    items.append(k_item)
            if h % 2 == 0:
                for kc4 in range(16):
                    def v_item(kc4=kc4):
                        bk = B_KV
                        for q4 in range(4):
                            kc = kc4 * 4 + q4
                            r, tt = kc // 32, (kc % 32) * 128
                            for k in range(2):
                                MM(sc, self.psb(bk)[:, q4 * 128:(q4 + 1) * 128],
                                   lt[r][0][:, k, tt:tt + 128],
                                   wv[:, k, pair * 128:(pair + 1) * 128],
                                   k == 0, k == 1, [lt[r][1], tl_wv], [pst[bk]])
                        src = self.psb(bk)[:, 0:512].rearrange("p (c e d) -> p c e d", c=4, e=2)
                        for e_ in range(2):
                            CP(sc, "dve", v_ap[:, kc4 * 4:(kc4 + 1) * 4, e_ * 65:e_ * 65 + 64],
                               src[:, :, e_, :], [pst[bk]], [v_tl])
                    items.append(v_item)
            return items

        def finish_unit(h, qb, acc_b, unit):
            e = h % 2
            pair = h // 2
            acc = self.psb(acc_b)
            sc.add("dve", (lambda o, a: (lambda en: en.reciprocal(o, a)))(
                rcp[64:65, :], acc[64:65, 0:512]), [pst[acc_b]], [tl_rcp])
            CP(sc, "dve", rhi[64:65, :], rcp[64:65, :], [tl_rcp], [tl_rhi])
            TT(sc, "dve", rlo[64:65, :], rcp[64:65, :], rhi[64:65, :], ALU.subtract,
               [tl_rcp, tl_rhi], [tl_rlo])
            MM(sc, self.psb(B_BC)[0:64, 0:512], ones_b[64:65, 0:64], rhi[64:65, :], True, False,
               [tl_ones, tl_rhi], [pst[B_BC]])
            MM(sc, self.psb(B_BC)[0:64, 0:512], ones_b[64:65, 0:64], rlo[64:65, :], False, True,
               [tl_ones, tl_rlo], [pst[B_BC]])
            o_ap, o_tl = osb[unit % 2]
            n_ap, n_tl = an[unit % 2]
            CP(sc, "dve", o_ap[0:64, :], acc[0:64, 0:512], [pst[acc_b]], [o_tl])
            TT(sc, "dve", n_ap[0:64, :], o_ap[0:64, :], self.psb(B_BC)[0:64, 0:512], ALU.mult,
               [o_tl, pst[B_BC]], [n_tl])
            DMA(sc, "pool", aT[e * 64:(e + 1) * 64, pair, qb * 512:(qb + 1) * 512],
                n_ap[0:64, :], [n_tl], [aT_tl[h][qb]])

        DMA(sc, "sp", QTh[0][0][0:96, :], QT[0:96, :], [], [QTh[0][1]])
        for it in kv_items(0):
            it()
        steps = [(h, qb, kp) for h in range(H) for qb in range(8) for kp in range(32)]
        LA = 1
        nsteps = len(steps)
        bg = []
        for j in range(nsteps + LA):
            if j < nsteps:
                h, qb, kp = steps[j]
                if qb == 0 and kp == 0:
                    if h + 1 < H:
                        DMA(sc, "sp", QTh[(h + 1) % 2][0][0:96, :], QT[(h + 1) * 96:(h + 2) * 96, :],
                            [], [QTh[(h + 1) % 2][1]])
                        bg = kv_items(h + 1)
                    else:
                        bg = []
                kt_ap, kt_tl = KT[h % 2]
                q_ap, q_tl = QTh[h % 2]
                qs = q_ap[0:96, qb * 512:(qb + 1) * 512]
                sd = j % 2
                p_ap, p_tl = P[j % 3]
                for t in range(2):
                    kc = 2 * kp + t
                    bk = 2 * sd + t
                    MM(sc, self.psb(bk)[:, 0:512], kt_ap[0:96, kc * 128:(kc + 1) * 128], qs,
                       True, True, [kt_tl, q_tl], [pst[bk]])
                for t in range(2):
                    ACTF(sc, p_ap[:, t * 512:(t + 1) * 512], self.psb(2 * sd + t)[:, 0:512], AF.Exp,
                         [pst[2 * sd + t]], [p_tl], scale=SCALE)
            if j >= LA:
                jj = j - LA
                h2, qb2, kp2 = steps[jj]
                unit2 = h2 * 8 + qb2
                acc_b = B_ACC[unit2 % 2]
                acc = self.psb(acc_b)
                pair2 = h2 // 2
                v_ap, v_tl = V[pair2 % 2]
                e2 = h2 % 2
                p2_ap, p2_tl = P[jj % 3]
                for t in range(2):
                    kc = 2 * kp2 + t
                    MM(sc, acc[:, 0:512], v_ap[:, kc, e2 * 65:e2 * 65 + 128],
                       p2_ap[:, t * 512:(t + 1) * 512], kc == 0, kc == 63, [v_tl, p2_tl],
                       [pst[acc_b]])
                if kp2 == 31:
                    finish_unit(h2, qb2, acc_b, unit2)
                if bg and (jj % 8 == 7):
                    bg.pop(0)()
            if j < nsteps and steps[j][1] == 7 and steps[j][2] == 31:
                while bg:
                    bg.pop(0)()

        sc.barrier()
        A.release(mix_mark)
        self.setup_stage_ring(1024)
        cT, tl_cT = A.alloc("cT", (2, T), BF16)
        DMA(sc, "sp", cT, cTd[:, :].rearrange("(c p) t -> p c t", c=2), [], [tl_cT])
        w_o, tl_wo = A.alloc("w_o", (8, 1024), BF16)
        for k in range(8):
            self.load_cast(w_o[:, k, :], tl_wo, w_o_d[:, k, :], 1024)
        xt = [A.alloc("xo%d" % i, (D,), F32) for i in range(2)]
        xo = [A.alloc("xr%d" % i, (D,), F32) for i in range(2)]
        for i in range(NT):
            x_ap, x_tl = xt[i % 2]
            r_ap, r_tl = xo[i % 2]
            DMA(sc, "sp", x_ap, xin[i * 128:(i + 1) * 128, :], [], [x_tl])
            b0 = 2 * (i % 2)
            tsl = slice(i * 128, (i + 1) * 128)
            qb = i // 4
            for nh in range(2):
                bk = b0 + nh
                for k in range(8):
                    if k < 4:
                        lhs = aT[:, k, tsl]
                        rt = [aT_tl[2 * k][qb], aT_tl[2 * k + 1][qb]]
                    elif k < 6:
                        lhs = bT[:, k - 4, tsl]
                        rt = [tl_bT]
                    else:
                        lhs = cT[:, k - 6, tsl]
                        rt = [tl_cT]
                    MM(sc, self.psb(bk)[:, 0:512], lhs, w_o[:, k, nh * 512:(nh + 1) * 512],
                       k == 0, k == 7, rt + [tl_wo], [pst[bk]])
                TT(sc, "dve", r_ap[:, nh * 512:(nh + 1) * 512], x_ap[:, nh * 512:(nh + 1) * 512],
                   self.psb(bk)[:, 0:512], ALU.add, [x_tl, pst[bk]], [r_tl])
            DMA(sc, "pool", xmid[tsl, :], r_ap, [r_tl], [])

        sc.barrier()
        A.release(self.base_mark)
        self.setup_stage_ring()
        ffng, tl_fg = A.alloc("ffng", (8,), F32)
        self.load_f32(ffng, tl_fg, ffng_d[:, :])
        wg, tl_wg = A.alloc("wg", (8, DFF), BF16)
        wu, tl_wu = A.alloc("wu", (8, DFF), BF16)
        wd, tl_wd = A.alloc("wd", (NFF, 1024), BF16)
        ffn_eng = ("dve", "act", "dve", "act", "pool")
        fe = [0]

        def next_eng():
            e_ = ffn_eng[fe[0] % len(ffn_eng)]
            fe[0] += 1
            return e_

        tl_wg = [[sc.tile("wg%d_%d" % (k, hf)) for hf in range(2)] for k in range(8)]
        tl_wu = [[sc.tile("wu%d_%d" % (k, hf)) for hf in range(2)] for k in range(8)]
        tl_wd = [sc.tile("wd%d" % j) for j in range(NFF)]
        for hf in range(2):
            cs = slice(hf * 1408, (hf + 1) * 1408)
            for k in range(8):
                self.load_cast(wg[:, k, cs], tl_wg[k][hf], wg_d[:, k, cs], 1408,
                               scale=ffng[:, k:k + 1], scale_tl=tl_fg, eng=next_eng())
                self.load_cast(wu[:, k, cs], tl_wu[k][hf], wu_d[:, k, cs], 1408,
                               scale=ffng[:, k:k + 1], scale_tl=tl_fg, eng=next_eng())
        for j in range(NFF):
            self.load_cast(wd[:, j, :], tl_wd[j], wd_d[:, j, :], 1024, eng=next_eng())
        if final:
            fing, tl_fing = A.alloc("fing", (D,), F32)
            self.load_f32(fing, tl_fing, fin_d[:, :])
        NB = 256
        NSUB = NB // 128
        xb = [[A.alloc("xb%d_%d" % (i, s), (D,), F32) for s in range(NSUB)] for i in range(2)]
        hn2, tl_hn2 = A.alloc("hn2", (D,), BF16)
        junk, tl_junk = A.alloc("junk2", (D,), BF16)
        st, tl_st = A.alloc("st2", (8,), F32)
        h2T = [A.alloc("h2T%d" % i, (8, NB), BF16) for i in range(2)]
        actT, tl_act = A.alloc("actT", (NFF, NB), BF16)
        sg = [A.alloc("silu%d" % i, (NB,), F32) for i in range(2)]
        B_TP = 0
        B_G = (1, 2)
        B_U = (3, 4)
        B_Y = (5, 6)
        gi = 0
        yi = 0
        for blk in range(T // NB):
            bp = blk % 2
            h2_ap, h2_tl = h2T[bp]
            for s in range(NSUB):
                x_ap, x_tl = xb[bp][s]
                tok0 = blk * NB + s * 128
                DMA(sc, "sp", x_ap, xmid[tok0:tok0 + 128, :], [], [x_tl])
                ACTF(sc, junk, x_ap, AF.Square, [x_tl], [tl_junk, tl_st], accum=st[:, 0:1])
                RSTD(sc, st[:, 1:2], st[:, 0:1], D, self.epsb[:, 0:1], [tl_st, self.tl_eps], [tl_st])
                TS(sc, "dve", hn2, x_ap, st[:, 1:2], None, ALU.mult, None, [x_tl, tl_st],
                   [tl_hn2])
                tp = self.psb(B_TP, BF16)
                for k in range(8):
                    TR(sc, tp[:, k * 128:(k + 1) * 128], hn2[:, k * 128:(k + 1) * 128], ident,
                       [tl_hn2, tl_id], [pst[B_TP]])
                CP(sc, "dve", h2_ap[:, :, s * 128:(s + 1) * 128],
                   tp.rearrange("p (k t) -> p k t", k=8), [pst[B_TP]], [h2_tl])
            for j in range(NFF):
                gb = B_G[gi % 2]
                ub = B_U[gi % 2]
                s_ap, s_tl = sg[gi % 2]
                gi += 1
                for k in range(8):
                    MM(sc, self.psb(gb)[:, 0:NB], wg[:, k, j * 128:(j + 1) * 128], h2_ap[:, k, :],
                       k == 0, k == 7, [tl_wg[k][j // 11], h2_tl], [pst[gb]])
                for k in range(8):
                    MM(sc, self.psb(ub)[:, 0:NB], wu[:, k, j * 128:(j + 1) * 128], h2_ap[:, k, :],
                       k == 0, k == 7, [tl_wu[k][j // 11], h2_tl], [pst[ub]])
                ACTF(sc, s_ap, self.psb(gb)[:, 0:NB], AF.Silu, [pst[gb]], [s_tl])
                TT(sc, "dve", actT[:, j, :], s_ap, self.psb(ub)[:, 0:NB], ALU.mult,
                   [s_tl, pst[ub]], [tl_act])
            for s in range(NSUB):
                x_ap, x_tl = xb[bp][s]
                y_ap, y_tl = x_ap, x_tl
                tok0 = blk * NB + s * 128
                for nh in range(2):
                    bk = B_Y[nh]
                    for j in range(NFF):
                        MM(sc, self.psb(bk)[:, 0:512], actT[:, j, s * 128:(s + 1) * 128],
                           wd[:, j, nh * 512:(nh + 1) * 512], j == 0, j == NFF - 1,
                           [tl_act, tl_wd[j]], [pst[bk]])
                    TT(sc, "dve", y_ap[:, nh * 512:(nh + 1) * 512],
                       x_ap[:, nh * 512:(nh + 1) * 512], self.psb(bk)[:, 0:512], ALU.add,
                       [x_tl, pst[bk]], [y_tl])
                if final:
                    ACTF(sc, junk, y_ap, AF.Square, [y_tl], [tl_junk, tl_st], accum=st[:, 2:3])
                    RSTD(sc, st[:, 3:4], st[:, 2:3], D, self.epsb[:, 0:1], [tl_st, self.tl_eps],
                         [tl_st])
                    STT(sc, "dve", y_ap, y_ap, st[:, 3:4], fing, ALU.mult, ALU.mult,
                        [y_tl, tl_st, tl_fing], [y_tl])
                DMA(sc, "pool", xout[tok0:tok0 + 128, :], y_ap, [y_tl], [])


def _pk(w, k):
    n = w.shape[1]
    return np.ascontiguousarray(w.reshape(k, 128, n).transpose(1, 0, 2))


def _prep_a(l, P):
    o1, o2, o3, o4 = 384, 640, 672, 928
    w_in = P["w_in"][l]
    perm = np.concatenate([np.arange(0, o1), np.arange(o2, o3), np.arange(o1, o2),
                           np.arange(o4, 1440), np.arange(o3, o4)])
    sfx = "_a%d" % l
    d = {}
    d["w_in" + sfx] = _pk(w_in[:, perm], 8)
    d["mixg" + sfx] = np.ascontiguousarray(P["mix_norm"][l].reshape(8, 128).T)
    d["w_uq" + sfx] = _pk(P["w_uq"][l], 3)
    d["qg" + sfx] = np.ascontiguousarray(P["q_norm"][l].reshape(3, 128).T)
    d["kvg" + sfx] = np.ascontiguousarray(np.broadcast_to(P["kv_norm"][l][None, :], (128, 256)))
    d["sgg" + sfx] = np.ascontiguousarray(np.broadcast_to(P["sg_norm"][l][None, :], (128, 256)))
    d["wsT" + sfx] = np.ascontiguousarray(P["w_s"][l].transpose(2, 0, 1))
    d["bsT" + sfx] = np.ascontiguousarray(P["b_s"][l].T)
    inv = (10000.0 ** (-np.arange(0, 32, 2, dtype=np.float32) / 32)).astype(np.float32)
    d["invf" + sfx] = np.ascontiguousarray(np.broadcast_to(inv[None, :], (128, 16)))
    return d


def _prep_p(l, P, final):
    sfx = "_p%d" % l
    d = {}
    wkv = P["w_ukv"][l].reshape(256, 8, 128)
    d["wkn" + sfx] = _pk(np.ascontiguousarray(wkv[:, :, :64]).reshape(256, 512), 2)
    d["wv" + sfx] = _pk(np.ascontiguousarray(wkv[:, :, 64:]).reshape(256, 512), 2)
    wpb = np.zeros((128, 2, 128), np.float32)
    for g in range(4):
        c, o = g // 2, (g % 2) * 64
        wpb[o:o + 64, c, o:o + 64] = P["w_pool"][l][g]
    d["wpb" + sfx] = wpb
    d["psc" + sfx] = np.ascontiguousarray(P["pool_scale"][l].reshape(2, 128).T)
    d["w_o" + sfx] = _pk(P["w_o"][l], 8)
    d["ffng" + sfx] = np.ascontiguousarray(P["ffn_norm"][l].reshape(8, 128).T)
    d["w_gate" + sfx] = _pk(P["w_gate"][l], 8)
    d["w_up" + sfx] = _pk(P["w_up"][l], 8)
    d["w_down" + sfx] = _pk(P["w_down"][l], NFF)
    if final:
        d["fing"] = np.ascontiguousarray(np.broadcast_to(P["final_norm"][None, :], (128, D)))
    return d


def _core_consts(c, sfx):
    half = c % 2
    hmask = np.zeros((128, 2), np.float32)
    hmask[:, 0] = 1.0 if half == 1 else 0.0
    hmask[:, 1] = 1.0 if half == 0 else 0.0
    edge = np.zeros((128, 2, 16), np.float32)
    for g in range(4):
        w = 2 << g
        left = w // 2
        right = w - 1 - left
        cch, o = g // 2, (g % 2) * 64
        for e in range(16):
            t = (e if e < 8 else T - 16 + e) + half * T
            lo = max(t - left, 0)
            hi = min(t + right + 1, S)
            edge[o:o + 64, cch, e] = 1.0 / float(hi - lo)
    return {"hmask" + sfx: hmask, "edge" + sfx: edge}


_IDENT = np.eye(128, dtype=np.float32)
_PROGS = {}


def _get_prog(key, post_layer, a_layer, final):
    if key not in _PROGS:
        p = Prog(post_layer, a_layer, final)
        p.build()
        _PROGS[key] = p
    return _PROGS[key]


def _pair_cat(res, name, c):
    b = c // 2
    return np.concatenate([res[2 * b][name], res[2 * b + 1][name]], axis=0)


def _kernel_fused(P, xs, poss, cores):
    if "F" not in _PROGS:
        p = Prog(None, None, True, fused=True)
        p.build()
        _PROGS["F"] = p
    p = _PROGS["F"]
    shared = {}
    shared.update(_prep_a(0, P))
    shared.update(_prep_p(0, P, False))
    shared.update(_prep_a(1, P))
    shared.update(_prep_p(1, P, True))
    maps = []
    for c in cores:
        m = {"ident": _IDENT, "x_in_a0": xs[c], "pos_a0": poss[c], "pos_a1": poss[c]}
        m.update(_core_consts(c, "_p0"))
        m.update(_core_consts(c, "_p1"))
        m.update(shared)
        maps.append(m)
    r = run_bass_kernel_spmd(p.nc, maps, core_ids=cores).results
    out = np.empty((4, S, D), np.float32)
    for c in cores:
        out[c // 2, (c % 2) * T:(c % 2 + 1) * T, :] = r[c]["y_out"]
    return out


def kernel(**inputs):
    P = {k: np.asarray(v) for k, v in inputs.items()}
    x = P["x"]
    pos = P["positions"]
    cores = list(range(NCORES))
    xs = [np.ascontiguousarray(x[c // 2, (c % 2) * T:(c % 2 + 1) * T, :]) for c in cores]
    poss = [np.ascontiguousarray(pos[c // 2, (c % 2) * T:(c % 2 + 1) * T].reshape(NT, 128).T)
            for c in cores]
    if FUSED:
        return _kernel_fused(P, xs, poss, cores)

    p1 = _get_prog("L1", None, 0, False)
    wa0 = _prep_a(0, P)
    maps = []
    for c in cores:
        m = {"ident": _IDENT, "x_in_a0": xs[c], "pos_a0": poss[c]}
        m.update(wa0)
        maps.append(m)
    r1 = run_bass_kernel_spmd(p1.nc, maps, core_ids=cores).results

    p2 = _get_prog("L2", 0, 1, False)
    wp0 = _prep_p(0, P, False)
    wa1 = _prep_a(1, P)
    maps = []
    for c in cores:
        m = {"ident": _IDENT, "x_in_p0": xs[c], "pos_a1": poss[c],
             "latp_p0": _pair_cat(r1, "latT_a0", c), "halop_p0": _pair_cat(r1, "halo_a0", c),
             "QT_p0": r1[c]["QT_a0"], "ppT_p0": r1[c]["ppT_a0"], "cT_p0": r1[c]["cT_a0"]}
        m.update(_core_consts(c, "_p0"))
        m.update(wp0)
        m.update(wa1)
        maps.append(m)
    r2 = run_bass_kernel_spmd(p2.nc, maps, core_ids=cores).results

    p3 = _get_prog("L3", 1, None, True)
    wp1 = _prep_p(1, P, True)
    maps = []
    for c in cores:
        m = {"ident": _IDENT, "x_in_p1": r2[c]["xout_p0"],
             "latp_p1": _pair_cat(r2, "latT_a1", c), "halop_p1": _pair_cat(r2, "halo_a1", c),
             "QT_p1": r2[c]["QT_a1"], "ppT_p1": r2[c]["ppT_a1"], "cT_p1": r2[c]["cT_a1"]}
        m.update(_core_consts(c, "_p1"))
        m.update(wp1)
        maps.append(m)
    r3 = run_bass_kernel_spmd(p3.nc, maps, core_ids=cores).results

    out = np.empty((4, S, D), np.float32)
    for c in cores:
        out[c // 2, (c % 2) * T:(c % 2 + 1) * T, :] = r3[c]["y_out"]
    return out
```

```python
import numpy as np
import ml_dtypes
from contextlib import ExitStack
import concourse.bass as bass
import concourse.mybir as mybir
from concourse.bass_utils import run_bass_kernel_spmd

F32 = mybir.dt.float32
BF16 = mybir.dt.bfloat16
I32 = mybir.dt.int32
AF = mybir.ActivationFunctionType
ALU = mybir.AluOpType
AX = mybir.AxisListType

NCORES = 8
D = 1024
T = 4096
NT = 32
S = 8192
H = 8
DFF = 2816
NFF = 22
EPS = 1e-6
SCALE = 96.0 ** -0.5
PI = float(np.pi)
FUSED = True


class Tl:
    __slots__ = ("name", "w", "r")

    def __init__(self, name=""):
        self.name = name
        self.w = []
        self.r = []


class Sched:
    STREAMS = ("pe", "act", "dve", "pool", "sp")
    KRING = 8

    def __init__(self, same_sync=True):
        self.ops = {s: [] for s in self.STREAMS}
        self.ndma = {s: 0 for s in self.STREAMS}
        self.known_e = {s: {} for s in self.STREAMS}
        self.known_d = {s: set() for s in self.STREAMS}
        self.same_sync = same_sync
        self.tiles = []
        self.ncc = 0

    def tile(self, name=""):
        t = Tl(name)
        self.tiles.append(t)
        return t

    def _filter(self, stream, deps):
        ke = self.known_e[stream]
        kd = self.known_d[stream]
        best = {}
        res = []
        for tok in deps:
            if tok[0] == "e":
                _, P, i = tok
                if P == stream and (stream == "pe" or not self.same_sync):
                    continue
                if ke.get(P, -1) >= i:
                    continue
                if best.get(P, -1) < i:
                    best[P] = i
            else:
                if tok in kd:
                    continue
                kd.add(tok)
                res.append(tok)
        for P, i in best.items():
            res.append(("e", P, i))
            ke[P] = i
            self.ops[P][i]["inc"] = True
        return res

    def add(self, stream, fn, reads=(), writes=(), dma=False, cc=False):
        deps = set()
        for t in reads:
            deps.update(t.w)
        for t in writes:
            deps.update(t.w)
            deps.update(t.r)
        ops = self.ops[stream]
        idx = len(ops)
        if dma:
            j = self.ndma[stream]
            self.ndma[stream] += 1
            tok = ("d", stream, j)
            if j >= self.KRING:
                deps.add(("d", stream, j - self.KRING))
        elif cc:
            tok = ("c", "cc", self.ncc)
            if self.ncc > 0:
                deps.add(("c", "cc", self.ncc - 1))
            self.ncc += 1
        else:
            tok = ("e", stream, idx)
        waits = self._filter(stream, deps)
        ops.append(dict(fn=fn, waits=waits, dma=dma, cc=cc, tok=tok, inc=False))
        for t in reads:
            if tok[0] == "e":
                t.r = [x for x in t.r if not (x[0] == "e" and x[1] == tok[1])]
            t.r.append(tok)
        for t in writes:
            t.w = [tok]
            t.r = []
        return tok

    def barrier(self, skip_cc=False):
        toks = []
        for s in self.STREAMS:
            ops = self.ops[s]
            for i in range(len(ops) - 1, -1, -1):
                if (not ops[i]["dma"]) and (not ops[i].get("cc")) and ops[i]["fn"] is not None:
                    toks.append(("e", s, i))
                    break
            n = self.ndma[s]
            for j in range(max(0, n - self.KRING), n):
                toks.append(("d", s, j))
        if self.ncc > 0 and not skip_cc:
            toks.append(("c", "cc", self.ncc - 1))
        for s in self.STREAMS:
            waits = self._filter(s, set(toks))
            self.ops[s].append(dict(fn=None, waits=waits, dma=False, tok=None, inc=False))
        for t in self.tiles:
            if skip_cc and any(x[0] == "c" for x in t.w):
                continue
            t.w = []
            t.r = []

    def emit(self, nc, es):
        K = self.KRING
        esem = {s: es.enter_context(nc.semaphore("e_" + s)) for s in self.STREAMS}
        csem = es.enter_context(nc.semaphore("ccsem"))
        dsem = {s: [es.enter_context(nc.semaphore("d_%s_%d" % (s, k))) for k in range(K)]
                for s in ("sp", "pool", "act")}
        for s in self.STREAMS:
            c = 0
            for op in self.ops[s]:
                if op["inc"]:
                    c += 1
                op["cnt"] = c
        ops_all = self.ops

        def run(stream, eng):
            for op in ops_all[stream]:
                for tok in op["waits"]:
                    if tok[0] == "e":
                        eng.wait_ge(esem[tok[1]], ops_all[tok[1]][tok[2]]["cnt"])
                    elif tok[0] == "c":
                        eng.wait_ge(csem, tok[2] + 1)
                    else:
                        eng.wait_ge(dsem[tok[1]][tok[2] % K], 16 * (tok[2] // K + 1))
                if op["fn"] is None:
                    continue
                ins = op["fn"](eng)
                if op["dma"]:
                    j = op["tok"][2]
                    ins.then_inc(dsem[stream][j % K], 16)
                elif op.get("cc"):
                    ins.then_inc(csem, 1)
                elif op["inc"]:
                    ins.then_inc(esem[stream], 1)

        with nc.Block() as block:
            @block.tensor
            def _(e):
                run("pe", e)

            @block.scalar
            def _(e):
                run("act", e)

            @block.vector
            def _(e):
                run("dve", e)

            @block.gpsimd
            def _(e):
                run("pool", e)

            @block.sync
            def _(e):
                run("sp", e)


class Arena:
    def __init__(self, nc, sc, nbytes):
        self.sc = sc
        self.nb = nbytes
        self.t = nc.alloc_sbuf_tensor("arena", [128, nbytes // 2], BF16)
        self.off = 0

    def mark(self):
        return self.off

    def release(self, m):
        self.off = m

    def alloc(self, name, free, dtype, tl=True):
        esz = 4 if dtype in (F32, I32) else 2
        n = 1
        for f in free:
            n *= f
        nby = (n * esz + 63) // 64 * 64
        assert self.off + nby <= self.nb, ("SBUF arena overflow", name, self.off, nby)
        v = self.t[:, self.off // 2:(self.off + n * esz) // 2]
        if dtype != BF16:
            v = v.bitcast(dtype)
        if len(free) == 2:
            v = v.rearrange("p (a b) -> p a b", b=free[1])
        elif len(free) == 3:
            v = v.rearrange("p (a b c) -> p a b c", b=free[1], c=free[2])
        self.off += nby
        return v, (self.sc.tile(name) if tl else None)


def MM(sc, out, lhsT, rhs, start, stop, R, W):
    sc.add("pe", lambda e: e.matmul(out, lhsT, rhs, start=start, stop=stop), R, W)


def TR(sc, out, in_, ident, R, W):
    sc.add("pe", lambda e: e.transpose(out, in_, ident), R, W)


def ACTF(sc, out, in_, func, R, W, bias=0.0, scale=1.0, accum=None):
    if accum is None:
        sc.add("act", lambda e: e.activation(out, in_, func, bias=bias, scale=scale), R, W)
    else:
        sc.add("act", lambda e: e.activation(out, in_, func, bias=bias, scale=scale,
                                             accum_out=accum), R, W)


def TS(sc, eng, out, in0, s1, s2, op0, op1, R, W):
    if op1 is None:
        sc.add(eng, lambda e: e.tensor_scalar(out, in0, s1, None, op0), R, W)
    else:
        sc.add(eng, lambda e: e.tensor_scalar(out, in0, s1, s2, op0, op1), R, W)


def TT(sc, eng, out, in0, in1, op, R, W):
    sc.add(eng, lambda e: e.tensor_tensor(out, in0, in1, op), R, W)


def STT(sc, eng, out, in0, scalar, in1, op0, op1, R, W):
    sc.add(eng, lambda e: e.scalar_tensor_tensor(out, in0, scalar, in1, op0, op1), R, W)


def CP(sc, eng, out, in_, R, W):
    if eng == "act":
        sc.add("act", lambda e: e.copy(out, in_), R, W)
    else:
        sc.add(eng, lambda e: e.tensor_copy(out, in_), R, W)


def RSTD(sc, out, ssq, n, eps_ap, R, W):
    sc.add("act", lambda e: e.activation(out, ssq, AF.Sqrt, bias=eps_ap, scale=1.0 / n), R, W)
    sc.add("dve", lambda e: e.reciprocal(out, out), W, W)


def DMA(sc, stream, out, in_, R, W):
    sc.add(stream, lambda e: e.dma_start(out=out, in_=in_), R, W, dma=True)


class Prog:
    def __init__(self, post_layer, a_layer, final, fused=False):
        self.post_layer = post_layer
        self.a_layer = a_layer
        self.final = final
        self.fused = fused
        self.nc = bass.Bass("TRN2", target_bir_lowering=False)
        self.sc = Sched()
        self.ext_in = []
        self.ext_out = []
        self.dr = {}

    def din(self, name, shape, dtype):
        h = self.nc.dram_tensor(name, list(shape), dtype, kind="ExternalInput")
        self.ext_in.append(name)
        self.dr[name] = h
        return h

    def dout(self, name, shape, dtype):
        h = self.nc.dram_tensor(name, list(shape), dtype, kind="ExternalOutput")
        self.ext_out.append(name)
        self.dr[name] = h
        return h

    def dint(self, name, shape, dtype):
        key = name.rstrip("0123456789") if self.fused else name
        if not hasattr(self, "_shared"):
            self._shared = {}
        if key not in self._shared:
            self._shared[key] = self.nc.dram_tensor(key, list(shape), dtype)
        h = self._shared[key]
        self.dr[name] = h
        return h

    def build(self):
        nc, sc = self.nc, self.sc
        self.A = Arena(nc, sc, 210500)
        A = self.A
        self.ps2 = []
        self.pst = []
        for b in range(4):
            self.ps2.append(nc.alloc_psum_tensor("psd%d" % b, [128, 1024], F32))
        for b in range(8):
            self.pst.append(sc.tile("ps%d" % b))
        ident_d = self.din("ident", [128, 128], F32)
        self.identf, tl_if = A.alloc("identf", (128,), F32)
        self.ident, self.tl_id = A.alloc("ident", (128,), BF16)
        DMA(sc, "sp", self.identf, ident_d[:, :], [], [tl_if])
        CP(sc, "dve", self.ident, self.identf, [tl_if], [self.tl_id])
        self.epsb, self.tl_eps = A.alloc("epsb", (1,), F32)
        epsb = self.epsb
        sc.add("dve", lambda e: e.memset(epsb, EPS), [], [self.tl_eps])
        self.base_mark = A.mark()

        import os
        self.onel = bool(os.environ.get("ONEL"))
        if self.fused and self.onel:
            self.stage_a(0)
            self.exchange(0)
            self.stage_post(0)
        elif self.fused:
            self.stage_a(0)
            self.exchange(0)
            self.stage_post(0)
            self.stage_a(1)
            self.exchange(1)
            self.stage_post(1)
        else:
            if self.post_layer is not None:
                self.stage_post(self.post_layer)
            if self.a_layer is not None:
                self.stage_a(self.a_layer)
        sc.barrier()
        with ExitStack() as es:
            sc.emit(nc, es)
        return nc

    def psb(self, b, dtype=F32):
        v = self.ps2[b // 2][:, (b % 2) * 512:(b % 2 + 1) * 512]
        if dtype == BF16:
            v = v.bitcast(BF16)
        return v

    def psd(self, k):
        return self.ps2[k][:, :]

    def setup_stage_ring(self, ncols=1440):
        A = self.A
        self.wst = [A.alloc("wst%d" % i, (ncols,), F32) for i in range(3)]
        self.wst_i = 0

    def load_cast(self, dst, dst_tl, src, ncols, scale=None, eng=None, scale_tl=None):
        sc = self.sc
        sR = [scale_tl] if scale_tl is not None else []
        stg, stl = self.wst[self.wst_i % 3]
        if eng is None:
            eng = ("pool", "dve", "act")[self.wst_i % 3]
        self.wst_i += 1
        sv = stg[:, 0:ncols]
        DMA(sc, "sp", sv, src, [], [stl])
        if scale is None:
            CP(sc, eng, dst, sv, [stl], [dst_tl])
        elif eng == "act":
            ACTF(sc, dst, sv, AF.Copy, [stl] + sR, [dst_tl], scale=scale)
        else:
            TS(sc, eng, dst, sv, scale, None, ALU.mult, None, [stl] + sR, [dst_tl])

    def load_f32(self, dst, dst_tl, src):
        DMA(self.sc, "sp", dst, src, [], [dst_tl])

    def stage_a(self, l):
        nc, sc, A = self.nc, self.sc, self.A
        sc.barrier()
        A.release(self.base_mark)
        sfx = "_a%d" % l
        if self.fused:
            xin = self.din("x_in" + sfx, [T, D], F32) if l == 0 else self.dr["xout_p0"]
        elif self.post_layer is not None:
            xin = self.dr["xout_p%d" % self.post_layer]
        else:
            xin = self.din("x_in" + sfx, [T, D], F32)
        mk_out = self.dint if self.fused else self.dout
        latT = mk_out("latT" + sfx, [288, T], BF16)
        halo = mk_out("halo" + sfx, [256, 16], F32)
        QT = mk_out("QT" + sfx, [H * 96, T], BF16)
        ppT = mk_out("ppT" + sfx, [256, T], F32)
        cT = mk_out("cT" + sfx, [256, T], BF16)
        pos_d = self.din("pos" + sfx, [128, NT], I32)
        invf_d = self.din("invf" + sfx, [128, 16], F32)
        w_in_d = self.din("w_in" + sfx, [128, 8, 1440], F32)
        mixg_d = self.din("mixg" + sfx, [128, 8], F32)
        w_uq_d = self.din("w_uq" + sfx, [128, 3, 768], F32)
        qg_d = self.din("qg" + sfx, [128, 3], F32)
        kvg_d = self.din("kvg" + sfx, [128, 256], F32)
        sgg_d = self.din("sgg" + sfx, [128, 256], F32)
        wsT_d = self.din("wsT" + sfx, [128, 4, 128], F32)
        bsT_d = self.din("bsT" + sfx, [128, 4], F32)

        self.setup_stage_ring()
        w_in, tl_win = A.alloc("w_in", (8, 1440), BF16)
        w_uq, tl_wuq = A.alloc("w_uq", (3, 768), BF16)
        wsT, tl_wsT = A.alloc("wsT", (4, 128), BF16)
        mixg, tl_mixg = A.alloc("mixg", (8,), F32)
        qg, tl_qg = A.alloc("qg", (3,), F32)
        kvg, tl_kvg = A.alloc("kvg", (256,), F32)
        sgg, tl_sgg = A.alloc("sgg", (256,), F32)
        bsT, tl_bsT = A.alloc("bsT", (4,), F32)
        self.load_f32(mixg, tl_mixg, mixg_d[:, :])
        self.load_f32(qg, tl_qg, qg_d[:, :])
        self.load_f32(kvg, tl_kvg, kvg_d[:, :])
        self.load_f32(sgg, tl_sgg, sgg_d[:, :])
        self.load_f32(bsT, tl_bsT, bsT_d[:, :])
        for k in range(8):
            self.load_cast(w_in[:, k, :], tl_win, w_in_d[:, k, :], 1440, scale=mixg[:, k:k + 1],
                           scale_tl=tl_mixg)
        for k in range(3):
            self.load_cast(w_uq[:, k, :], tl_wuq, w_uq_d[:, k, :], 768, scale=qg[:, k:k + 1],
                           scale_tl=tl_qg)
        self.load_cast(wsT.rearrange("p a b -> p (a b)"), tl_wsT,
                       wsT_d[:, :, :].rearrange("p a b -> p (a b)"), 512)

        posi, tl_posi = A.alloc("posi", (NT,), I32)
        posf, tl_posf = A.alloc("posf", (NT,), F32)
        invf, tl_invf = A.alloc("invf", (16,), F32)
        ang, tl_ang = A.alloc("ang", (NT, 16), F32)
        ang2, tl_ang2 = A.alloc("ang2", (NT, 16), F32)
        cost, tl_cos = A.alloc("cost", (NT, 16), F32)
        sint, tl_sin = A.alloc("sint", (NT, 16), F32)
        angi, tl_angi = A.alloc("angi", (NT, 16), I32)
        DMA(sc, "sp", posi, pos_d[:, :], [], [tl_posi])
        DMA(sc, "sp", invf, invf_d[:, :], [], [tl_invf])
        CP(sc, "dve", posf, posi, [tl_posi], [tl_posf])
        TT(sc, "dve", ang, posf.unsqueeze(2).to_broadcast([128, NT, 16]),
           invf.unsqueeze(1).to_broadcast([128, NT, 16]), ALU.mult, [tl_posf, tl_invf], [tl_ang])
        for (dst, dtl, shift) in ((sint, tl_sin, 0.0), (cost, tl_cos, 0.5 * PI)):
            TS(sc, "dve", ang2, ang, shift, 1.0 / (2 * PI), ALU.add, ALU.mult, [tl_ang], [tl_ang2])
            CP(sc, "dve", angi, ang2, [tl_ang2], [tl_angi])
            CP(sc, "dve", ang2, angi, [tl_angi], [tl_ang2])
            STT(sc, "dve", ang2, ang2, -2 * PI, ang, ALU.mult, ALU.add, [tl_ang2, tl_ang], [tl_ang2])
            if shift != 0.0:
                TS(sc, "dve", ang2, ang2, shift, None, ALU.add, None, [tl_ang2], [tl_ang2])
            TS(sc, "dve", ang2, ang2, PI, -PI, ALU.min, ALU.max, [tl_ang2], [tl_ang2])
            ACTF(sc, dst, ang2, AF.Sin, [tl_ang2], [dtl])
        xt = [A.alloc("xt%d" % i, (D,), F32) for i in range(2)]
        junk, tl_junk = A.alloc("junk", (D,), BF16)
        hn = [A.alloc("hn%d" % i, (D,), BF16) for i in range(2)]
        hT = [A.alloc("hT%d" % i, (8, 128), BF16) for i in range(2)]
        st = [A.alloc("st%d" % i, (16,), F32) for i in range(2)]
        cq = [A.alloc("cq%d" % i, (384,), BF16) for i in range(2)]
        cqT = [A.alloc("cqT%d" % i, (3, 128), BF16) for i in range(2)]
        qsb = [A.alloc("qsb%d" % i, (8, 96), BF16) for i in range(2)]
        rm1 = [A.alloc("rm1%d" % i, (8, 2, 16), F32) for i in range(2)]
        rm2 = [A.alloc("rm2%d" % i, (8, 2, 16), F32) for i in range(2)]
        km1 = [A.alloc("km1%d" % i, (2, 16), F32) for i in range(2)]
        km2 = [A.alloc("km2%d" % i, (2, 16), F32) for i in range(2)]
        lat = [A.alloc("lat%d" % i, (288,), BF16) for i in range(2)]
        latf = [A.alloc("latf%d" % i, (256,), F32) for i in range(2)]
        z = [A.alloc("z%d" % i, (512,), F32) for i in range(2)]
        vsq = [A.alloc("vsq%d" % i, (256,), F32) for i in range(2)]
        vtmp = [A.alloc("vtmp%d" % i, (256,), F32) for i in range(2)]
        vn = [A.alloc("vn%d" % i, (256,), BF16) for i in range(2)]
        cc = [A.alloc("cc%d" % i, (256,), BF16) for i in range(2)]
        QTst = [A.alloc("QTst%d" % i, (8, 512), BF16) for i in range(2)]
        latTst = [A.alloc("latTst%d" % i, (3, 512), BF16) for i in range(2)]
        ppTst = [A.alloc("ppTst%d" % i, (2, 512), F32) for i in range(2)]
        cTst = [A.alloc("cTst%d" % i, (2, 512), BF16) for i in range(2)]

        ident, tl_id = self.ident, self.tl_id
        B_TPA, B_TPB, B_PA, B_PB, B_PC, B_PP, B_Q0, B_Q1 = range(8)
        pst = self.pst
        qps = self.psb(B_Q0)
        qps2 = self.psb(B_Q1)

        for i in range(NT):
            par = i % 2
            g4, j4 = i // 4, i % 4
            gp = g4 % 2
            x_ap, x_tl = xt[par]
            hn_ap, hn_tl = hn[par]
            hT_ap, hT_tl = hT[par]
            st_ap, st_tl = st[par]
            DMA(sc, "sp", x_ap, xin[i * 128:(i + 1) * 128, :], [], [x_tl])
            ACTF(sc, junk, x_ap, AF.Square, [x_tl], [tl_junk, st_tl], accum=st_ap[:, 0:1])
            RSTD(sc, st_ap[:, 1:2], st_ap[:, 0:1], D, self.epsb[:, 0:1], [st_tl, self.tl_eps],
                 [st_tl])
            TS(sc, "dve", hn_ap, x_ap, st_ap[:, 1:2], None, ALU.mult, None,
               [x_tl, st_tl], [hn_tl])
            tpa = self.psb(B_TPA, BF16)
            for k in range(8):
                TR(sc, tpa[:, k * 128:(k + 1) * 128], hn_ap[:, k * 128:(k + 1) * 128], ident,
                   [hn_tl, tl_id], [pst[B_TPA]])
            CP(sc, "act", hT_ap.rearrange("p a b -> p (a b)"), tpa, [pst[B_TPA]], [hT_tl])
            pA = self.psb(B_PA)
            pB = self.psb(B_PB)
            pC = self.psb(B_PC)
            pP = self.psb(B_PP)
            for k in range(8):
                MM(sc, pA[:, 0:416], hT_ap[:, k, :], w_in[:, k, 0:416], k == 0, k == 7,
                   [hT_tl, tl_win], [pst[B_PA]])
            for k in range(8):
                MM(sc, pB[:, 0:256], hT_ap[:, k, :], w_in[:, k, 416:672], k == 0, k == 7,
                   [hT_tl, tl_win], [pst[B_PB]])
            for k in range(8):
                MM(sc, pC[:, 0:512], hT_ap[:, k, :], w_in[:, k, 672:1184], k == 0, k == 7,
                   [hT_tl, tl_win], [pst[B_PC]])
            for c in range(2):
                for k in range(8):
                    MM(sc, pP[:, c * 128:(c + 1) * 128],
                       w_in[:, k, 1184 + c * 128:1184 + (c + 1) * 128], hT_ap[:, k, :],
                       k == 0, k == 7, [hT_tl, tl_win], [pst[B_PP]])
            pp_ap, pp_tl = ppTst[gp]
            CP(sc, "dve", pp_ap[:, :, j4 * 128:(j4 + 1) * 128],
               pP[:, 0:256].rearrange("p (c t) -> p c t", c=2), [pst[B_PP]], [pp_tl])
            cq_ap, cq_tl = cq[par]
            ACTF(sc, junk[:, 0:384], pA[:, 0:384], AF.Square, [pst[B_PA]], [tl_junk, st_tl],
                 accum=st_ap[:, 2:3])
            RSTD(sc, st_ap[:, 3:4], st_ap[:, 2:3], 384, self.epsb[:, 0:1], [st_tl, self.tl_eps],
                 [st_tl])
            TS(sc, "dve", cq_ap, pA[:, 0:384], st_ap[:, 3:4], None, ALU.mult,
               None, [pst[B_PA], st_tl], [cq_tl])
            tpb = self.psb(B_TPB, BF16)
            for k in range(3):
                TR(sc, tpb[:, k * 128:(k + 1) * 128], cq_ap[:, k * 128:(k + 1) * 128], ident,
                   [cq_tl, tl_id], [pst[B_TPB]])
            cqT_ap, cqT_tl = cqT[par]
            CP(sc, "act", cqT_ap.rearrange("p a b -> p (a b)"), tpb[:, 0:384], [pst[B_TPB]],
               [cqT_tl])
            lat_ap, lat_tl = lat[par]
            latf_ap, latf_tl = latf[par]
            ACTF(sc, junk[:, 0:256], pB[:, 0:256], AF.Square, [pst[B_PB]], [tl_junk, st_tl],
                 accum=st_ap[:, 4:5])
            RSTD(sc, st_ap[:, 5:6], st_ap[:, 4:5], 256, self.epsb[:, 0:1], [st_tl, self.tl_eps],
                 [st_tl])
            TS(sc, "dve", latf_ap, pB[:, 0:256], st_ap[:, 5:6], None, ALU.mult, None,
               [pst[B_PB], st_tl], [latf_tl])
            TT(sc, "dve", lat_ap[:, 0:256], latf_ap, kvg, ALU.mult, [latf_tl, tl_kvg], [lat_tl])
            cos_i = cost[:, i, :]
            sin_i = sint[:, i, :]
            k1_ap, k1_tl = km1[par]
            k2_ap, k2_tl = km2[par]
            krv = pA[:, 384:416].rearrange("p (a b) -> p a b", a=2)
            TT(sc, "dve", k1_ap, krv, cos_i.unsqueeze(1).to_broadcast([128, 2, 16]), ALU.mult,
               [pst[B_PA], tl_cos], [k1_tl])
            TT(sc, "dve", k2_ap, krv, sin_i.unsqueeze(1).to_broadcast([128, 2, 16]), ALU.mult,
               [pst[B_PA], tl_sin], [k2_tl])
            TT(sc, "dve", lat_ap[:, 256:272], k1_ap[:, 0, :], k2_ap[:, 1, :], ALU.subtract,
               [k1_tl, k2_tl], [lat_tl])
            TT(sc, "dve", lat_ap[:, 272:288], k2_ap[:, 0, :], k1_ap[:, 1, :], ALU.add,
               [k1_tl, k2_tl], [lat_tl])
            for k in range(3):
                MM(sc, qps[:, 0:512], cqT_ap[:, k, :], w_uq[:, k, 0:512], k == 0, k == 2,
                   [cqT_tl, tl_wuq], [pst[B_Q0]])
            for k in range(3):
                MM(sc, qps2[:, 0:256], cqT_ap[:, k, :], w_uq[:, k, 512:768], k == 0, k == 2,
                   [cqT_tl, tl_wuq], [pst[B_Q1]])
            q_ap, q_tl = qsb[par]
            r1_ap, r1_tl = rm1[par]
            r2_ap, r2_tl = rm2[par]
            def rope_batch(src, srct, h0, nh):
                sv = src.rearrange("p (h d) -> p h d", h=nh)
                CP(sc, "act", q_ap[:, h0:h0 + nh, 0:64], sv[:, :, 0:64], srct, [q_tl])
                rv = sv[:, :, 64:96].rearrange("p h (a b) -> p h a b", a=2)
                cb = cos_i.unsqueeze(1).unsqueeze(1).to_broadcast([128, nh, 2, 16])
                sb_ = sin_i.unsqueeze(1).unsqueeze(1).to_broadcast([128, nh, 2, 16])
                TT(sc, "dve", r1_ap[:, h0:h0 + nh, :, :], rv, cb, ALU.mult, srct + [tl_cos], [r1_tl])
                TT(sc, "dve", r2_ap[:, h0:h0 + nh, :, :], rv, sb_, ALU.mult, srct + [tl_sin], [r2_tl])

            rope_batch(qps[:, 0:480], [pst[B_Q0]], 0, 5)
            rope_batch(qps2[:, 64:256], [pst[B_Q1]], 6, 2)
            CP(sc, "act", q_ap[:, 5, 0:32], qps[:, 480:512], [pst[B_Q0]], [q_tl])
            CP(sc, "act", q_ap[:, 5, 32:64], qps2[:, 0:32], [pst[B_Q1]], [q_tl])
            rv5 = qps2[:, 32:64].rearrange("p (a b) -> p a b", a=2)
            TT(sc, "dve", r1_ap[:, 5, :, :], rv5, cos_i.unsqueeze(1).to_broadcast([128, 2, 16]),
               ALU.mult, [pst[B_Q1], tl_cos], [r1_tl])
            TT(sc, "dve", r2_ap[:, 5, :, :], rv5, sin_i.unsqueeze(1).to_broadcast([128, 2, 16]),
               ALU.mult, [pst[B_Q1], tl_sin], [r2_tl])
            TT(sc, "dve", q_ap[:, :, 64:80], r1_ap[:, :, 0, :], r2_ap[:, :, 1, :], ALU.subtract,
               [r1_tl, r2_tl], [q_tl])
            TT(sc, "dve", q_ap[:, :, 80:96], r2_ap[:, :, 0, :], r1_ap[:, :, 1, :], ALU.add,
               [r1_tl, r2_tl], [q_tl])
            for h in range(H):
                TR(sc, tpb[0:96, h * 128:(h + 1) * 128], q_ap[:, h, :], ident, [q_tl, tl_id],
                   [pst[B_TPB]])
            qt_ap, qt_tl = QTst[gp]
            CP(sc, "act", qt_ap[0:96, :, j4 * 128:(j4 + 1) * 128],
               tpb[0:96, :].rearrange("p (h t) -> p h t", h=8), [pst[B_TPB]], [qt_tl])
            lt_ap, lt_tl = latTst[gp]
            TR(sc, tpa[:, 0:128], lat_ap[:, 0:128], ident, [lat_tl, tl_id], [pst[B_TPA]])
            TR(sc, tpa[:, 128:256], lat_ap[:, 128:256], ident, [lat_tl, tl_id], [pst[B_TPA]])
            TR(sc, tpa[0:32, 256:384], lat_ap[:, 256:288], ident, [lat_tl, tl_id], [pst[B_TPA]])
            CP(sc, "act", lt_ap[:, 0:2, j4 * 128:(j4 + 1) * 128],
               tpa[:, 0:256].rearrange("p (c t) -> p c t", c=2), [pst[B_TPA]], [lt_tl])
            CP(sc, "act", lt_ap[0:32, 2, j4 * 128:(j4 + 1) * 128], tpa[0:32, 256:384],
               [pst[B_TPA]], [lt_tl])
            z_ap, z_tl = z[par]
            ACTF(sc, z_ap, pC[:, 0:512], AF.Gelu_apprx_tanh, [pst[B_PC]], [z_tl])
            vs_ap, vs_tl = vsq[par]
            vt_ap, vt_tl = vtmp[par]
            vn_ap, vn_tl = vn[par]
            TT(sc, "dve", vs_ap, z_ap[:, 256:512], z_ap[:, 256:512], ALU.mult, [z_tl], [vs_tl])
            sc.add("dve", (lambda o, a: (lambda e: e.reduce_sum(o, a, AX.X)))(
                st_ap[:, 8:12], vs_ap.rearrange("p (g d) -> p g d", g=4)), [vs_tl], [st_tl])
            RSTD(sc, st_ap[:, 12:16], st_ap[:, 8:12], 64, self.epsb[:, 0:1], [st_tl, self.tl_eps],
                 [st_tl])
            TT(sc, "dve", vt_ap.rearrange("p (g d) -> p g d", g=4),
               z_ap[:, 256:512].rearrange("p (g d) -> p g d", g=4),
               st_ap[:, 12:16].unsqueeze(2).to_broadcast([128, 4, 64]), ALU.mult,
               [z_tl, st_tl], [vt_tl])
            TT(sc, "dve", vn_ap, vt_ap, sgg, ALU.mult, [vt_tl, tl_sgg], [vn_tl])
            for g in range(4):
                MM(sc, pP[:, 256 + g * 64:256 + (g + 1) * 64], wsT[:, g, :],
                   vn_ap[:, g * 64:(g + 1) * 64], True, True, [vn_tl, tl_wsT], [pst[B_PP]])
            c_ap, c_tl = cc[par]
            for g in range(4):
                STT(sc, "dve", c_ap[:, g * 64:(g + 1) * 64], pP[:, 256 + g * 64:256 + (g + 1) * 64],
                    bsT[:, g:g + 1], z_ap[:, g * 64:(g + 1) * 64], ALU.add, ALU.mult,
                    [pst[B_PP], tl_bsT, z_tl], [c_tl])
            TR(sc, tpb[:, 0:128], c_ap[:, 0:128], ident, [c_tl, tl_id], [pst[B_TPB]])
            TR(sc, tpb[:, 128:256], c_ap[:, 128:256], ident, [c_tl, tl_id], [pst[B_TPB]])
            ct_ap, ct_tl = cTst[gp]
            CP(sc, "act", ct_ap[:, :, j4 * 128:(j4 + 1) * 128],
               tpb[:, 0:256].rearrange("p (c t) -> p c t", c=2), [pst[B_TPB]], [ct_tl])
            if j4 == 3:
                ts_ = slice(g4 * 512, (g4 + 1) * 512)
                DMA(sc, "pool", QT[:, ts_].rearrange("(h r) t -> r h t", h=8), qt_ap[0:96, :, :],
                    [qt_tl], [])
                DMA(sc, "pool", latT[0:256, ts_].rearrange("(c p) t -> p c t", c=2),
                    lt_ap[:, 0:2, :], [lt_tl], [])
                DMA(sc, "pool", latT[256:288, ts_], lt_ap[0:32, 2, :], [lt_tl], [])
                DMA(sc, "pool", ppT[:, ts_].rearrange("(c p) t -> p c t", c=2), pp_ap, [pp_tl], [])
                DMA(sc, "pool", cT[:, ts_].rearrange("(c p) t -> p c t", c=2), ct_ap, [ct_tl], [])
                if g4 == 0:
                    DMA(sc, "pool", halo[:, 0:8].rearrange("(c p) t -> p c t", c=2),
                        pp_ap[:, :, 0:8], [pp_tl], [])
                if g4 == 7:
                    DMA(sc, "pool", halo[:, 8:16].rearrange("(c p) t -> p c t", c=2),
                        pp_ap[:, :, 504:512], [pp_tl], [])

    def exchange(self, l):
        sc = self.sc
        sc.barrier()
        latT = self.dr["latT_a%d" % l]
        halo = self.dr["halo_a%d" % l]
        g1 = self.dint("g1_%d" % l, [2 * 128, T], BF16)
        g2 = self.dint("g2_%d" % l, [2 * 160, T], BF16)
        gh = self.dint("gh_%d" % l, [2 * 256, 16], F32)
        tl_h = sc.tile("gath_h%d" % l)
        tl_1 = sc.tile("gath_1_%d" % l)
        tl_2 = sc.tile("gath_2_%d" % l)
        self.tl_gh, self.tl_g1, self.tl_g2 = [tl_h], [tl_1], [tl_2]
        pairs = [[0, 1], [2, 3], [4, 5], [6, 7]]
        for (src, dst, tl) in ((halo[:, :], gh, tl_h), (latT[0:128, :], g1, tl_1),
                               (latT[128:288, :], g2, tl_2)):
            sc.add("pool", (lambda a, b: (lambda e: e.collective_compute(
                "AllGather", ALU.bypass, replica_groups=pairs, ins=[a], outs=[b[:, :]])))(src, dst),
                [], [tl], cc=True)

    def stage_post(self, l):
        nc, sc, A = self.nc, self.sc, self.A
        sfx = "_p%d" % l
        pst = self.pst
        ident, tl_id = self.ident, self.tl_id
        if self.fused:
            final = (l == 1) or self.onel
            xin = self.dr["x_in_a0"] if l == 0 else self.dr["xout_p0"]
            g1, g2, gh = self.dr["g1_%d" % l], self.dr["g2_%d" % l], self.dr["gh_%d" % l]
            tl_gh, tl_g1, tl_g2 = self.tl_gh, self.tl_g1, self.tl_g2
            ckv_src = lambda r, c: (g1[r * 128:(r + 1) * 128, :] if c == 0
                                    else g2[r * 160:r * 160 + 128, :])
            kr_src = lambda r: g2[r * 160 + 128:r * 160 + 160, :]
            halo_src = lambda r, a, b: gh[r * 256:(r + 1) * 256, a:b]
            QT = self.dr["QT_a%d" % l]
            ppT = self.dr["ppT_a%d" % l]
            cTd = self.dr["cT_a%d" % l]
            xmid_ = self.dint("xmid" + sfx, [T, D], F32)
            if final:
                xout = self.dout("y_out", [T, D], F32)
            else:
                xout = xmid_
                self.dr["xout" + sfx] = xout
        else:
            final = self.final
            tl_gh, tl_g1, tl_g2 = [], [], []
            xin = self.din("x_in" + sfx, [T, D], F32)
            latp = self.din("latp" + sfx, [2 * 288, T], BF16)
            halop = self.din("halop" + sfx, [2 * 256, 16], F32)
            ckv_src = lambda r, c: latp[r * 288 + c * 128:r * 288 + (c + 1) * 128, :]
            kr_src = lambda r: latp[r * 288 + 256:r * 288 + 288, :]
            halo_src = lambda r, a, b: halop[r * 256:(r + 1) * 256, a:b]
            QT = self.din("QT" + sfx, [H * 96, T], BF16)
            ppT = self.din("ppT" + sfx, [256, T], F32)
            cTd = self.din("cT" + sfx, [256, T], BF16)
            if final:
                xout = self.dout("y_out", [T, D], F32)
            else:
                xout = self.dout("xout" + sfx, [T, D], F32)
        xmid = self.dint("xmid" + sfx, [T, D], F32)
        hmask_d = self.din("hmask" + sfx, [128, 2], F32)
        edge_d = self.din("edge" + sfx, [128, 2, 16], F32)
        wkn_d = self.din("wkn" + sfx, [128, 2, 512], F32)
        wv_d = self.din("wv" + sfx, [128, 2, 512], F32)
        wpb_d = self.din("wpb" + sfx, [128, 2, 128], F32)
        psc_d = self.din("psc" + sfx, [128, 2], F32)
        w_o_d = self.din("w_o" + sfx, [128, 8, 1024], F32)
        ffng_d = self.din("ffng" + sfx, [128, 8], F32)
        wg_d = self.din("w_gate" + sfx, [128, 8, DFF], F32)
        wu_d = self.din("w_up" + sfx, [128, 8, DFF], F32)
        wd_d = self.din("w_down" + sfx, [128, NFF, 1024], F32)
        if final:
            fin_d = self.din("fing", [128, D], F32)

        sc.barrier(skip_cc=self.fused)
        A.release(self.base_mark)
        aT, tl_aT = A.alloc("aT", (4, T), BF16, tl=False)
        bT, tl_bT = A.alloc("bT", (2, T), BF16)
        aT_tl = [[sc.tile("aT%d_%d" % (h, qb)) for qb in range(8)] for h in range(H)]
        mix_mark = A.mark()

        self.setup_stage_ring(512)
        wpb, tl_wpb = A.alloc("wpb", (2, 128), BF16)
        psc, tl_psc = A.alloc("psc", (2,), F32)
        hmask, tl_hm = A.alloc("hmask", (2,), F32)
        edge, tl_edge = A.alloc("edge", (2, 16), F32)
        self.load_cast(wpb.rearrange("p a b -> p (a b)"), tl_wpb,
                       wpb_d[:, :, :].rearrange("p a b -> p (a b)"), 256, eng="dve")
        self.load_f32(psc, tl_psc, psc_d[:, :])
        self.load_f32(hmask, tl_hm, hmask_d[:, :])
        self.load_f32(edge, tl_edge, edge_d[:, :, :])
        W = T + 16
        ppx, tl_ppx = A.alloc("ppx", (2, W), F32)
        s1, tl_s1 = A.alloc("pl_s1", (W,), F32)
        s2, tl_s2 = A.alloc("pl_s2", (W,), F32)
        dT, tl_dT = A.alloc("dT", (2, T), BF16)
        etmp, tl_et = A.alloc("etmp", (16,), F32)
        DMA(sc, "sp", ppx[:, :, 8:8 + T], ppT[:, :].rearrange("(c p) t -> p c t", c=2), [],
            [tl_ppx])
        hl, tl_hl = A.alloc("hl", (2, 16), F32)
        DMA(sc, "sp", hl[:, :, 0:8], halo_src(0, 8, 16).rearrange("(c p) t -> p c t", c=2), tl_gh,
            [tl_hl])
        DMA(sc, "sp", hl[:, :, 8:16], halo_src(1, 0, 8).rearrange("(c p) t -> p c t", c=2), tl_gh,
            [tl_hl])
        TS(sc, "dve", ppx[:, :, 0:8], hl[:, :, 0:8], hmask[:, 0:1], None, ALU.mult, None,
           [tl_hl, tl_hm], [tl_ppx])
        TS(sc, "dve", ppx[:, :, 8 + T:16 + T], hl[:, :, 8:16], hmask[:, 1:2], None, ALU.mult, None,
           [tl_hl, tl_hm], [tl_ppx])
        for c in range(2):
            p_c = ppx[:, c, :]
            levels_needed = (1, 2) if c == 0 else (3, 4)
            TT(sc, "dve", s1[:, 0:W - 1], p_c[:, 0:W - 1], p_c[:, 1:W], ALU.add, [tl_ppx], [tl_s1])
            cur, cur_tl, oth, oth_tl = s1, tl_s1, s2, tl_s2
            lvl = 1
            ln = W - 1
            for gi in range(2):
                g = c * 2 + gi
                want = levels_needed[gi]
                while lvl < want:
                    sh = 1 << lvl
                    TT(sc, "dve", oth[:, 0:ln - sh], cur[:, 0:ln - sh], cur[:, sh:ln], ALU.add,
                       [cur_tl], [oth_tl])
                    ln -= sh
                    cur, cur_tl, oth, oth_tl = oth, oth_tl, cur, cur_tl
                    lvl += 1
                w = 2 << g
                left = w // 2
                pr = slice(gi * 64, gi * 64 + 64)
                o = 8 - left
                STT(sc, "dve", dT[pr, c, :], cur[pr, o:o + T], 1.0 / w, p_c[pr, 8:8 + T],
                    ALU.mult, ALU.subtract, [cur_tl, tl_ppx], [tl_dT])
                for (tc0, ec0) in ((0, 0), (T - 8, 8)):
                    TT(sc, "dve", etmp[pr, 0:8], cur[pr, o + tc0:o + tc0 + 8],
                       edge[pr, c, ec0:ec0 + 8], ALU.mult, [cur_tl, tl_edge], [tl_et])
                    TT(sc, "dve", dT[pr, c, tc0:tc0 + 8], etmp[pr, 0:8],
                       p_c[pr, 8 + tc0:16 + tc0], ALU.subtract, [tl_et, tl_ppx], [tl_dT])
        for c in range(2):
            for tb in range(8):
                bk = 2 + (tb % 2)
                MM(sc, self.psb(bk)[:, 0:512], wpb[:, c, :], dT[:, c, tb * 512:(tb + 1) * 512],
                   True, True, [tl_wpb, tl_dT], [pst[bk]])
                TS(sc, "dve", bT[:, c, tb * 512:(tb + 1) * 512], self.psb(bk)[:, 0:512],
                   psc[:, c:c + 1], None, ALU.mult, None, [pst[bk], tl_psc], [tl_bT])

        sc.barrier()
        A.release(mix_mark)
        self.setup_stage_ring(512)
        wkn, tl_wkn = A.alloc("wkn", (2, 512), BF16)
        wv, tl_wv = A.alloc("wv", (2, 512), BF16)
        for k in range(2):
            self.load_cast(wkn[:, k, :], tl_wkn, wkn_d[:, k, :], 512, eng="dve")
            self.load_cast(wv[:, k, :], tl_wv, wv_d[:, k, :], 512, eng="pool")
        ones_b, tl_ones = A.alloc("ones_b", (128,), BF16)
        sc.add("dve", lambda e: e.memset(ones_b, 1.0), [], [tl_ones])
        rhi, tl_rhi = A.alloc("rhi", (512,), BF16)
        rlo, tl_rlo = A.alloc("rlo", (512,), BF16)
        lt = [A.alloc("lat_r%d" % r, (2, T), BF16) for r in range(2)]
        for r in range(2):
            for c in range(2):
                DMA(sc, "sp", lt[r][0][:, c, :], ckv_src(r, c), (tl_g1 if c == 0 else tl_g2), [lt[r][1]])
        KT = [A.alloc("KT%d" % i, (S,), BF16) for i in range(2)]
        for i in range(2):
            for r in range(2):
                DMA(sc, "sp", KT[i][0][64:96, r * T:(r + 1) * T], kr_src(r), tl_g2, [KT[i][1]])
        V = [A.alloc("V%d" % i, (64, 193), BF16) for i in range(2)]
        for i in range(2):
            v_ap, v_tl = V[i]
            sc.add("pool", (lambda a: (lambda e: e.memset(a, 0.0)))(v_ap), [], [v_tl])
            sc.add("pool", (lambda a: (lambda e: e.memset(a, 1.0)))(v_ap[:, :, 64:65]), [], [v_tl])
            sc.add("pool", (lambda a: (lambda e: e.memset(a, 1.0)))(v_ap[:, :, 129:130]), [], [v_tl])
        QTh = [A.alloc("QTh%d" % i, (T,), BF16) for i in range(2)]
        P = [A.alloc("P%d" % i, (1024,), BF16) for i in range(3)]
        rcp, tl_rcp = A.alloc("rcp", (512,), F32)
        osb = [A.alloc("osb%d" % i, (512,), F32) for i in range(2)]
        an = [A.alloc("an%d" % i, (512,), BF16) for i in range(2)]
        B_ACC = (4, 5)
        B_BC = 6
        B_KV = 7

        def kv_items(h):
            items = []
            kt_ap, kt_tl = KT[h % 2]
            pair = h // 2
            v_ap, v_tl = V[pair % 2]
            for r in range(2):
                for tb in range(8):
                    def k_item(r=r, tb=tb):
                        bk = B_KV
                        for k in range(2):
                            MM(sc, self.psb(bk)[0:64, 0:512], wkn[:, k, h * 64:(h + 1) * 64],
                               lt[r][0][:, k, tb * 512:(tb + 1) * 512], k == 0, k == 1,
                               [tl_wkn, lt[r][1]], [pst[bk]])
                        CP(sc, "dve", kt_ap[0:64, r * T + tb * 512:r * T + (tb + 1) * 512],
                           self.psb(bk)[0:64, 0:512], [pst[bk]], [kt_tl])
                    items.append(k_item)
            if h % 2 == 0:
                for kc4 in range(16):
                    def v_item(kc4=kc4):
                        bk = B_KV
                        for q4 in range(4):
                            kc = kc4 * 4 + q4
                            r, tt = kc // 32, (kc % 32) * 128
                            for k in range(2):
                                MM(sc, self.psb(bk)[:, q4 * 128:(q4 + 1) * 128],
                                   lt[r][0][:, k, tt:tt + 128],
                                   wv[:, k, pair * 128:(pair + 1) * 128],
                                   k == 0, k == 1, [lt[r][1], tl_wv], [pst[bk]])
                        src = self.psb(bk)[:, 0:512].rearrange("p (c e d) -> p c e d", c=4, e=2)
                        for e_ in range(2):
                            CP(sc, "dve", v_ap[:, kc4 * 4:(kc4 + 1) * 4, e_ * 65:e_ * 65 + 64],
                               src[:, :, e_, :], [pst[bk]], [v_tl])
                    items.append(v_item)
            return items

        def finish_unit(h, qb, acc_b, unit):
            e = h % 2
            pair = h // 2
            acc = self.psb(acc_b)
            sc.add("dve", (lambda o, a: (lambda en: en.reciprocal(o, a)))(
                rcp[64:65, :], acc[64:65, 0:512]), [pst[acc_b]], [tl_rcp])
            CP(sc, "dve", rhi[64:65, :], rcp[64:65, :], [tl_rcp], [tl_rhi])
            TT(sc, "dve", rlo[64:65, :], rcp[64:65, :], rhi[64:65, :], ALU.subtract,
               [tl_rcp, tl_rhi], [tl_rlo])
            MM(sc, self.psb(B_BC)[0:64, 0:512], ones_b[64:65, 0:64], rhi[64:65, :], True, False,
               [tl_ones, tl_rhi], [pst[B_BC]])
            MM(sc, self.psb(B_BC)[0:64, 0:512], ones_b[64:65, 0:64], rlo[64:65, :], False, True,
               [tl_ones, tl_rlo], [pst[B_BC]])
            o_ap, o_tl = osb[unit % 2]
            n_ap, n_tl = an[unit % 2]
            CP(sc, "dve", o_ap[0:64, :], acc[0:64, 0:512], [pst[acc_b]], [o_tl])
            TT(sc, "dve", n_ap[0:64, :], o_ap[0:64, :], self.psb(B_BC)[0:64, 0:512], ALU.mult,
               [o_tl, pst[B_BC]], [n_tl])
            DMA(sc, "pool", aT[e * 64:(e + 1) * 64, pair, qb * 512:(qb + 1) * 512],
                n_ap[0:64, :], [n_tl], [aT_tl[h][qb]])

        DMA(sc, "sp", QTh[0][0][0:96, :], QT[0:96, :], [], [QTh[0][1]])
        for it in kv_items(0):
            it()
        steps = [(h, qb, kp) for h in range(H) for qb in range(8) for kp in range(32)]
        LA = 1
        nsteps = len(steps)
        bg = []
        for j in range(nsteps + LA):
            if j < nsteps:
                h, qb, kp = steps[j]
                if qb == 0 and kp == 0:
                    if h + 1 < H:
                        DMA(sc, "sp", QTh[(h + 1) % 2][0][0:96, :], QT[(h + 1) * 96:(h + 2) * 96, :],
                            [], [QTh[(h + 1) % 2][1]])
                        bg = kv_items(h + 1)
                    else:
                        bg = []
                kt_ap, kt_tl = KT[h % 2]
                q_ap, q_tl = QTh[h % 2]
                qs = q_ap[0:96, qb * 512:(qb + 1) * 512]
                sd = j % 2
                p_ap, p_tl = P[j % 3]
                for t in range(2):
                    kc = 2 * kp + t
                    bk = 2 * sd + t
                    MM(sc, self.psb(bk)[:, 0:512], kt_ap[0:96, kc * 128:(kc + 1) * 128], qs,
                       True, True, [kt_tl, q_tl], [pst[bk]])
                for t in range(2):
                    ACTF(sc, p_ap[:, t * 512:(t + 1) * 512], self.psb(2 * sd + t)[:, 0:512], AF.Exp,
                         [pst[2 * sd + t]], [p_tl], scale=SCALE)
            if j >= LA:
                jj = j - LA
                h2, qb2, kp2 = steps[jj]
                unit2 = h2 * 8 + qb2
                acc_b = B_ACC[unit2 % 2]
                acc = self.psb(acc_b)
                pair2 = h2 // 2
                v_ap, v_tl = V[pair2 % 2]
                e2 = h2 % 2
                p2_ap, p2_tl = P[jj % 3]
                for t in range(2):
                    kc = 2 * kp2 + t
                    MM(sc, acc[:, 0:512], v_ap[:, kc, e2 * 65:e2 * 65 + 128],
                       p2_ap[:, t * 512:(t + 1) * 512], kc == 0, kc == 63, [v_tl, p2_tl],
                       [pst[acc_b]])
                if kp2 == 31:
                    finish_unit(h2, qb2, acc_b, unit2)
                if bg and (jj % 8 == 7):
                    bg.pop(0)()
            if j < nsteps and steps[j][1] == 7 and steps[j][2] == 31:
                while bg:
                    bg.pop(0)()

        sc.barrier()
        A.release(mix_mark)
        self.setup_stage_ring(1024)
        cT, tl_cT = A.alloc("cT", (2, T), BF16)
        DMA(sc, "sp", cT, cTd[:, :].rearrange("(c p) t -> p c t", c=2), [], [tl_cT])
        w_o, tl_wo = A.alloc("w_o", (8, 1024), BF16)
        for k in range(8):
            self.load_cast(w_o[:, k, :], tl_wo, w_o_d[:, k, :], 1024)
        xt = [A.alloc("xo%d" % i, (D,), F32) for i in range(2)]
        xo = [A.alloc("xr%d" % i, (D,), F32) for i in range(2)]
        for i in range(NT):
            x_ap, x_tl = xt[i % 2]
            r_ap, r_tl = xo[i % 2]
            DMA(sc, "sp", x_ap, xin[i * 128:(i + 1) * 128, :], [], [x_tl])
            b0 = 2 * (i % 2)
            tsl = slice(i * 128, (i + 1) * 128)
            qb = i // 4
            for nh in range(2):
                bk = b0 + nh
                for k in range(8):
                    if k < 4:
                        lhs = aT[:, k, tsl]
                        rt = [aT_tl[2 * k][qb], aT_tl[2 * k + 1][qb]]
                    elif k < 6:
                        lhs = bT[:, k - 4, tsl]
                        rt = [tl_bT]
                    else:
                        lhs = cT[:, k - 6, tsl]
                        rt = [tl_cT]
                    MM(sc, self.psb(bk)[:, 0:512], lhs, w_o[:, k, nh * 512:(nh + 1) * 512],
                       k == 0, k == 7, rt + [tl_wo], [pst[bk]])
                TT(sc, "dve", r_ap[:, nh * 512:(nh + 1) * 512], x_ap[:, nh * 512:(nh + 1) * 512],
                   self.psb(bk)[:, 0:512], ALU.add, [x_tl, pst[bk]], [r_tl])
            DMA(sc, "pool", xmid[tsl, :], r_ap, [r_tl], [])

        sc.barrier()
        A.release(self.base_mark)
        self.setup_stage_ring()
        ffng, tl_fg = A.alloc("ffng", (8,), F32)
        self.load_f32(ffng, tl_fg, ffng_d[:, :])
        wg, tl_wg = A.alloc("wg", (8, DFF), BF16)
        wu, tl_wu = A.alloc("wu", (8, DFF), BF16)
        wd, tl_wd = A.alloc("wd", (NFF, 1024), BF16)
        ffn_eng = ("dve", "act", "dve", "act", "pool")
        fe = [0]

        def next_eng():
            e_ = ffn_eng[fe[0] % len(ffn_eng)]
            fe[0] += 1
            return e_

        tl_wg = [[sc.tile("wg%d_%d" % (k, hf)) for hf in range(2)] for k in range(8)]
        tl_wu = [[sc.tile("wu%d_%d" % (k, hf)) for hf in range(2)] for k in range(8)]
        tl_wd = [sc.tile("wd%d" % j) for j in range(NFF)]
        for hf in range(2):
            cs = slice(hf * 1408, (hf + 1) * 1408)
            for k in range(8):
                self.load_cast(wg[:, k, cs], tl_wg[k][hf], wg_d[:, k, cs], 1408,
                               scale=ffng[:, k:k + 1], scale_tl=tl_fg, eng=next_eng())
                self.load_cast(wu[:, k, cs], tl_wu[k][hf], wu_d[:, k, cs], 1408,
                               scale=ffng[:, k:k + 1], scale_tl=tl_fg, eng=next_eng())
        for j in range(NFF):
            self.load_cast(wd[:, j, :], tl_wd[j], wd_d[:, j, :], 1024, eng=next_eng())
        if final:
            fing, tl_fing = A.alloc("fing", (D,), F32)
            self.load_f32(fing, tl_fing, fin_d[:, :])
        NB = 256
        NSUB = NB // 128
        xb = [[A.alloc("xb%d_%d" % (i, s), (D,), F32) for s in range(NSUB)] for i in range(2)]
        hn2, tl_hn2 = A.alloc("hn2", (D,), BF16)
        junk, tl_junk = A.alloc("junk2", (D,), BF16)
        st, tl_st = A.alloc("st2", (8,), F32)
        h2T = [A.alloc("h2T%d" % i, (8, NB), BF16) for i in range(2)]
        actT, tl_act = A.alloc("actT", (NFF, NB), BF16)
        sg = [A.alloc("silu%d" % i, (NB,), F32) for i in range(2)]
        B_TP = 0
        B_G = (1, 2)
        B_U = (3, 4)
        B_Y = (5, 6)
        gi = 0
        yi = 0
        for blk in range(T // NB):
            bp = blk % 2
            h2_ap, h2_tl = h2T[bp]
            for s in range(NSUB):
                x_ap, x_tl = xb[bp][s]
                tok0 = blk * NB + s * 128
                DMA(sc, "sp", x_ap, xmid[tok0:tok0 + 128, :], [], [x_tl])
                ACTF(sc, junk, x_ap, AF.Square, [x_tl], [tl_junk, tl_st], accum=st[:, 0:1])
                RSTD(sc, st[:, 1:2], st[:, 0:1], D, self.epsb[:, 0:1], [tl_st, self.tl_eps], [tl_st])
                TS(sc, "dve", hn2, x_ap, st[:, 1:2], None, ALU.mult, None, [x_tl, tl_st],
                   [tl_hn2])
                tp = self.psb(B_TP, BF16)
                for k in range(8):
                    TR(sc, tp[:, k * 128:(k + 1) * 128], hn2[:, k * 128:(k + 1) * 128], ident,
                       [tl_hn2, tl_id], [pst[B_TP]])
                CP(sc, "dve", h2_ap[:, :, s * 128:(s + 1) * 128],
                   tp.rearrange("p (k t) -> p k t", k=8), [pst[B_TP]], [h2_tl])
            for j in range(NFF):
                gb = B_G[gi % 2]
                ub = B_U[gi % 2]
                s_ap, s_tl = sg[gi % 2]
                gi += 1
                for k in range(8):
                    MM(sc, self.psb(gb)[:, 0:NB], wg[:, k, j * 128:(j + 1) * 128], h2_ap[:, k, :],
                       k == 0, k == 7, [tl_wg[k][j // 11], h2_tl], [pst[gb]])
                for k in range(8):
                    MM(sc, self.psb(ub)[:, 0:NB], wu[:, k, j * 128:(j + 1) * 128], h2_ap[:, k, :],
                       k == 0, k == 7, [tl_wu[k][j // 11], h2_tl], [pst[ub]])
                ACTF(sc, s_ap, self.psb(gb)[:, 0:NB], AF.Silu, [pst[gb]], [s_tl])
                TT(sc, "dve", actT[:, j, :], s_ap, self.psb(ub)[:, 0:NB], ALU.mult,
                   [s_tl, pst[ub]], [tl_act])
            for s in range(NSUB):
                x_ap, x_tl = xb[bp][s]
                y_ap, y_tl = x_ap, x_tl
                tok0 = blk * NB + s * 128
                for nh in range(2):
                    bk = B_Y[nh]
                    for j in range(NFF):
                        MM(sc, self.psb(bk)[:, 0:512], actT[:, j, s * 128:(s + 1) * 128],
                           wd[:, j, nh * 512:(nh + 1) * 512], j == 0, j == NFF - 1,
                           [tl_act, tl_wd[j]], [pst[bk]])
                    TT(sc, "dve", y_ap[:, nh * 512:(nh + 1) * 512],
                       x_ap[:, nh * 512:(nh + 1) * 512], self.psb(bk)[:, 0:512], ALU.add,
                       [x_tl, pst[bk]], [y_tl])
                if final:
                    ACTF(sc, junk, y_ap, AF.Square, [y_tl], [tl_junk, tl_st], accum=st[:, 2:3])
                    RSTD(sc, st[:, 3:4], st[:, 2:3], D, self.epsb[:, 0:1], [tl_st, self.tl_eps],
                         [tl_st])
                    STT(sc, "dve", y_ap, y_ap, st[:, 3:4], fing, ALU.mult, ALU.mult,
                        [y_tl, tl_st, tl_fing], [y_tl])
                DMA(sc, "pool", xout[tok0:tok0 + 128, :], y_ap, [y_tl], [])


def _pk(w, k):
    n = w.shape[1]
    return np.ascontiguousarray(w.reshape(k, 128, n).transpose(1, 0, 2))


def _prep_a(l, P):
    o1, o2, o3, o4 = 384, 640, 672, 928
    w_in = P["w_in"][l]
    perm = np.concatenate([np.arange(0, o1), np.arange(o2, o3), np.arange(o1, o2),
                           np.arange(o4, 1440), np.arange(o3, o4)])
    sfx = "_a%d" % l
    d = {}
    d["w_in" + sfx] = _pk(w_in[:, perm], 8)
    d["mixg" + sfx] = np.ascontiguousarray(P["mix_norm"][l].reshape(8, 128).T)
    d["w_uq" + sfx] = _pk(P["w_uq"][l], 3)
    d["qg" + sfx] = np.ascontiguousarray(P["q_norm"][l].reshape(3, 128).T)
    d["kvg" + sfx] = np.ascontiguousarray(np.broadcast_to(P["kv_norm"][l][None, :], (128, 256)))
    d["sgg" + sfx] = np.ascontiguousarray(np.broadcast_to(P["sg_norm"][l][None, :], (128, 256)))
    d["wsT" + sfx] = np.ascontiguousarray(P["w_s"][l].transpose(2, 0, 1))
    d["bsT" + sfx] = np.ascontiguousarray(P["b_s"][l].T)
    inv = (10000.0 ** (-np.arange(0, 32, 2, dtype=np.float32) / 32)).astype(np.float32)
    d["invf" + sfx] = np.ascontiguousarray(np.broadcast_to(inv[None, :], (128, 16)))
    return d


def _prep_p(l, P, final):
    sfx = "_p%d" % l
    d = {}
    wkv = P["w_ukv"][l].reshape(256, 8, 128)
    d["wkn" + sfx] = _pk(np.ascontiguousarray(wkv[:, :, :64]).reshape(256, 512), 2)
    d["wv" + sfx] = _pk(np.ascontiguousarray(wkv[:, :, 64:]).reshape(256, 512), 2)
    wpb = np.zeros((128, 2, 128), np.float32)
    for g in range(4):
        c, o = g // 2, (g % 2) * 64
        wpb[o:o + 64, c, o:o + 64] = P["w_pool"][l][g]
    d["wpb" + sfx] = wpb
    d["psc" + sfx] = np.ascontiguousarray(P["pool_scale"][l].reshape(2, 128).T)
    d["w_o" + sfx] = _pk(P["w_o"][l], 8)
    d["ffng" + sfx] = np.ascontiguousarray(P["ffn_norm"][l].reshape(8, 128).T)
    d["w_gate" + sfx] = _pk(P["w_gate"][l], 8)
    d["w_up" + sfx] = _pk(P["w_up"][l], 8)
    d["w_down" + sfx] = _pk(P["w_down"][l], NFF)
    if final:
        d["fing"] = np.ascontiguousarray(np.broadcast_to(P["final_norm"][None, :], (128, D)))
    return d


def _core_consts(c, sfx):
    half = c % 2
    hmask = np.zeros((128, 2), np.float32)
    hmask[:, 0] = 1.0 if half == 1 else 0.0
    hmask[:, 1] = 1.0 if half == 0 else 0.0
    edge = np.zeros((128, 2, 16), np.float32)
    for g in range(4):
        w = 2 << g
        left = w // 2
        right = w - 1 - left
        cch, o = g // 2, (g % 2) * 64
        for e in range(16):
            t = (e if e < 8 else T - 16 + e) + half * T
            lo = max(t - left, 0)
            hi = min(t + right + 1, S)
            edge[o:o + 64, cch, e] = 1.0 / float(hi - lo)
    return {"hmask" + sfx: hmask, "edge" + sfx: edge}


_IDENT = np.eye(128, dtype=np.float32)
_PROGS = {}


def _get_prog(key, post_layer, a_layer, final):
    if key not in _PROGS:
        p = Prog(post_layer, a_layer, final)
        p.build()
        _PROGS[key] = p
    return _PROGS[key]


def _pair_cat(res, name, c):
    b = c // 2
    return np.concatenate([res[2 * b][name], res[2 * b + 1][name]], axis=0)


def _kernel_fused(P, xs, poss, cores):
    if "F" not in _PROGS:
        p = Prog(None, None, True, fused=True)
        p.build()
        _PROGS["F"] = p
    p = _PROGS["F"]
    shared = {}
    shared.update(_prep_a(0, P))
    shared.update(_prep_p(0, P, False))
    shared.update(_prep_a(1, P))
    shared.update(_prep_p(1, P, True))
    maps = []
    for c in cores:
        m = {"ident": _IDENT, "x_in_a0": xs[c], "pos_a0": poss[c], "pos_a1": poss[c]}
        m.update(_core_consts(c, "_p0"))
        m.update(_core_consts(c, "_p1"))
        m.update(shared)
        maps.append(m)
    r = run_bass_kernel_spmd(p.nc, maps, core_ids=cores).results
    out = np.empty((4, S, D), np.float32)
    for c in cores:
        out[c // 2, (c % 2) * T:(c % 2 + 1) * T, :] = r[c]["y_out"]
    return out


def kernel(**inputs):
    P = {k: np.asarray(v) for k, v in inputs.items()}
    x = P["x"]
    pos = P["positions"]
    cores = list(range(NCORES))
    xs = [np.ascontiguousarray(x[c // 2, (c % 2) * T:(c % 2 + 1) * T, :]) for c in cores]
    poss = [np.ascontiguousarray(pos[c // 2, (c % 2) * T:(c % 2 + 1) * T].reshape(NT, 128).T)
            for c in cores]
    if FUSED:
        return _kernel_fused(P, xs, poss, cores)

    p1 = _get_prog("L1", None, 0, False)
    wa0 = _prep_a(0, P)
    maps = []
    for c in cores:
        m = {"ident": _IDENT, "x_in_a0": xs[c], "pos_a0": poss[c]}
        m.update(wa0)
        maps.append(m)
    r1 = run_bass_kernel_spmd(p1.nc, maps, core_ids=cores).results

    p2 = _get_prog("L2", 0, 1, False)
    wp0 = _prep_p(0, P, False)
    wa1 = _prep_a(1, P)
    maps = []
    for c in cores:
        m = {"ident": _IDENT, "x_in_p0": xs[c], "pos_a1": poss[c],
             "latp_p0": _pair_cat(r1, "latT_a0", c), "halop_p0": _pair_cat(r1, "halo_a0", c),
             "QT_p0": r1[c]["QT_a0"], "ppT_p0": r1[c]["ppT_a0"], "cT_p0": r1[c]["cT_a0"]}
        m.update(_core_consts(c, "_p0"))
        m.update(wp0)
        m.update(wa1)
        maps.append(m)
    r2 = run_bass_kernel_spmd(p2.nc, maps, core_ids=cores).results

    p3 = _get_prog("L3", 1, None, True)
    wp1 = _prep_p(1, P, True)
    maps = []
    for c in cores:
        m = {"ident": _IDENT, "x_in_p1": r2[c]["xout_p0"],
             "latp_p1": _pair_cat(r2, "latT_a1", c), "halop_p1": _pair_cat(r2, "halo_a1", c),
             "QT_p1": r2[c]["QT_a1"], "ppT_p1": r2[c]["ppT_a1"], "cT_p1": r2[c]["cT_a1"]}
        m.update(_core_consts(c, "_p1"))
        m.update(wp1)
        maps.append(m)
    r3 = run_bass_kernel_spmd(p3.nc, maps, core_ids=cores).results

    out = np.empty((4, S, D), np.float32)
    for c in cores:
        out[c // 2, (c % 2) * T:(c % 2 + 1) * T, :] = r3[c]["y_out"]
    return out
```

```python
import numpy as np
import ml_dtypes
from contextlib import ExitStack
import concourse.bass as bass
import concourse.mybir as mybir
from concourse.bass_utils import run_bass_kernel_spmd

F32 = mybir.dt.float32
BF16 = mybir.dt.bfloat16
I32 = mybir.dt.int32
AF = mybir.ActivationFunctionType
ALU = mybir.AluOpType
AX = mybir.AxisListType

NCORES = 8
D = 1024
T = 4096
NT = 32
S = 8192
H = 8
DFF = 2816
NFF = 22
EPS = 1e-6
SCALE = 96.0 ** -0.5
PI = float(np.pi)
FUSED = True


class Tl:
    __slots__ = ("name", "w", "r")

    def __init__(self, name=""):
        self.name = name
        self.w = []
        self.r = []


class Sched:
    STREAMS = ("pe", "act", "dve", "pool", "sp")
    KRING = 8

    def __init__(self, same_sync=True):
        self.ops = {s: [] for s in self.STREAMS}
        self.ndma = {s: 0 for s in self.STREAMS}
        self.known_e = {s: {} for s in self.STREAMS}
        self.known_d = {s: set() for s in self.STREAMS}
        self.same_sync = same_sync
        self.tiles = []
        self.ncc = 0

    def tile(self, name=""):
        t = Tl(name)
        self.tiles.append(t)
        return t

    def _filter(self, stream, deps):
        ke = self.known_e[stream]
        kd = self.known_d[stream]
        best = {}
        res = []
        for tok in deps:
            if tok[0] == "e":
                _, P, i = tok
                if P == stream and (stream == "pe" or not self.same_sync):
                    continue
                if ke.get(P, -1) >= i:
                    continue
                if best.get(P, -1) < i:
                    best[P] = i
            else:
                if tok in kd:
                    continue
                kd.add(tok)
                res.append(tok)
        for P, i in best.items():
            res.append(("e", P, i))
            ke[P] = i
            self.ops[P][i]["inc"] = True
        return res

    def add(self, stream, fn, reads=(), writes=(), dma=False, cc=False):
        deps = set()
        for t in reads:
            deps.update(t.w)
        for t in writes:
            deps.update(t.w)
            deps.update(t.r)
        ops = self.ops[stream]
        idx = len(ops)
        if dma:
            j = self.ndma[stream]
            self.ndma[stream] += 1
            tok = ("d", stream, j)
            if j >= self.KRING:
                deps.add(("d", stream, j - self.KRING))
        elif cc:
            tok = ("c", "cc", self.ncc)
            if self.ncc > 0:
                deps.add(("c", "cc", self.ncc - 1))
            self.ncc += 1
        else:
            tok = ("e", stream, idx)
        waits = self._filter(stream, deps)
        ops.append(dict(fn=fn, waits=waits, dma=dma, cc=cc, tok=tok, inc=False))
        for t in reads:
            if tok[0] == "e":
                t.r = [x for x in t.r if not (x[0] == "e" and x[1] == tok[1])]
            t.r.append(tok)
        for t in writes:
            t.w = [tok]
            t.r = []
        return tok

    def barrier(self, skip_cc=False):
        toks = []
        for s in self.STREAMS:
            ops = self.ops[s]
            for i in range(len(ops) - 1, -1, -1):
                if (not ops[i]["dma"]) and (not ops[i].get("cc")) and ops[i]["fn"] is not None:
                    toks.append(("e", s, i))
                    break
            n = self.ndma[s]
            for j in range(max(0, n - self.KRING), n):
                toks.append(("d", s, j))
        if self.ncc > 0 and not skip_cc:
            toks.append(("c", "cc", self.ncc - 1))
        for s in self.STREAMS:
            waits = self._filter(s, set(toks))
            self.ops[s].append(dict(fn=None, waits=waits, dma=False, tok=None, inc=False))
        for t in self.tiles:
            if skip_cc and any(x[0] == "c" for x in t.w):
                continue
            t.w = []
            t.r = []

    def emit(self, nc, es):
        K = self.KRING
        esem = {s: es.enter_context(nc.semaphore("e_" + s)) for s in self.STREAMS}
        csem = es.enter_context(nc.semaphore("ccsem"))
        dsem = {s: [es.enter_context(nc.semaphore("d_%s_%d" % (s, k))) for k in range(K)]
                for s in ("sp", "pool", "act")}
        for s in self.STREAMS:
            c = 0
            for op in self.ops[s]:
                if op["inc"]:
                    c += 1
                op["cnt"] = c
        ops_all = self.ops

        def run(stream, eng):
            for op in ops_all[stream]:
                for tok in op["waits"]:
                    if tok[0] == "e":
                        eng.wait_ge(esem[tok[1]], ops_all[tok[1]][tok[2]]["cnt"])
                    elif tok[0] == "c":
                        eng.wait_ge(csem, tok[2] + 1)
                    else:
                        eng.wait_ge(dsem[tok[1]][tok[2] % K], 16 * (tok[2] // K + 1))
                if op["fn"] is None:
                    continue
                ins = op["fn"](eng)
                if op["dma"]:
                    j = op["tok"][2]
                    ins.then_inc(dsem[stream][j % K], 16)
                elif op.get("cc"):
                    ins.then_inc(csem, 1)
                elif op["inc"]:
                    ins.then_inc(esem[stream], 1)

        with nc.Block() as block:
            @block.tensor
            def _(e):
                run("pe", e)

            @block.scalar
            def _(e):
                run("act", e)

            @block.vector
            def _(e):
                run("dve", e)

            @block.gpsimd
            def _(e):
                run("pool", e)

            @block.sync
            def _(e):
                run("sp", e)


class Arena:
    def __init__(self, nc, sc, nbytes):
        self.sc = sc
        self.nb = nbytes
        self.t = nc.alloc_sbuf_tensor("arena", [128, nbytes // 2], BF16)
        self.off = 0

    def mark(self):
        return self.off

    def release(self, m):
        self.off = m

    def alloc(self, name, free, dtype, tl=True):
        esz = 4 if dtype in (F32, I32) else 2
        n = 1
        for f in free:
            n *= f
        nby = (n * esz + 63) // 64 * 64
        assert self.off + nby <= self.nb, ("SBUF arena overflow", name, self.off, nby)
        v = self.t[:, self.off // 2:(self.off + n * esz) // 2]
        if dtype != BF16:
            v = v.bitcast(dtype)
        if len(free) == 2:
            v = v.rearrange("p (a b) -> p a b", b=free[1])
        elif len(free) == 3:
            v = v.rearrange("p (a b c) -> p a b c", b=free[1], c=free[2])
        self.off += nby
        return v, (self.sc.tile(name) if tl else None)


def MM(sc, out, lhsT, rhs, start, stop, R, W):
    sc.add("pe", lambda e: e.matmul(out, lhsT, rhs, start=start, stop=stop), R, W)


def TR(sc, out, in_, ident, R, W):
    sc.add("pe", lambda e: e.transpose(out, in_, ident), R, W)


def ACTF(sc, out, in_, func, R, W, bias=0.0, scale=1.0, accum=None):
    if accum is None:
        sc.add("act", lambda e: e.activation(out, in_, func, bias=bias, scale=scale), R, W)
    else:
        sc.add("act", lambda e: e.activation(out, in_, func, bias=bias, scale=scale,
                                             accum_out=accum), R, W)


def TS(sc, eng, out, in0, s1, s2, op0, op1, R, W):
    if op1 is None:
        sc.add(eng, lambda e: e.tensor_scalar(out, in0, s1, None, op0), R, W)
    else:
        sc.add(eng, lambda e: e.tensor_scalar(out, in0, s1, s2, op0, op1), R, W)


def TT(sc, eng, out, in0, in1, op, R, W):
    sc.add(eng, lambda e: e.tensor_tensor(out, in0, in1, op), R, W)


def STT(sc, eng, out, in0, scalar, in1, op0, op1, R, W):
    sc.add(eng, lambda e: e.scalar_tensor_tensor(out, in0, scalar, in1, op0, op1), R, W)


def CP(sc, eng, out, in_, R, W):
    if eng == "act":
        sc.add("act", lambda e: e.copy(out, in_), R, W)
    else:
        sc.add(eng, lambda e: e.tensor_copy(out, in_), R, W)


def RSTD(sc, out, ssq, n, eps_ap, R, W):
    sc.add("act", lambda e: e.activation(out, ssq, AF.Sqrt, bias=eps_ap, scale=1.0 / n), R, W)
    sc.add("dve", lambda e: e.reciprocal(out, out), W, W)


def DMA(sc, stream, out, in_, R, W):
    sc.add(stream, lambda e: e.dma_start(out=out, in_=in_), R, W, dma=True)


class Prog:
    def __init__(self, post_layer, a_layer, final, fused=False):
        self.post_layer = post_layer
        self.a_layer = a_layer
        self.final = final
        self.fused = fused
        self.nc = bass.Bass("TRN2", target_bir_lowering=False)
        self.sc = Sched()
        self.ext_in = []
        self.ext_out = []
        self.dr = {}

    def din(self, name, shape, dtype):
        h = self.nc.dram_tensor(name, list(shape), dtype, kind="ExternalInput")
        self.ext_in.append(name)
        self.dr[name] = h
        return h

    def dout(self, name, shape, dtype):
        h = self.nc.dram_tensor(name, list(shape), dtype, kind="ExternalOutput")
        self.ext_out.append(name)
        self.dr[name] = h
        return h

    def dint(self, name, shape, dtype):
        key = name.rstrip("0123456789") if self.fused else name
        if not hasattr(self, "_shared"):
            self._shared = {}
        if key not in self._shared:
            self._shared[key] = self.nc.dram_tensor(key, list(shape), dtype)
        h = self._shared[key]
        self.dr[name] = h
        return h

    def build(self):
        nc, sc = self.nc, self.sc
        self.A = Arena(nc, sc, 210500)
        A = self.A
        self.ps2 = []
        self.pst = []
        for b in range(4):
            self.ps2.append(nc.alloc_psum_tensor("psd%d" % b, [128, 1024], F32))
        for b in range(8):
            self.pst.append(sc.tile("ps%d" % b))
        ident_d = self.din("ident", [128, 128], F32)
        self.identf, tl_if = A.alloc("identf", (128,), F32)
        self.ident, self.tl_id = A.alloc("ident", (128,), BF16)
        DMA(sc, "sp", self.identf, ident_d[:, :], [], [tl_if])
        CP(sc, "dve", self.ident, self.identf, [tl_if], [self.tl_id])
        self.epsb, self.tl_eps = A.alloc("epsb", (1,), F32)
        epsb = self.epsb
        sc.add("dve", lambda e: e.memset(epsb, EPS), [], [self.tl_eps])
        self.base_mark = A.mark()

        import os
        self.onel = bool(os.environ.get("ONEL"))
        if self.fused and self.onel:
            self.stage_a(0)
            self.exchange(0)
            self.stage_post(0)
        elif self.fused:
            self.stage_a(0)
            self.exchange(0)
            self.stage_post(0)
            self.stage_a(1)
            self.exchange(1)
            self.stage_post(1)
        else:
            if self.post_layer is not None:
                self.stage_post(self.post_layer)
            if self.a_layer is not None:
                self.stage_a(self.a_layer)
        sc.barrier()
        with ExitStack() as es:
            sc.emit(nc, es)
        return nc

    def psb(self, b, dtype=F32):
        v = self.ps2[b // 2][:, (b % 2) * 512:(b % 2 + 1) * 512]
        if dtype == BF16:
            v = v.bitcast(BF16)
        return v

    def psd(self, k):
        return self.ps2[k][:, :]

    def setup_stage_ring(self, ncols=1440):
        A = self.A
        self.wst = [A.alloc("wst%d" % i, (ncols,), F32) for i in range(3)]
        self.wst_i = 0

    def load_cast(self, dst, dst_tl, src, ncols, scale=None, eng=None, scale_tl=None):
        sc = self.sc
        sR = [scale_tl] if scale_tl is not None else []
        stg, stl = self.wst[self.wst_i % 3]
        if eng is None:
            eng = ("pool", "dve", "act")[self.wst_i % 3]
        self.wst_i += 1
        sv = stg[:, 0:ncols]
        DMA(sc, "sp", sv, src, [], [stl])
        if scale is None:
            CP(sc, eng, dst, sv, [stl], [dst_tl])
        elif eng == "act":
            ACTF(sc, dst, sv, AF.Copy, [stl] + sR, [dst_tl], scale=scale)
        else:
            TS(sc, eng, dst, sv, scale, None, ALU.mult, None, [stl] + sR, [dst_tl])

    def load_f32(self, dst, dst_tl, src):
        DMA(self.sc, "sp", dst, src, [], [dst_tl])

    def stage_a(self, l):
        nc, sc, A = self.nc, self.sc, self.A
        sc.barrier()
        A.release(self.base_mark)
        sfx = "_a%d" % l
        if self.fused:
            xin = self.din("x_in" + sfx, [T, D], F32) if l == 0 else self.dr["xout_p0"]
        elif self.post_layer is not None:
            xin = self.dr["xout_p%d" % self.post_layer]
        else:
            xin = self.din("x_in" + sfx, [T, D], F32)
        mk_out = self.dint if self.fused else self.dout
        latT = mk_out("latT" + sfx, [288, T], BF16)
        halo = mk_out("halo" + sfx, [256, 16], F32)
        QT = mk_out("QT" + sfx, [H * 96, T], BF16)
        ppT = mk_out("ppT" + sfx, [256, T], F32)
        cT = mk_out("cT" + sfx, [256, T], BF16)
        pos_d = self.din("pos" + sfx, [128, NT], I32)
        invf_d = self.din("invf" + sfx, [128, 16], F32)
        w_in_d = self.din("w_in" + sfx, [128, 8, 1440], F32)
        mixg_d = self.din("mixg" + sfx, [128, 8], F32)
        w_uq_d = self.din("w_uq" + sfx, [128, 3, 768], F32)
        qg_d = self.din("qg" + sfx, [128, 3], F32)
        kvg_d = self.din("kvg" + sfx, [128, 256], F32)
        sgg_d = self.din("sgg" + sfx, [128, 256], F32)
        wsT_d = self.din("wsT" + sfx, [128, 4, 128], F32)
        bsT_d = self.din("bsT" + sfx, [128, 4], F32)

        self.setup_stage_ring()
        w_in, tl_win = A.alloc("w_in", (8, 1440), BF16)
        w_uq, tl_wuq = A.alloc("w_uq", (3, 768), BF16)
        wsT, tl_wsT = A.alloc("wsT", (4, 128), BF16)
        mixg, tl_mixg = A.alloc("mixg", (8,), F32)
        qg, tl_qg = A.alloc("qg", (3,), F32)
        kvg, tl_kvg = A.alloc("kvg", (256,), F32)
        sgg, tl_sgg = A.alloc("sgg", (256,), F32)
        bsT, tl_bsT = A.alloc("bsT", (4,), F32)
        self.load_f32(mixg, tl_mixg, mixg_d[:, :])
        self.load_f32(qg, tl_qg, qg_d[:, :])
        self.load_f32(kvg, tl_kvg, kvg_d[:, :])
        self.load_f32(sgg, tl_sgg, sgg_d[:, :])
        self.load_f32(bsT, tl_bsT, bsT_d[:, :])
        for k in range(8):
            self.load_cast(w_in[:, k, :], tl_win, w_in_d[:, k, :], 1440, scale=mixg[:, k:k + 1],
                           scale_tl=tl_mixg)
        for k in range(3):
            self.load_cast(w_uq[:, k, :], tl_wuq, w_uq_d[:, k, :], 768, scale=qg[:, k:k + 1],
                           scale_tl=tl_qg)
        self.load_cast(wsT.rearrange("p a b -> p (a b)"), tl_wsT,
                       wsT_d[:, :, :].rearrange("p a b -> p (a b)"), 512)

        posi, tl_posi = A.alloc("posi", (NT,), I32)
        posf, tl_posf = A.alloc("posf", (NT,), F32)
        invf, tl_invf = A.alloc("invf", (16,), F32)
        ang, tl_ang = A.alloc("ang", (NT, 16), F32)
        ang2, tl_ang2 = A.alloc("ang2", (NT, 16), F32)
        cost, tl_cos = A.alloc("cost", (NT, 16), F32)
        sint, tl_sin = A.alloc("sint", (NT, 16), F32)
        angi, tl_angi = A.alloc("angi", (NT, 16), I32)
        DMA(sc, "sp", posi, pos_d[:, :], [], [tl_posi])
        DMA(sc, "sp", invf, invf_d[:, :], [], [tl_invf])
        CP(sc, "dve", posf, posi, [tl_posi], [tl_posf])
        TT(sc, "dve", ang, posf.unsqueeze(2).to_broadcast([128, NT, 16]),
           invf.unsqueeze(1).to_broadcast([128, NT, 16]), ALU.mult, [tl_posf, tl_invf], [tl_ang])
        for (dst, dtl, shift) in ((sint, tl_sin, 0.0), (cost, tl_cos, 0.5 * PI)):
            TS(sc, "dve", ang2, ang, shift, 1.0 / (2 * PI), ALU.add, ALU.mult, [tl_ang], [tl_ang2])
            CP(sc, "dve", angi, ang2, [tl_ang2], [tl_angi])
            CP(sc, "dve", ang2, angi, [tl_angi], [tl_ang2])
            STT(sc, "dve", ang2, ang2, -2 * PI, ang, ALU.mult, ALU.add, [tl_ang2, tl_ang], [tl_ang2])
            if shift != 0.0:
                TS(sc, "dve", ang2, ang2, shift, None, ALU.add, None, [tl_ang2], [tl_ang2])
            TS(sc, "dve", ang2, ang2, PI, -PI, ALU.min, ALU.max, [tl_ang2], [tl_ang2])
            ACTF(sc, dst, ang2, AF.Sin, [tl_ang2], [dtl])
        xt = [A.alloc("xt%d" % i, (D,), F32) for i in range(2)]
        junk, tl_junk = A.alloc("junk", (D,), BF16)
        hn = [A.alloc("hn%d" % i, (D,), BF16) for i in range(2)]
        hT = [A.alloc("hT%d" % i, (8, 128), BF16) for i in range(2)]
        st = [A.alloc("st%d" % i, (16,), F32) for i in range(2)]
        cq = [A.alloc("cq%d" % i, (384,), BF16) for i in range(2)]
        cqT = [A.alloc("cqT%d" % i, (3, 128), BF16) for i in range(2)]
        qsb = [A.alloc("qsb%d" % i, (8, 96), BF16) for i in range(2)]
        rm1 = [A.alloc("rm1%d" % i, (8, 2, 16), F32) for i in range(2)]
        rm2 = [A.alloc("rm2%d" % i, (8, 2, 16), F32) for i in range(2)]
        km1 = [A.alloc("km1%d" % i, (2, 16), F32) for i in range(2)]
        km2 = [A.alloc("km2%d" % i, (2, 16), F32) for i in range(2)]
        lat = [A.alloc("lat%d" % i, (288,), BF16) for i in range(2)]
        latf = [A.alloc("latf%d" % i, (256,), F32) for i in range(2)]
        z = [A.alloc("z%d" % i, (512,), F32) for i in range(2)]
        vsq = [A.alloc("vsq%d" % i, (256,), F32) for i in range(2)]
        vtmp = [A.alloc("vtmp%d" % i, (256,), F32) for i in range(2)]
        vn = [A.alloc("vn%d" % i, (256,), BF16) for i in range(2)]
        cc = [A.alloc("cc%d" % i, (256,), BF16) for i in range(2)]
        QTst = [A.alloc("QTst%d" % i, (8, 512), BF16) for i in range(2)]
        latTst = [A.alloc("latTst%d" % i, (3, 512), BF16) for i in range(2)]
        ppTst = [A.alloc("ppTst%d" % i, (2, 512), F32) for i in range(2)]
        cTst = [A.alloc("cTst%d" % i, (2, 512), BF16) for i in range(2)]

        ident, tl_id = self.ident, self.tl_id
        B_TPA, B_TPB, B_PA, B_PB, B_PC, B_PP, B_Q0, B_Q1 = range(8)
        pst = self.pst
        qps = self.psb(B_Q0)
        qps2 = self.psb(B_Q1)

        for i in range(NT):
            par = i % 2
            g4, j4 = i // 4, i % 4
            gp = g4 % 2
            x_ap, x_tl = xt[par]
            hn_ap, hn_tl = hn[par]
            hT_ap, hT_tl = hT[par]
            st_ap, st_tl = st[par]
            DMA(sc, "sp", x_ap, xin[i * 128:(i + 1) * 128, :], [], [x_tl])
            ACTF(sc, junk, x_ap, AF.Square, [x_tl], [tl_junk, st_tl], accum=st_ap[:, 0:1])
            RSTD(sc, st_ap[:, 1:2], st_ap[:, 0:1], D, self.epsb[:, 0:1], [st_tl, self.tl_eps],
                 [st_tl])
            TS(sc, "dve", hn_ap, x_ap, st_ap[:, 1:2], None, ALU.mult, None,
               [x_tl, st_tl], [hn_tl])
            tpa = self.psb(B_TPA, BF16)
            for k in range(8):
                TR(sc, tpa[:, k * 128:(k + 1) * 128], hn_ap[:, k * 128:(k + 1) * 128], ident,
                   [hn_tl, tl_id], [pst[B_TPA]])
            CP(sc, "act", hT_ap.rearrange("p a b -> p (a b)"), tpa, [pst[B_TPA]], [hT_tl])
            pA = self.psb(B_PA)
            pB = self.psb(B_PB)
            pC = self.psb(B_PC)
            pP = self.psb(B_PP)
            for k in range(8):
                MM(sc, pA[:, 0:416], hT_ap[:, k, :], w_in[:, k, 0:416], k == 0, k == 7,
                   [hT_tl, tl_win], [pst[B_PA]])
            for k in range(8):
                MM(sc, pB[:, 0:256], hT_ap[:, k, :], w_in[:, k, 416:672], k == 0, k == 7,
                   [hT_tl, tl_win], [pst[B_PB]])
            for k in range(8):
                MM(sc, pC[:, 0:512], hT_ap[:, k, :], w_in[:, k, 672:1184], k == 0, k == 7,
                   [hT_tl, tl_win], [pst[B_PC]])
            for c in range(2):
                for k in range(8):
                    MM(sc, pP[:, c * 128:(c + 1) * 128],
                       w_in[:, k, 1184 + c * 128:1184 + (c + 1) * 128], hT_ap[:, k, :],
                       k == 0, k == 7, [hT_tl, tl_win], [pst[B_PP]])
            pp_ap, pp_tl = ppTst[gp]
            CP(sc, "dve", pp_ap[:, :, j4 * 128:(j4 + 1) * 128],
               pP[:, 0:256].rearrange("p (c t) -> p c t", c=2), [pst[B_PP]], [pp_tl])
            cq_ap, cq_tl = cq[par]
            ACTF(sc, junk[:, 0:384], pA[:, 0:384], AF.Square, [pst[B_PA]], [tl_junk, st_tl],
                 accum=st_ap[:, 2:3])
            RSTD(sc, st_ap[:, 3:4], st_ap[:, 2:3], 384, self.epsb[:, 0:1], [st_tl, self.tl_eps],
                 [st_tl])
            TS(sc, "dve", cq_ap, pA[:, 0:384], st_ap[:, 3:4], None, ALU.mult,
               None, [pst[B_PA], st_tl], [cq_tl])
            tpb = self.psb(B_TPB, BF16)
            for k in range(3):
                TR(sc, tpb[:, k * 128:(k + 1) * 128], cq_ap[:, k * 128:(k + 1) * 128], ident,
                   [cq_tl, tl_id], [pst[B_TPB]])
            cqT_ap, cqT_tl = cqT[par]
            CP(sc, "act", cqT_ap.rearrange("p a b -> p (a b)"), tpb[:, 0:384], [pst[B_TPB]],
               [cqT_tl])
            lat_ap, lat_tl = lat[par]
            latf_ap, latf_tl = latf[par]
            ACTF(sc, junk[:, 0:256], pB[:, 0:256], AF.Square, [pst[B_PB]], [tl_junk, st_tl],
                 accum=st_ap[:, 4:5])
            RSTD(sc, st_ap[:, 5:6], st_ap[:, 4:5], 256, self.epsb[:, 0:1], [st_tl, self.tl_eps],
                 [st_tl])
            TS(sc, "dve", latf_ap, pB[:, 0:256], st_ap[:, 5:6], None, ALU.mult, None,
               [pst[B_PB], st_tl], [latf_tl])
            TT(sc, "dve", lat_ap[:, 0:256], latf_ap, kvg, ALU.mult, [latf_tl, tl_kvg], [lat_tl])
            cos_i = cost[:, i, :]
            sin_i = sint[:, i, :]
            k1_ap, k1_tl = km1[par]
            k2_ap, k2_tl = km2[par]
            krv = pA[:, 384:416].rearrange("p (a b) -> p a b", a=2)
            TT(sc, "dve", k1_ap, krv, cos_i.unsqueeze(1).to_broadcast([128, 2, 16]), ALU.mult,
               [pst[B_PA], tl_cos], [k1_tl])
            TT(sc, "dve", k2_ap, krv, sin_i.unsqueeze(1).to_broadcast([128, 2, 16]), ALU.mult,
               [pst[B_PA], tl_sin], [k2_tl])
            TT(sc, "dve", lat_ap[:, 256:272], k1_ap[:, 0, :], k2_ap[:, 1, :], ALU.subtract,
               [k1_tl, k2_tl], [lat_tl])
            TT(sc, "dve", lat_ap[:, 272:288], k2_ap[:, 0, :], k1_ap[:, 1, :], ALU.add,
               [k1_tl, k2_tl], [lat_tl])
            for k in range(3):
                MM(sc, qps[:, 0:512], cqT_ap[:, k, :], w_uq[:, k, 0:512], k == 0, k == 2,
                   [cqT_tl, tl_wuq], [pst[B_Q0]])
            for k in range(3):
                MM(sc, qps2[:, 0:256], cqT_ap[:, k, :], w_uq[:, k, 512:768], k == 0, k == 2,
                   [cqT_tl, tl_wuq], [pst[B_Q1]])
            q_ap, q_tl = qsb[par]
            r1_ap, r1_tl = rm1[par]
            r2_ap, r2_tl = rm2[par]
            def rope_batch(src, srct, h0, nh):
                sv = src.rearrange("p (h d) -> p h d", h=nh)
                CP(sc, "act", q_ap[:, h0:h0 + nh, 0:64], sv[:, :, 0:64], srct, [q_tl])
                rv = sv[:, :, 64:96].rearrange("p h (a b) -> p h a b", a=2)
                cb = cos_i.unsqueeze(1).unsqueeze(1).to_broadcast([128, nh, 2, 16])
                sb_ = sin_i.unsqueeze(1).unsqueeze(1).to_broadcast([128, nh, 2, 16])
                TT(sc, "dve", r1_ap[:, h0:h0 + nh, :, :], rv, cb, ALU.mult, srct + [tl_cos], [r1_tl])
                TT(sc, "dve", r2_ap[:, h0:h0 + nh, :, :], rv, sb_, ALU.mult, srct + [tl_sin], [r2_tl])

            rope_batch(qps[:, 0:480], [pst[B_Q0]], 0, 5)
            rope_batch(qps2[:, 64:256], [pst[B_Q1]], 6, 2)
            CP(sc, "act", q_ap[:, 5, 0:32], qps[:, 480:512], [pst[B_Q0]], [q_tl])
            CP(sc, "act", q_ap[:, 5, 32:64], qps2[:, 0:32], [pst[B_Q1]], [q_tl])
            rv5 = qps2[:, 32:64].rearrange("p (a b) -> p a b", a=2)
            TT(sc, "dve", r1_ap[:, 5, :, :], rv5, cos_i.unsqueeze(1).to_broadcast([128, 2, 16]),
               ALU.mult, [pst[B_Q1], tl_cos], [r1_tl])
            TT(sc, "dve", r2_ap[:, 5, :, :], rv5, sin_i.unsqueeze(1).to_broadcast([128, 2, 16]),
               ALU.mult, [pst[B_Q1], tl_sin], [r2_tl])
            TT(sc, "dve", q_ap[:, :, 64:80], r1_ap[:, :, 0, :], r2_ap[:, :, 1, :], ALU.subtract,
               [r1_tl, r2_tl], [q_tl])
            TT(sc, "dve", q_ap[:, :, 80:96], r2_ap[:, :, 0, :], r1_ap[:, :, 1, :], ALU.add,
               [r1_tl, r2_tl], [q_tl])
            for h in range(H):
                TR(sc, tpb[0:96, h * 128:(h + 1) * 128], q_ap[:, h, :], ident, [q_tl, tl_id],
                   [pst[B_TPB]])
            qt_ap, qt_tl = QTst[gp]
            CP(sc, "act", qt_ap[0:96, :, j4 * 128:(j4 + 1) * 128],
               tpb[0:96, :].rearrange("p (h t) -> p h t", h=8), [pst[B_TPB]], [qt_tl])
            lt_ap, lt_tl = latTst[gp]
            TR(sc, tpa[:, 0:128], lat_ap[:, 0:128], ident, [lat_tl, tl_id], [pst[B_TPA]])
            TR(sc, tpa[:, 128:256], lat_ap[:, 128:256], ident, [lat_tl, tl_id], [pst[B_TPA]])
            TR(sc, tpa[0:32, 256:384], lat_ap[:, 256:288], ident, [lat_tl, tl_id], [pst[B_TPA]])
            CP(sc, "act", lt_ap[:, 0:2, j4 * 128:(j4 + 1) * 128],
               tpa[:, 0:256].rearrange("p (c t) -> p c t", c=2), [pst[B_TPA]], [lt_tl])
            CP(sc, "act", lt_ap[0:32, 2, j4 * 128:(j4 + 1) * 128], tpa[0:32, 256:384],
               [pst[B_TPA]], [lt_tl])
            z_ap, z_tl = z[par]
            ACTF(sc, z_ap, pC[:, 0:512], AF.Gelu_apprx_tanh, [pst[B_PC]], [z_tl])
            vs_ap, vs_tl = vsq[par]
            vt_ap, vt_tl = vtmp[par]
            vn_ap, vn_tl = vn[par]
            TT(sc, "dve", vs_ap, z_ap[:, 256:512], z_ap[:, 256:512], ALU.mult, [z_tl], [vs_tl])
            sc.add("dve", (lambda o, a: (lambda e: e.reduce_sum(o, a, AX.X)))(
                st_ap[:, 8:12], vs_ap.rearrange("p (g d) -> p g d", g=4)), [vs_tl], [st_tl])
            RSTD(sc, st_ap[:, 12:16], st_ap[:, 8:12], 64, self.epsb[:, 0:1], [st_tl, self.tl_eps],
                 [st_tl])
            TT(sc, "dve", vt_ap.rearrange("p (g d) -> p g d", g=4),
               z_ap[:, 256:512].rearrange("p (g d) -> p g d", g=4),
               st_ap[:, 12:16].unsqueeze(2).to_broadcast([128, 4, 64]), ALU.mult,
               [z_tl, st_tl], [vt_tl])
            TT(sc, "dve", vn_ap, vt_ap, sgg, ALU.mult, [vt_tl, tl_sgg], [vn_tl])
            for g in range(4):
                MM(sc, pP[:, 256 + g * 64:256 + (g + 1) * 64], wsT[:, g, :],
                   vn_ap[:, g * 64:(g + 1) * 64], True, True, [vn_tl, tl_wsT], [pst[B_PP]])
            c_ap, c_tl = cc[par]
            for g in range(4):
                STT(sc, "dve", c_ap[:, g * 64:(g + 1) * 64], pP[:, 256 + g * 64:256 + (g + 1) * 64],
                    bsT[:, g:g + 1], z_ap[:, g * 64:(g + 1) * 64], ALU.add, ALU.mult,
                    [pst[B_PP], tl_bsT, z_tl], [c_tl])
            TR(sc, tpb[:, 0:128], c_ap[:, 0:128], ident, [c_tl, tl_id], [pst[B_TPB]])
            TR(sc, tpb[:, 128:256], c_ap[:, 128:256], ident, [c_tl, tl_id], [pst[B_TPB]])
            ct_ap, ct_tl = cTst[gp]
            CP(sc, "act", ct_ap[:, :, j4 * 128:(j4 + 1) * 128],
               tpb[:, 0:256].rearrange("p (c t) -> p c t", c=2), [pst[B_TPB]], [ct_tl])
            if j4 == 3:
                ts_ = slice(g4 * 512, (g4 + 1) * 512)
                DMA(sc, "pool", QT[:, ts_].rearrange("(h r) t -> r h t", h=8), qt_ap[0:96, :, :],
                    [qt_tl], [])
                DMA(sc, "pool", latT[0:256, ts_].rearrange("(c p) t -> p c t", c=2),
                    lt_ap[:, 0:2, :], [lt_tl], [])
                DMA(sc, "pool", latT[256:288, ts_], lt_ap[0:32, 2, :], [lt_tl], [])
                DMA(sc, "pool", ppT[:, ts_].rearrange("(c p) t -> p c t", c=2), pp_ap, [pp_tl], [])
                DMA(sc, "pool", cT[:, ts_].rearrange("(c p) t -> p c t", c=2), ct_ap, [ct_tl], [])
                if g4 == 0:
                    DMA(sc, "pool", halo[:, 0:8].rearrange("(c p) t -> p c t", c=2),
                        pp_ap[:, :, 0:8], [pp_tl], [])
                if g4 == 7:
                    DMA(sc, "pool", halo[:, 8:16].rearrange("(c p) t -> p c t", c=2),
                        pp_ap[:, :, 504:512], [pp_tl], [])

    def exchange(self, l):
        sc = self.sc
        sc.barrier()
        latT = self.dr["latT_a%d" % l]
        halo = self.dr["halo_a%d" % l]
        g1 = self.dint("g1_%d" % l, [2 * 128, T], BF16)
        g2 = self.dint("g2_%d" % l, [2 * 160, T], BF16)
        gh = self.dint("gh_%d" % l, [2 * 256, 16], F32)
        tl_h = sc.tile("gath_h%d" % l)
        tl_1 = sc.tile("gath_1_%d" % l)
        tl_2 = sc.tile("gath_2_%d" % l)
        self.tl_gh, self.tl_g1, self.tl_g2 = [tl_h], [tl_1], [tl_2]
        pairs = [[0, 1], [2, 3], [4, 5], [6, 7]]
        for (src, dst, tl) in ((halo[:, :], gh, tl_h), (latT[0:128, :], g1, tl_1),
                               (latT[128:288, :], g2, tl_2)):
            sc.add("pool", (lambda a, b: (lambda e: e.collective_compute(
                "AllGather", ALU.bypass, replica_groups=pairs, ins=[a], outs=[b[:, :]])))(src, dst),
                [], [tl], cc=True)

    def stage_post(self, l):
        nc, sc, A = self.nc, self.sc, self.A
        sfx = "_p%d" % l
        pst = self.pst
        ident, tl_id = self.ident, self.tl_id
        if self.fused:
            final = (l == 1) or self.onel
            xin = self.dr["x_in_a0"] if l == 0 else self.dr["xout_p0"]
            g1, g2, gh = self.dr["g1_%d" % l], self.dr["g2_%d" % l], self.dr["gh_%d" % l]
            tl_gh, tl_g1, tl_g2 = self.tl_gh, self.tl_g1, self.tl_g2
            ckv_src = lambda r, c: (g1[r * 128:(r + 1) * 128, :] if c == 0
                                    else g2[r * 160:r * 160 + 128, :])
            kr_src = lambda r: g2[r * 160 + 128:r * 160 + 160, :]
            halo_src = lambda r, a, b: gh[r * 256:(r + 1) * 256, a:b]
            QT = self.dr["QT_a%d" % l]
            ppT = self.dr["ppT_a%d" % l]
            cTd = self.dr["cT_a%d" % l]
            xmid_ = self.dint("xmid" + sfx, [T, D], F32)
            if final:
                xout = self.dout("y_out", [T, D], F32)
            else:
                xout = xmid_
                self.dr["xout" + sfx] = xout
        else:
            final = self.final
            tl_gh, tl_g1, tl_g2 = [], [], []
            xin = self.din("x_in" + sfx, [T, D], F32)
            latp = self.din("latp" + sfx, [2 * 288, T], BF16)
            halop = self.din("halop" + sfx, [2 * 256, 16], F32)
            ckv_src = lambda r, c: latp[r * 288 + c * 128:r * 288 + (c + 1) * 128, :]
            kr_src = lambda r: latp[r * 288 + 256:r * 288 + 288, :]
            halo_src = lambda r, a, b: halop[r * 256:(r + 1) * 256, a:b]
            QT = self.din("QT" + sfx, [H * 96, T], BF16)
            ppT = self.din("ppT" + sfx, [256, T], F32)
            cTd = self.din("cT" + sfx, [256, T], BF16)
            if final:
                xout = self.dout("y_out", [T, D], F32)
            else:
                xout = self.dout("xout" + sfx, [T, D], F32)
        xmid = self.dint("xmid" + sfx, [T, D], F32)
        hmask_d = self.din("hmask" + sfx, [128, 2], F32)
        edge_d = self.din("edge" + sfx, [128, 2, 16], F32)
        wkn_d = self.din("wkn" + sfx, [128, 2, 512], F32)
        wv_d = self.din("wv" + sfx, [128, 2, 512], F32)
        wpb_d = self.din("wpb" + sfx, [128, 2, 128], F32)
        psc_d = self.din("psc" + sfx, [128, 2], F32)
        w_o_d = self.din("w_o" + sfx, [128, 8, 1024], F32)
        ffng_d = self.din("ffng" + sfx, [128, 8], F32)
        wg_d = self.din("w_gate" + sfx, [128, 8, DFF], F32)
        wu_d = self.din("w_up" + sfx, [128, 8, DFF], F32)
        wd_d = self.din("w_down" + sfx, [128, NFF, 1024], F32)
        if final:
            fin_d = self.din("fing", [128, D], F32)

        sc.barrier(skip_cc=self.fused)
        A.release(self.base_mark)
        aT, tl_aT = A.alloc("aT", (4, T), BF16, tl=False)
        bT, tl_bT = A.alloc("bT", (2, T), BF16)
        aT_tl = [[sc.tile("aT%d_%d" % (h, qb)) for qb in range(8)] for h in range(H)]
        mix_mark = A.mark()

        self.setup_stage_ring(512)
        wpb, tl_wpb = A.alloc("wpb", (2, 128), BF16)
        psc, tl_psc = A.alloc("psc", (2,), F32)
        hmask, tl_hm = A.alloc("hmask", (2,), F32)
        edge, tl_edge = A.alloc("edge", (2, 16), F32)
        self.load_cast(wpb.rearrange("p a b -> p (a b)"), tl_wpb,
                       wpb_d[:, :, :].rearrange("p a b -> p (a b)"), 256, eng="dve")
        self.load_f32(psc, tl_psc, psc_d[:, :])
        self.load_f32(hmask, tl_hm, hmask_d[:, :])
        self.load_f32(edge, tl_edge, edge_d[:, :, :])
        W = T + 16
        ppx, tl_ppx = A.alloc("ppx", (2, W), F32)
        s1, tl_s1 = A.alloc("pl_s1", (W,), F32)
        s2, tl_s2 = A.alloc("pl_s2", (W,), F32)
        dT, tl_dT = A.alloc("dT", (2, T), BF16)
        etmp, tl_et = A.alloc("etmp", (16,), F32)
        DMA(sc, "sp", ppx[:, :, 8:8 + T], ppT[:, :].rearrange("(c p) t -> p c t", c=2), [],
            [tl_ppx])
        hl, tl_hl = A.alloc("hl", (2, 16), F32)
        DMA(sc, "sp", hl[:, :, 0:8], halo_src(0, 8, 16).rearrange("(c p) t -> p c t", c=2), tl_gh,
            [tl_hl])
        DMA(sc, "sp", hl[:, :, 8:16], halo_src(1, 0, 8).rearrange("(c p) t -> p c t", c=2), tl_gh,
            [tl_hl])
        TS(sc, "dve", ppx[:, :, 0:8], hl[:, :, 0:8], hmask[:, 0:1], None, ALU.mult, None,
           [tl_hl, tl_hm], [tl_ppx])
        TS(sc, "dve", ppx[:, :, 8 + T:16 + T], hl[:, :, 8:16], hmask[:, 1:2], None, ALU.mult, None,
           [tl_hl, tl_hm], [tl_ppx])
        for c in range(2):
            p_c = ppx[:, c, :]
            levels_needed = (1, 2) if c == 0 else (3, 4)
            TT(sc, "dve", s1[:, 0:W - 1], p_c[:, 0:W - 1], p_c[:, 1:W], ALU.add, [tl_ppx], [tl_s1])
            cur, cur_tl, oth, oth_tl = s1, tl_s1, s2, tl_s2
            lvl = 1
            ln = W - 1
            for gi in range(2):
                g = c * 2 + gi
                want = levels_needed[gi]
                while lvl < want:
                    sh = 1 << lvl
                    TT(sc, "dve", oth[:, 0:ln - sh], cur[:, 0:ln - sh], cur[:, sh:ln], ALU.add,
                       [cur_tl], [oth_tl])
                    ln -= sh
                    cur, cur_tl, oth, oth_tl = oth, oth_tl, cur, cur_tl
                    lvl += 1
                w = 2 << g
                left = w // 2
                pr = slice(gi * 64, gi * 64 + 64)
                o = 8 - left
                STT(sc, "dve", dT[pr, c, :], cur[pr, o:o + T], 1.0 / w, p_c[pr, 8:8 + T],
                    ALU.mult, ALU.subtract, [cur_tl, tl_ppx], [tl_dT])
                for (tc0, ec0) in ((0, 0), (T - 8, 8)):
                    TT(sc, "dve", etmp[pr, 0:8], cur[pr, o + tc0:o + tc0 + 8],
                       edge[pr, c, ec0:ec0 + 8], ALU.mult, [cur_tl, tl_edge], [tl_et])
                    TT(sc, "dve", dT[pr, c, tc0:tc0 + 8], etmp[pr, 0:8],
                       p_c[pr, 8 + tc0:16 + tc0], ALU.subtract, [tl_et, tl_ppx], [tl_dT])
        for c in range(2):
            for tb in range(8):
                bk = 2 + (tb % 2)
                MM(sc, self.psb(bk)[:, 0:512], wpb[:, c, :], dT[:, c, tb * 512:(tb + 1) * 512],
                   True, True, [tl_wpb, tl_dT], [pst[bk]])
                TS(sc, "dve", bT[:, c, tb * 512:(tb + 1) * 512], self.psb(bk)[:, 0:512],
                   psc[:, c:c + 1], None, ALU.mult, None, [pst[bk], tl_psc], [tl_bT])

        sc.barrier()
        A.release(mix_mark)
        self.setup_stage_ring(512)
        wkn, tl_wkn = A.alloc("wkn", (2, 512), BF16)
        wv, tl_wv = A.alloc("wv", (2, 512), BF16)
        for k in range(2):
            self.load_cast(wkn[:, k, :], tl_wkn, wkn_d[:, k, :], 512, eng="dve")
            self.load_cast(wv[:, k, :], tl_wv, wv_d[:, k, :], 512, eng="pool")
        ones_b, tl_ones = A.alloc("ones_b", (128,), BF16)
        sc.add("dve", lambda e: e.memset(ones_b, 1.0), [], [tl_ones])
        rhi, tl_rhi = A.alloc("rhi", (512,), BF16)
        rlo, tl_rlo = A.alloc("rlo", (512,), BF16)
        lt = [A.alloc("lat_r%d" % r, (2, T), BF16) for r in range(2)]
        for r in range(2):
            for c in range(2):
                DMA(sc, "sp", lt[r][0][:, c, :], ckv_src(r, c), (tl_g1 if c == 0 else tl_g2), [lt[r][1]])
        KT = [A.alloc("KT%d" % i, (S,), BF16) for i in range(2)]
        for i in range(2):
            for r in range(2):
                DMA(sc, "sp", KT[i][0][64:96, r * T:(r + 1) * T], kr_src(r), tl_g2, [KT[i][1]])
        V = [A.alloc("V%d" % i, (64, 193), BF16) for i in range(2)]
        for i in range(2):
            v_ap, v_tl = V[i]
            sc.add("pool", (lambda a: (lambda e: e.memset(a, 0.0)))(v_ap), [], [v_tl])
            sc.add("pool", (lambda a: (lambda e: e.memset(a, 1.0)))(v_ap[:, :, 64:65]), [], [v_tl])
            sc.add("pool", (lambda a: (lambda e: e.memset(a, 1.0)))(v_ap[:, :, 129:130]), [], [v_tl])
        QTh = [A.alloc("QTh%d" % i, (T,), BF16) for i in range(2)]
        P = [A.alloc("P%d" % i, (1024,), BF16) for i in range(3)]
        rcp, tl_rcp = A.alloc("rcp", (512,), F32)
        osb = [A.alloc("osb%d" % i, (512,), F32) for i in range(2)]
        an = [A.alloc("an%d" % i, (512,), BF16) for i in range(2)]
        B_ACC = (4, 5)
        B_BC = 6
        B_KV = 7

        def kv_items(h):
            items = []
            kt_ap, kt_tl = KT[h % 2]
            pair = h // 2
            v_ap, v_tl = V[pair % 2]
            for r in range(2):
                for tb in range(8):
                    def k_item(r=r, tb=tb):
                        bk = B_KV
                        for k in range(2):
                            MM(sc, self.psb(bk)[0:64, 0:512], wkn[:, k, h * 64:(h + 1) * 64],
                               lt[r][0][:, k, tb * 512:(tb + 1) * 512], k == 0, k == 1,
                               [tl_wkn, lt[r][1]], [pst[bk]])
                        CP(sc, "dve", kt_ap[0:64, r * T + tb * 512:r * T + (tb + 1) * 512],
                           self.psb(bk)[0:64, 0:512], [pst[bk]], [kt_tl])
                    items.append(k_item)
            if h % 2 == 0:
                for kc4 in range(16):
                    def v_item(kc4=kc4):
                        bk = B_KV
                        for q4 in range(4):
                            kc = kc4 * 4 + q4
                            r, tt = kc // 32, (kc % 32) * 128
                            for k in range(2):
                                MM(sc, self.psb(bk)[:, q4 * 128:(q4 + 1) * 128],
                                   lt[r][0][:, k, tt:tt + 128],
                                   wv[:, k, pair * 128:(pair + 1) * 128],
                                   k == 0, k == 1, [lt[r][1], tl_wv], [pst[bk]])
                        src = self.psb(bk)[:, 0:512].rearrange("p (c e d) -> p c e d", c=4, e=2)
                        for e_ in range(2):
                            CP(sc, "dve", v_ap[:, kc4 * 4:(kc4 + 1) * 4, e_ * 65:e_ * 65 + 64],
                               src[:, :, e_, :], [pst[bk]], [v_tl])
                    items.append(v_item)
            return items

        def finish_unit(h, qb, acc_b, unit):
            e = h % 2
            pair = h // 2
            acc = self.psb(acc_b)
            sc.add("dve", (lambda o, a: (lambda en: en.reciprocal(o, a)))(
                rcp[64:65, :], acc[64:65, 0:512]), [pst[acc_b]], [tl_rcp])
            CP(sc, "dve", rhi[64:65, :], rcp[64:65, :], [tl_rcp], [tl_rhi])
            TT(sc, "dve", rlo[64:65, :], rcp[64:65, :], rhi[64:65, :], ALU.subtract,
               [tl_rcp, tl_rhi], [tl_rlo])
            MM(sc, self.psb(B_BC)[0:64, 0:512], ones_b[64:65, 0:64], rhi[64:65, :], True, False,
               [tl_ones, tl_rhi], [pst[B_BC]])
            MM(sc, self.psb(B_BC)[0:64, 0:512], ones_b[64:65, 0:64], rlo[64:65, :], False, True,
               [tl_ones, tl_rlo], [pst[B_BC]])
            o_ap, o_tl = osb[unit % 2]
            n_ap, n_tl = an[unit % 2]
            CP(sc, "dve", o_ap[0:64, :], acc[0:64, 0:512], [pst[acc_b]], [o_tl])
            TT(sc, "dve", n_ap[0:64, :], o_ap[0:64, :], self.psb(B_BC)[0:64, 0:512], ALU.mult,
               [o_tl, pst[B_BC]], [n_tl])
            DMA(sc, "pool", aT[e * 64:(e + 1) * 64, pair, qb * 512:(qb + 1) * 512],
                n_ap[0:64, :], [n_tl], [aT_tl[h][qb]])

        DMA(sc, "sp", QTh[0][0][0:96, :], QT[0:96, :], [], [QTh[0][1]])
        for it in kv_items(0):
            it()
        steps = [(h, qb, kp) for h in range(H) for qb in range(8) for kp in range(32)]
        LA = 1
        nsteps = len(steps)
        bg = []
        for j in range(nsteps + LA):
            if j < nsteps:
                h, qb, kp = steps[j]
                if qb == 0 and kp == 0:
                    if h + 1 < H:
                        DMA(sc, "sp", QTh[(h + 1) % 2][0][0:96, :], QT[(h + 1) * 96:(h + 2) * 96, :],
                            [], [QTh[(h + 1) % 2][1]])
                        bg = kv_items(h + 1)
                    else:
                        bg = []
                kt_ap, kt_tl = KT[h % 2]
                q_ap, q_tl = QTh[h % 2]
                qs = q_ap[0:96, qb * 512:(qb + 1) * 512]
                sd = j % 2
                p_ap, p_tl = P[j % 3]
                for t in range(2):
                    kc = 2 * kp + t
                    bk = 2 * sd + t
                    MM(sc, self.psb(bk)[:, 0:512], kt_ap[0:96, kc * 128:(kc + 1) * 128], qs,
                       True, True, [kt_tl, q_tl], [pst[bk]])
                for t in range(2):
                    ACTF(sc, p_ap[:, t * 512:(t + 1) * 512], self.psb(2 * sd + t)[:, 0:512], AF.Exp,
                         [pst[2 * sd + t]], [p_tl], scale=SCALE)
            if j >= LA:
                jj = j - LA
                h2, qb2, kp2 = steps[jj]
                unit2 = h2 * 8 + qb2
                acc_b = B_ACC[unit2 % 2]
                acc = self.psb(acc_b)
                pair2 = h2 // 2
                v_ap, v_tl = V[pair2 % 2]
                e2 = h2 % 2
                p2_ap, p2_tl = P[jj % 3]
                for t in range(2):
                    kc = 2 * kp2 + t
                    MM(sc, acc[:, 0:512], v_ap[:, kc, e2 * 65:e2 * 65 + 128],
                       p2_ap[:, t * 512:(t + 1) * 512], kc == 0, kc == 63, [v_tl, p2_tl],
                       [pst[acc_b]])
                if kp2 == 31:
                    finish_unit(h2, qb2, acc_b, unit2)
                if bg and (jj % 8 == 7):
                    bg.pop(0)()
            if j < nsteps and steps[j][1] == 7 and steps[j][2] == 31:
                while bg:
                    bg.pop(0)()

        sc.barrier()
        A.release(mix_mark)
        self.setup_stage_ring(1024)
        cT, tl_cT = A.alloc("cT", (2, T), BF16)
        DMA(sc, "sp", cT, cTd[:, :].rearrange("(c p) t -> p c t", c=2), [], [tl_cT])
        w_o, tl_wo = A.alloc("w_o", (8, 1024), BF16)
        tl_wo_k = [sc.tile("w_o%d" % k) for k in range(8)]
        for k in range(8):
            self.load_cast(w_o[:, k, :], tl_wo_k[k], w_o_d[:, k, :], 1024, eng=("dve", "act")[k % 2])
        xt = [A.alloc("xo%d" % i, (D,), F32) for i in range(2)]
        xo = [A.alloc("xr%d" % i, (D,), F32) for i in range(2)]
        for i in range(NT):
            x_ap, x_tl = xt[i % 2]
            r_ap, r_tl = xo[i % 2]
            DMA(sc, "sp", x_ap, xin[i * 128:(i + 1) * 128, :], [], [x_tl])
            b0 = 2 * (i % 2)
            tsl = slice(i * 128, (i + 1) * 128)
            qb = i // 4
            for nh in range(2):
                bk = b0 + nh
                for k in range(8):
                    if k < 4:
                        lhs = aT[:, k, tsl]
                        rt = [aT_tl[2 * k][qb], aT_tl[2 * k + 1][qb]]
                    elif k < 6:
                        lhs = bT[:, k - 4, tsl]
                        rt = [tl_bT]
                    else:
                        lhs = cT[:, k - 6, tsl]
                        rt = [tl_cT]
                    MM(sc, self.psb(bk)[:, 0:512], lhs, w_o[:, k, nh * 512:(nh + 1) * 512],
                       k == 0, k == 7, rt + [tl_wo_k[k]], [pst[bk]])
                TT(sc, "dve", r_ap[:, nh * 512:(nh + 1) * 512], x_ap[:, nh * 512:(nh + 1) * 512],
                   self.psb(bk)[:, 0:512], ALU.add, [x_tl, pst[bk]], [r_tl])
            DMA(sc, "pool", xmid[tsl, :], r_ap, [r_tl], [])

        sc.barrier()
        A.release(self.base_mark)
        self.setup_stage_ring()
        ffng, tl_fg = A.alloc("ffng", (8,), F32)
        self.load_f32(ffng, tl_fg, ffng_d[:, :])
        wg, tl_wg = A.alloc("wg", (8, DFF), BF16)
        wu, tl_wu = A.alloc("wu", (8, DFF), BF16)
        wd, tl_wd = A.alloc("wd", (NFF, 1024), BF16)
        ffn_eng = ("dve", "act", "dve", "act", "pool")
        fe = [0]

        def next_eng():
            e_ = ffn_eng[fe[0] % len(ffn_eng)]
            fe[0] += 1
            return e_

        tl_wg = [[sc.tile("wg%d_%d" % (k, hf)) for hf in range(2)] for k in range(8)]
        tl_wu = [[sc.tile("wu%d_%d" % (k, hf)) for hf in range(2)] for k in range(8)]
        tl_wd = [sc.tile("wd%d" % j) for j in range(NFF)]
        for hf in range(2):
            cs = slice(hf * 1408, (hf + 1) * 1408)
            for k in range(8):
                self.load_cast(wg[:, k, cs], tl_wg[k][hf], wg_d[:, k, cs], 1408,
                               scale=ffng[:, k:k + 1], scale_tl=tl_fg, eng=next_eng())
                self.load_cast(wu[:, k, cs], tl_wu[k][hf], wu_d[:, k, cs], 1408,
                               scale=ffng[:, k:k + 1], scale_tl=tl_fg, eng=next_eng())
        for j in range(NFF):
            self.load_cast(wd[:, j, :], tl_wd[j], wd_d[:, j, :], 1024, eng=next_eng())
        if final:
            fing, tl_fing = A.alloc("fing", (D,), F32)
            self.load_f32(fing, tl_fing, fin_d[:, :])
        NB = 256
        NSUB = NB // 128
        xb = [[A.alloc("xb%d_%d" % (i, s), (D,), F32) for s in range(NSUB)] for i in range(2)]
        hn2, tl_hn2 = A.alloc("hn2", (D,), BF16)
        junk, tl_junk = A.alloc("junk2", (D,), BF16)
        st, tl_st = A.alloc("st2", (8,), F32)
        h2T = [A.alloc("h2T%d" % i, (8, NB), BF16) for i in range(2)]
        actT, tl_act = A.alloc("actT", (NFF, NB), BF16)
        sg = [A.alloc("silu%d" % i, (NB,), F32) for i in range(2)]
        B_TP = 0
        B_G = (1, 2)
        B_U = (3, 4)
        B_Y = (5, 6)
        gi = 0
        yi = 0
        for blk in range(T // NB):
            bp = blk % 2
            h2_ap, h2_tl = h2T[bp]
            for s in range(NSUB):
                x_ap, x_tl = xb[bp][s]
                tok0 = blk * NB + s * 128
                DMA(sc, "sp", x_ap, xmid[tok0:tok0 + 128, :], [], [x_tl])
                ACTF(sc, junk, x_ap, AF.Square, [x_tl], [tl_junk, tl_st], accum=st[:, 0:1])
                RSTD(sc, st[:, 1:2], st[:, 0:1], D, self.epsb[:, 0:1], [tl_st, self.tl_eps], [tl_st])
                TS(sc, "dve", hn2, x_ap, st[:, 1:2], None, ALU.mult, None, [x_tl, tl_st],
                   [tl_hn2])
                tp = self.psb(B_TP, BF16)
                for k in range(8):
                    TR(sc, tp[:, k * 128:(k + 1) * 128], hn2[:, k * 128:(k + 1) * 128], ident,
                       [tl_hn2, tl_id], [pst[B_TP]])
                CP(sc, "dve", h2_ap[:, :, s * 128:(s + 1) * 128],
                   tp.rearrange("p (k t) -> p k t", k=8), [pst[B_TP]], [h2_tl])
            for j in range(NFF):
                gb = B_G[gi % 2]
                ub = B_U[gi % 2]
                s_ap, s_tl = sg[gi % 2]
                gi += 1
                for k in range(8):
                    MM(sc, self.psb(gb)[:, 0:NB], wg[:, k, j * 128:(j + 1) * 128], h2_ap[:, k, :],
                       k == 0, k == 7, [tl_wg[k][j // 11], h2_tl], [pst[gb]])
                for k in range(8):
                    MM(sc, self.psb(ub)[:, 0:NB], wu[:, k, j * 128:(j + 1) * 128], h2_ap[:, k, :],
                       k == 0, k == 7, [tl_wu[k][j // 11], h2_tl], [pst[ub]])
                ACTF(sc, s_ap, self.psb(gb)[:, 0:NB], AF.Silu, [pst[gb]], [s_tl])
                TT(sc, "dve", actT[:, j, :], s_ap, self.psb(ub)[:, 0:NB], ALU.mult,
                   [s_tl, pst[ub]], [tl_act])
            for s in range(NSUB):
                x_ap, x_tl = xb[bp][s]
                y_ap, y_tl = x_ap, x_tl
                tok0 = blk * NB + s * 128
                for nh in range(2):
                    bk = B_Y[nh]
                    for j in range(NFF):
                        MM(sc, self.psb(bk)[:, 0:512], actT[:, j, s * 128:(s + 1) * 128],
                           wd[:, j, nh * 512:(nh + 1) * 512], j == 0, j == NFF - 1,
                           [tl_act, tl_wd[j]], [pst[bk]])
                    TT(sc, "dve", y_ap[:, nh * 512:(nh + 1) * 512],
                       x_ap[:, nh * 512:(nh + 1) * 512], self.psb(bk)[:, 0:512], ALU.add,
                       [x_tl, pst[bk]], [y_tl])
                if final:
                    ACTF(sc, junk, y_ap, AF.Square, [y_tl], [tl_junk, tl_st], accum=st[:, 2:3])
                    RSTD(sc, st[:, 3:4], st[:, 2:3], D, self.epsb[:, 0:1], [tl_st, self.tl_eps],
                         [tl_st])
                    STT(sc, "dve", y_ap, y_ap, st[:, 3:4], fing, ALU.mult, ALU.mult,
                        [y_tl, tl_st, tl_fing], [y_tl])
                DMA(sc, "pool", xout[tok0:tok0 + 128, :], y_ap, [y_tl], [])


def _pk(w, k):
    n = w.shape[1]
    return np.ascontiguousarray(w.reshape(k, 128, n).transpose(1, 0, 2))


def _prep_a(l, P):
    o1, o2, o3, o4 = 384, 640, 672, 928
    w_in = P["w_in"][l]
    perm = np.concatenate([np.arange(0, o1), np.arange(o2, o3), np.arange(o1, o2),
                           np.arange(o4, 1440), np.arange(o3, o4)])
    sfx = "_a%d" % l
    d = {}
    d["w_in" + sfx] = _pk(w_in[:, perm], 8)
    d["mixg" + sfx] = np.ascontiguousarray(P["mix_norm"][l].reshape(8, 128).T)
    d["w_uq" + sfx] = _pk(P["w_uq"][l], 3)
    d["qg" + sfx] = np.ascontiguousarray(P["q_norm"][l].reshape(3, 128).T)
    d["kvg" + sfx] = np.ascontiguousarray(np.broadcast_to(P["kv_norm"][l][None, :], (128, 256)))
    d["sgg" + sfx] = np.ascontiguousarray(np.broadcast_to(P["sg_norm"][l][None, :], (128, 256)))
    d["wsT" + sfx] = np.ascontiguousarray(P["w_s"][l].transpose(2, 0, 1))
    d["bsT" + sfx] = np.ascontiguousarray(P["b_s"][l].T)
    inv = (10000.0 ** (-np.arange(0, 32, 2, dtype=np.float32) / 32)).astype(np.float32)
    d["invf" + sfx] = np.ascontiguousarray(np.broadcast_to(inv[None, :], (128, 16)))
    return d


def _prep_p(l, P, final):
    sfx = "_p%d" % l
    d = {}
    wkv = P["w_ukv"][l].reshape(256, 8, 128)
    d["wkn" + sfx] = _pk(np.ascontiguousarray(wkv[:, :, :64]).reshape(256, 512), 2)
    d["wv" + sfx] = _pk(np.ascontiguousarray(wkv[:, :, 64:]).reshape(256, 512), 2)
    wpb = np.zeros((128, 2, 128), np.float32)
    for g in range(4):
        c, o = g // 2, (g % 2) * 64
        wpb[o:o + 64, c, o:o + 64] = P["w_pool"][l][g]
    d["wpb" + sfx] = wpb
    d["psc" + sfx] = np.ascontiguousarray(P["pool_scale"][l].reshape(2, 128).T)
    d["w_o" + sfx] = _pk(P["w_o"][l], 8)
    d["ffng" + sfx] = np.ascontiguousarray(P["ffn_norm"][l].reshape(8, 128).T)
    d["w_gate" + sfx] = _pk(P["w_gate"][l], 8)
    d["w_up" + sfx] = _pk(P["w_up"][l], 8)
    d["w_down" + sfx] = _pk(P["w_down"][l], NFF)
    if final:
        d["fing"] = np.ascontiguousarray(np.broadcast_to(P["final_norm"][None, :], (128, D)))
    return d


def _core_consts(c, sfx):
    half = c % 2
    hmask = np.zeros((128, 2), np.float32)
    hmask[:, 0] = 1.0 if half == 1 else 0.0
    hmask[:, 1] = 1.0 if half == 0 else 0.0
    edge = np.zeros((128, 2, 16), np.float32)
    for g in range(4):
        w = 2 << g
        left = w // 2
        right = w - 1 - left
        cch, o = g // 2, (g % 2) * 64
        for e in range(16):
            t = (e if e < 8 else T - 16 + e) + half * T
            lo = max(t - left, 0)
            hi = min(t + right + 1, S)
            edge[o:o + 64, cch, e] = 1.0 / float(hi - lo)
    return {"hmask" + sfx: hmask, "edge" + sfx: edge}


_IDENT = np.eye(128, dtype=np.float32)
_PROGS = {}


def _get_prog(key, post_layer, a_layer, final):
    if key not in _PROGS:
        p = Prog(post_layer, a_layer, final)
        p.build()
        _PROGS[key] = p
    return _PROGS[key]


def _pair_cat(res, name, c):
    b = c // 2
    return np.concatenate([res[2 * b][name], res[2 * b + 1][name]], axis=0)


def _kernel_fused(P, xs, poss, cores):
    if "F" not in _PROGS:
        p = Prog(None, None, True, fused=True)
        p.build()
        _PROGS["F"] = p
    p = _PROGS["F"]
    shared = {}
    shared.update(_prep_a(0, P))
    shared.update(_prep_p(0, P, False))
    shared.update(_prep_a(1, P))
    shared.update(_prep_p(1, P, True))
    maps = []
    for c in cores:
        m = {"ident": _IDENT, "x_in_a0": xs[c], "pos_a0": poss[c], "pos_a1": poss[c]}
        m.update(_core_consts(c, "_p0"))
        m.update(_core_consts(c, "_p1"))
        m.update(shared)
        maps.append(m)
    r = run_bass_kernel_spmd(p.nc, maps, core_ids=cores).results
    out = np.empty((4, S, D), np.float32)
    for c in cores:
        out[c // 2, (c % 2) * T:(c % 2 + 1) * T, :] = r[c]["y_out"]
    return out


def kernel(**inputs):
    P = {k: np.asarray(v) for k, v in inputs.items()}
    x = P["x"]
    pos = P["positions"]
    cores = list(range(NCORES))
    xs = [np.ascontiguousarray(x[c // 2, (c % 2) * T:(c % 2 + 1) * T, :]) for c in cores]
    poss = [np.ascontiguousarray(pos[c // 2, (c % 2) * T:(c % 2 + 1) * T].reshape(NT, 128).T)
            for c in cores]
    if FUSED:
        return _kernel_fused(P, xs, poss, cores)

    p1 = _get_prog("L1", None, 0, False)
    wa0 = _prep_a(0, P)
    maps = []
    for c in cores:
        m = {"ident": _IDENT, "x_in_a0": xs[c], "pos_a0": poss[c]}
        m.update(wa0)
        maps.append(m)
    r1 = run_bass_kernel_spmd(p1.nc, maps, core_ids=cores).results

    p2 = _get_prog("L2", 0, 1, False)
    wp0 = _prep_p(0, P, False)
    wa1 = _prep_a(1, P)
    maps = []
    for c in cores:
        m = {"ident": _IDENT, "x_in_p0": xs[c], "pos_a1": poss[c],
             "latp_p0": _pair_cat(r1, "latT_a0", c), "halop_p0": _pair_cat(r1, "halo_a0", c),
             "QT_p0": r1[c]["QT_a0"], "ppT_p0": r1[c]["ppT_a0"], "cT_p0": r1[c]["cT_a0"]}
        m.update(_core_consts(c, "_p0"))
        m.update(wp0)
        m.update(wa1)
        maps.append(m)
    r2 = run_bass_kernel_spmd(p2.nc, maps, core_ids=cores).results

    p3 = _get_prog("L3", 1, None, True)
    wp1 = _prep_p(1, P, True)
    maps = []
    for c in cores:
        m = {"ident": _IDENT, "x_in_p1": r2[c]["xout_p0"],
             "latp_p1": _pair_cat(r2, "latT_a1", c), "halop_p1": _pair_cat(r2, "halo_a1", c),
             "QT_p1": r2[c]["QT_a1"], "ppT_p1": r2[c]["ppT_a1"], "cT_p1": r2[c]["cT_a1"]}
        m.update(_core_consts(c, "_p1"))
        m.update(wp1)
        maps.append(m)
    r3 = run_bass_kernel_spmd(p3.nc, maps, core_ids=cores).results

    out = np.empty((4, S, D), np.float32)
    for c in cores:
        out[c // 2, (c % 2) * T:(c % 2 + 1) * T, :] = r3[c]["y_out"]
    return out
```
